# Optimizing a Trainium2 kernel written in Bass

```python
import math
import jax
import jax.numpy as jnp
from jax import lax
import numpy as np

D_MODEL = 1024
BATCH = 16
SEQ = 4096
DEPTH = 4
DEC_BATCH = 8
DEC_SEQ = 2048
PAST_LEN = 128

N_MIXERS = 3
GRID_W = 64
LN_EPS = 1e-5

M_EXPAND = 2
M_DI = M_EXPAND * D_MODEL
M_HEADDIM = 64
M_HEADS = M_DI // M_HEADDIM
M_STATE = 128
M_GROUPS = 8
M_CONV = 5
M_CHUNK = 128
M_CONV_CH = M_DI + 2 * M_GROUPS * M_STATE
M_IN = M_DI + M_CONV_CH + 2 * M_HEADS
M_NORM_EPS = 1e-5

R_HEADSIZE = 64
R_DIM = D_MODEL
R_HEADS = R_DIM // R_HEADSIZE
R_DECAY_LORA = 64
R_AAA_LORA = 64
R_IN = 4 * R_DIM + 2 * R_DECAY_LORA + 2 * R_AAA_LORA
R_GN_EPS = 64e-5

A_HEADDIM = 128
A_HEADS = D_MODEL // A_HEADDIM
A_KV_HEADS = 2
A_GROUP = A_HEADS // A_KV_HEADS
A_DIM = A_HEADS * A_HEADDIM
A_KV_DIM = A_KV_HEADS * A_HEADDIM
A_IN = 2 * A_DIM + 2 * A_KV_DIM
ROPE_AXIS_DIM = A_HEADDIM // 2
ROPE_THETA = 10000.0
QK_NORM_EPS = 1e-6
Q_BLOCK = 128

kernel_name = 'hybrid_bidir_ssd_rwkv7_axial_gqa_trunk'


def _n_layers_of(kind):
    return len(range(kind, DEPTH, N_MIXERS))


def _layernorm(x, g, b):
    xf = x.astype(jnp.float32)
    mu = jnp.mean(xf, -1, keepdims=True)
    var = jnp.mean(jnp.square(xf - mu), -1, keepdims=True)
    return ((xf - mu) * lax.rsqrt(var + LN_EPS) * g + b).astype(x.dtype)


def _rmsnorm(x, g, eps):
    xf = x.astype(jnp.float32)
    return (xf * lax.rsqrt(jnp.mean(xf * xf, -1, keepdims=True) + eps) * g).astype(x.dtype)


def _centred_dwconv(u, w, b):
    k, c = w.shape
    pad = k // 2
    out = lax.conv_general_dilated(u, w[:, None, :].astype(u.dtype), window_strides=(1,),
                                   padding=[(pad, pad)], dimension_numbers=('NWC', 'WIO', 'NWC'),
                                   feature_group_count=c)
    return out + b


def _ssd_chunked(xh, dt, a, bm, cm):
    bsz, seqlen, nh, hp = xh.shape
    ng, ns = bm.shape[2], bm.shape[3]
    nr = nh // ng
    nc = seqlen // M_CHUNK
    q = M_CHUNK
    x = xh.reshape(bsz, nc, q, ng, nr, hp)
    dtc = dt.astype(jnp.float32).reshape(bsz, nc, q, ng, nr)
    bc = bm.reshape(bsz, nc, q, ng, ns)
    cc = cm.reshape(bsz, nc, q, ng, ns)
    acs = jnp.cumsum(dtc * a.reshape(ng, nr), axis=2)
    xdt = x * dtc[..., None]
    acs_t = jnp.transpose(acs, (0, 1, 3, 4, 2))
    seg = acs_t[..., :, None] - acs_t[..., None, :]
    mask = jnp.tril(jnp.ones((q, q), dtype=bool))
    decay = jnp.exp(jnp.where(mask, seg, -jnp.inf))
    cb = jnp.einsum('bcqgn,bcsgn->bcgqs', cc, bc)
    scores = cb[:, :, :, None] * decay
    y_diag = jnp.einsum('bcgrqs,bcsgrp->bcqgrp', scores, xdt)
    xw = xdt * jnp.exp(acs[:, :, -1:] - acs)[..., None]
    states = jnp.einsum('bcsgn,bcsgrp->bcgrpn', bc, xw)
    chunk_decay = jnp.exp(acs[:, :, -1])

    def step(h, inp):
        st, dec = inp
        return h * dec[..., None, None] + st, h

    h0 = jnp.zeros_like(states[:, 0])
    _, prev = lax.scan(step, h0, (jnp.moveaxis(states, 1, 0), jnp.moveaxis(chunk_decay, 1, 0)))
    prev = jnp.moveaxis(prev, 0, 1)
    y_off = jnp.einsum('bcqgn,bcgrpn->bcqgrp', cc, prev) * jnp.exp(acs)[..., None]
    return (y_diag + y_off).reshape(bsz, seqlen, nh, hp).astype(xh.dtype)


def _mamba2_mixer(x, w_in, conv_w, conv_b, a_log, dt_bias, d_skip, norm_w, w_out):
    bsz, seqlen, _ = x.shape
    proj = x @ w_in
    z = proj[..., :M_DI]
    xbc = proj[..., M_DI:M_DI + M_CONV_CH]
    dt_raw = proj[..., M_DI + M_CONV_CH:]
    xbc = jax.nn.silu(_centred_dwconv(xbc, conv_w, conv_b))
    xh = xbc[..., :M_DI].reshape(bsz, seqlen, M_HEADS, M_HEADDIM)
    bm = xbc[..., M_DI:M_DI + M_GROUPS * M_STATE].reshape(bsz, seqlen, M_GROUPS, M_STATE)
    cm = xbc[..., M_DI + M_GROUPS * M_STATE:].reshape(bsz, seqlen, M_GROUPS, M_STATE)
    dt = jax.nn.softplus(dt_raw.astype(jnp.float32).reshape(bsz, seqlen, 2, M_HEADS) + dt_bias)
    a = -jnp.exp(a_log.astype(jnp.float32))
    y_f = _ssd_chunked(xh, dt[:, :, 0], a[0], bm, cm)
    y_b = jnp.flip(_ssd_chunked(jnp.flip(xh, 1), jnp.flip(dt[:, :, 1], 1), a[1],
                                jnp.flip(bm, 1), jnp.flip(cm, 1)), 1)
    y = y_f + y_b + xh * d_skip[:, None]
    y = y.reshape(bsz, seqlen, M_DI) * jax.nn.silu(z)
    y = _rmsnorm(y.reshape(bsz, seqlen, M_GROUPS, M_DI // M_GROUPS), 1.0, M_NORM_EPS)
    y = y.reshape(bsz, seqlen, M_DI) * norm_w
    return (y @ w_out).astype(x.dtype)


def _token_shift_centred(u):
    prev = jnp.pad(u[:, :-1], ((0, 0), (1, 0), (0, 0)))
    nxt = jnp.pad(u[:, 1:], ((0, 0), (0, 1), (0, 0)))
    return 0.5 * (prev + nxt)


def _rwkv7_scan(r, w, k, v, kk, a):
    bsz, seqlen, nh, n = r.shape

    def step(s, inp):
        r_t, w_t, k_t, v_t, kk_t, a_t = inp
        sa = jnp.einsum('bhvk,bhk->bhv', s, -kk_t)
        s = (s * w_t[:, :, None, :] + sa[..., None] * (kk_t * a_t)[:, :, None, :]
             + v_t[..., None] * k_t[:, :, None, :])
        return s, jnp.einsum('bhvk,bhk->bhv', s, r_t)

    s0 = jnp.zeros((bsz, nh, n, n), jnp.float32)
    seqs = tuple(jnp.moveaxis(t.astype(jnp.float32), 1, 0) for t in (r, w, k, v, kk, a))
    _, y = lax.scan(step, s0, seqs)
    return jnp.moveaxis(y, 0, 1)


def _rwkv7_mixer(x, w_in, mu, w0, w_up, a0, a_up, k_k, k_a, r_k, ln_w, ln_b, w_out):
    bsz, seqlen, _ = x.shape
    hs = (bsz, seqlen, R_HEADS, R_HEADSIZE)
    u = x @ w_in
    u = u + (_token_shift_centred(u) - u) * mu
    r = u[..., :R_DIM]
    k = u[..., R_DIM:2 * R_DIM]
    v = u[..., 2 * R_DIM:3 * R_DIM]
    g = jax.nn.silu(u[..., 3 * R_DIM:4 * R_DIM])
    lw = u[..., 4 * R_DIM:4 * R_DIM + 2 * R_DECAY_LORA].astype(jnp.float32).reshape(bsz, seqlen, 2, R_DECAY_LORA)
    la = u[..., 4 * R_DIM + 2 * R_DECAY_LORA:].astype(jnp.float32).reshape(bsz, seqlen, 2, R_AAA_LORA)
    w_log = -jax.nn.softplus(-(w0 + jnp.einsum('blzr,zrc->blzc', jnp.tanh(lw), w_up))) - 0.5
    decay = jnp.exp(-jnp.exp(w_log.astype(jnp.float32)))
    a = jax.nn.sigmoid((a0 + jnp.einsum('blzr,zrc->blzc', la, a_up)).astype(jnp.float32))
    kk = (k * k_k).astype(jnp.float32).reshape(hs)
    kk = kk / jnp.maximum(jnp.sqrt(jnp.sum(kk * kk, -1, keepdims=True)), 1e-12)
    k_dir = (k[:, :, None] * (1.0 + (a - 1.0) * k_a)).reshape(bsz, seqlen, 2, R_HEADS, R_HEADSIZE)
    decay = decay.reshape(bsz, seqlen, 2, R_HEADS, R_HEADSIZE)
    a_h = a.reshape(bsz, seqlen, 2, R_HEADS, R_HEADSIZE)
    r_h = r.reshape(hs)
    v_h = v.reshape(hs)
    y_f = _rwkv7_scan(r_h, decay[:, :, 0], k_dir[:, :, 0], v_h, kk, a_h[:, :, 0])
    fl = lambda t: jnp.flip(t, 1)
    y_b = fl(_rwkv7_scan(fl(r_h), fl(decay[:, :, 1]), fl(k_dir[:, :, 1]), fl(v_h), fl(kk), fl(a_h[:, :, 1])))
    y = y_f + y_b
    mu_y = jnp.mean(y, -1, keepdims=True)
    var_y = jnp.mean(jnp.square(y - mu_y), -1, keepdims=True)
    y = ((y - mu_y) * lax.rsqrt(var_y + R_GN_EPS)).reshape(bsz, seqlen, R_DIM) * ln_w + ln_b
    bonus = jnp.sum(jnp.sum(r_h[:, :, None] * k_dir * r_k, -1, keepdims=True), axis=2) * v_h
    out = ((y + bonus.reshape(bsz, seqlen, R_DIM)) * g).astype(x.dtype)
    return out @ w_out


def _axial_rope_tables(seqlen):
    rows = seqlen // GRID_W
    row = jnp.repeat(jnp.arange(rows, dtype=jnp.float32), GRID_W)
    col = jnp.tile(jnp.arange(GRID_W, dtype=jnp.float32), rows)
    inv = ROPE_THETA ** (-jnp.arange(0, ROPE_AXIS_DIM, 2, dtype=jnp.float32) / ROPE_AXIS_DIM)
    ang_r = row[:, None] * inv[None]
    ang_c = col[:, None] * inv[None]
    return jnp.cos(ang_r), jnp.sin(ang_r), jnp.cos(ang_c), jnp.sin(ang_c)


def _rotate(x, cos, sin):
    half = x.shape[-1] // 2
    x1, x2 = x[..., :half], x[..., half:]
    c, s = cos[:, None, :], sin[:, None, :]
    return jnp.concatenate([x1 * c - x2 * s, x2 * c + x1 * s], -1)


def _apply_axial_rope(x, tables):
    cr, sr, cc, sc = tables
    out = jnp.concatenate([_rotate(x[..., :ROPE_AXIS_DIM], cr, sr),
                           _rotate(x[..., ROPE_AXIS_DIM:], cc, sc)], -1)
    return out.astype(x.dtype)


def _axial_gqa_mixer(x, w_in, q_norm, k_norm, w_out):
    bsz, seqlen, _ = x.shape
    u = x @ w_in
    q = u[..., :A_DIM].reshape(bsz, seqlen, A_HEADS, A_HEADDIM)
    k = u[..., A_DIM:A_DIM + A_KV_DIM].reshape(bsz, seqlen, A_KV_HEADS, A_HEADDIM)
    v = u[..., A_DIM + A_KV_DIM:A_DIM + 2 * A_KV_DIM].reshape(bsz, seqlen, A_KV_HEADS, A_HEADDIM)
    g = jax.nn.silu(u[..., A_DIM + 2 * A_KV_DIM:])
    tables = _axial_rope_tables(seqlen)
    q = _apply_axial_rope(_rmsnorm(q, q_norm, QK_NORM_EPS), tables)
    k = _apply_axial_rope(_rmsnorm(k, k_norm, QK_NORM_EPS), tables)
    nblk = seqlen // Q_BLOCK
    qb = q.reshape(bsz, nblk, Q_BLOCK, A_KV_HEADS, A_GROUP, A_HEADDIM)
    qb = jnp.moveaxis(qb, 1, 0)
    scale = A_HEADDIM ** -0.5

    def block(qblk):
        s = jnp.einsum('bqkgd,bskd->bkgqs', qblk, k).astype(jnp.float32) * scale
        p = jax.nn.softmax(s, axis=-1)
        return jnp.einsum('bkgqs,bskd->bqkgd', p.astype(v.dtype), v)

    o = lax.map(block, qb)
    o = jnp.moveaxis(o, 0, 1).reshape(bsz, seqlen, A_DIM)
    return (o * g) @ w_out


def _trunk(x, ln_g, ln_b, m_w_in, m_conv_w, m_conv_b, m_a_log, m_dt_bias, m_d, m_norm_w, m_w_out,
           r_w_in, r_mu, r_w0, r_w_up, r_a0, r_a_up, r_k_k, r_k_a, r_r_k, r_ln_w, r_ln_b, r_w_out,
           a_w_in, a_q_norm, a_k_norm, a_w_out):
    alpha = (2.0 * DEPTH) ** 0.25
    for i in range(DEPTH):
        j = i // N_MIXERS
        kind = i % N_MIXERS
        if kind == 0:
            h = _mamba2_mixer(x, m_w_in[j], m_conv_w[j], m_conv_b[j], m_a_log[j], m_dt_bias[j],
                              m_d[j], m_norm_w[j], m_w_out[j])
        elif kind == 1:
            h = _rwkv7_mixer(x, r_w_in[j], r_mu[j], r_w0[j], r_w_up[j], r_a0[j], r_a_up[j],
                             r_k_k[j], r_k_a[j], r_r_k[j], r_ln_w[j], r_ln_b[j], r_w_out[j])
        else:
            h = _axial_gqa_mixer(x, a_w_in[j], a_q_norm[j], a_k_norm[j], a_w_out[j])
        x = _layernorm(alpha * x + h, ln_g[i], ln_b[i])
    return x


def setup_inputs(seed: int = 0) -> dict:
    key = jax.random.key(seed)
    ks = jax.random.split(key, 32)
    f32 = jnp.float32
    na, nb, nc = _n_layers_of(0), _n_layers_of(1), _n_layers_of(2)
    beta = (8.0 * DEPTH) ** -0.25
    nrm = lambda k, s, sc: jax.random.normal(k, s, f32) * sc
    dt0 = jnp.exp(jax.random.uniform(ks[7], (na, 2, M_HEADS), f32)
                  * (math.log(0.1) - math.log(0.001)) + math.log(0.001))
    return {
        'x_prompt': nrm(ks[0], (BATCH, SEQ, D_MODEL), 1.0),
        'x_sample': nrm(ks[1], (DEC_BATCH, DEC_SEQ, D_MODEL), 1.0),
        'ln_g': 1.0 + nrm(ks[2], (DEPTH, D_MODEL), 0.02),
        'ln_b': nrm(ks[3], (DEPTH, D_MODEL), 0.02),
        'm_w_in': nrm(ks[4], (na, D_MODEL, M_IN), D_MODEL ** -0.5),
        'm_conv_w': nrm(ks[5], (na, M_CONV, M_CONV_CH), M_CONV ** -0.5),
        'm_conv_b': nrm(ks[6], (na, M_CONV_CH), 0.02),
        'm_a_log': jnp.log(jax.random.uniform(ks[8], (na, 2, M_HEADS), f32, 1.0, 16.0)),
        'm_dt_bias': dt0 + jnp.log(-jnp.expm1(-dt0)),
        'm_d': 1.0 + nrm(ks[9], (na, M_HEADS), 0.1),
        'm_norm_w': 1.0 + nrm(ks[10], (na, M_DI), 0.02),
        'm_w_out': nrm(ks[11], (na, M_DI, D_MODEL), beta * M_DI ** -0.5),
        'r_w_in': nrm(ks[12], (nb, D_MODEL, R_IN), D_MODEL ** -0.5),
        'r_mu': jax.random.uniform(ks[13], (nb, R_IN), f32),
        'r_w0': jax.random.uniform(ks[14], (nb, 2, R_DIM), f32, -6.0, -1.0),
        'r_w_up': nrm(ks[15], (nb, 2, R_DECAY_LORA, R_DIM), 0.5 * R_DECAY_LORA ** -0.5),
        'r_a0': nrm(ks[16], (nb, 2, R_DIM), 0.1),
        'r_a_up': nrm(ks[17], (nb, 2, R_AAA_LORA, R_DIM), 0.5 * R_AAA_LORA ** -0.5),
        'r_k_k': 0.85 + nrm(ks[18], (nb, R_DIM), 0.05),
        'r_k_a': 1.0 + nrm(ks[19], (nb, R_DIM), 0.05),
        'r_r_k': nrm(ks[20], (nb, R_HEADS, R_HEADSIZE), 0.1),
        'r_ln_w': 1.0 + nrm(ks[21], (nb, R_DIM), 0.02),
        'r_ln_b': nrm(ks[22], (nb, R_DIM), 0.02),
        'r_w_out': nrm(ks[23], (nb, R_DIM, D_MODEL), beta * R_DIM ** -0.5),
        'a_w_in': nrm(ks[24], (nc, D_MODEL, A_IN), D_MODEL ** -0.5),
        'a_q_norm': 1.0 + nrm(ks[25], (nc, A_HEADDIM), 0.02),
        'a_k_norm': 1.0 + nrm(ks[26], (nc, A_HEADDIM), 0.02),
        'a_w_out': nrm(ks[27], (nc, A_DIM, D_MODEL), beta * A_DIM ** -0.5),
    }


def reference(x_prompt, x_sample, ln_g, ln_b, m_w_in, m_conv_w, m_conv_b, m_a_log, m_dt_bias, m_d,
              m_norm_w, m_w_out, r_w_in, r_mu, r_w0, r_w_up, r_a0, r_a_up, r_k_k, r_k_a, r_r_k,
              r_ln_w, r_ln_b, r_w_out, a_w_in, a_q_norm, a_k_norm, a_w_out):
    y_prompt = _trunk(x_prompt, ln_g, ln_b, m_w_in, m_conv_w, m_conv_b, m_a_log, m_dt_bias, m_d,
                      m_norm_w, m_w_out, r_w_in, r_mu, r_w0, r_w_up, r_a0, r_a_up, r_k_k, r_k_a,
                      r_r_k, r_ln_w, r_ln_b, r_w_out, a_w_in, a_q_norm, a_k_norm, a_w_out)
    y_sample = _trunk(x_sample, ln_g, ln_b, m_w_in, m_conv_w, m_conv_b, m_a_log, m_dt_bias, m_d,
                      m_norm_w, m_w_out, r_w_in, r_mu, r_w0, r_w_up, r_a0, r_a_up, r_k_k, r_k_a,
                      r_r_k, r_ln_w, r_ln_b, r_w_out, a_w_in, a_q_norm, a_k_norm, a_w_out)
    return (y_prompt, y_sample)
```

```python
import math
from contextlib import ExitStack
import numpy as np
import ml_dtypes
import concourse.bass as bass
import concourse.mybir as mybir
from concourse.bass_utils import run_bass_kernel_spmd

F32 = mybir.dt.float32
F32R = mybir.dt.float32r
BF16 = mybir.dt.bfloat16
AF = mybir.ActivationFunctionType
ALU = mybir.AluOpType
AX = mybir.AxisListType

D = 1024
DEPTH = 4
LN_EPS = 1e-5
ALPHA = (2.0 * DEPTH) ** 0.25
A_HD = 128
A_H = 8
A_KV = 2
A_IN = 2560
QK_EPS = 1e-6
M_DI = 2048
M_H = 32
M_P = 64
M_N = 128
M_G = 8
M_CONVCH = 4096
M_IN = 6208
R_IN = 4352
R_H = 16
import os
RW_STAGE = int(os.environ.get('RW_STAGE', '9'))
RW_SUB = int(os.environ.get('RW_SUB', '9'))
RW_X = int(os.environ.get('RW_X', '0'))


class Prog:
    def __init__(self, nc):
        self.nc = nc
        self.engs = {'pe': nc.tensor, 'act': nc.scalar, 'dve': nc.vector, 'pool': nc.gpsimd, 'sp': nc.sync}
        self.sem = {k: nc.alloc_semaphore(name='s_' + k) for k in self.engs}
        self.cnt = {k: 0 for k in self.engs}
        self.waited = {k: {} for k in self.engs}
        self.NR = 6
        self.ring = {q: [nc.alloc_semaphore(name='d_%s%d' % (q, i)) for i in range(self.NR)] for q in ('sp', 'pool', 'act')}
        self.ring_cnt = {q: [0] * self.NR for q in self.ring}
        self.ring_i = {q: 0 for q in self.ring}
        self.lastw = {}
        self.readers = {}
        self.n_ins = 0

    def _wait(self, eng, tok):
        sem, sid, val, _ = tok
        w = self.waited[eng]
        if w.get(sid, 0) >= val:
            return
        w[sid] = val
        self.engs[eng].wait_ge(sem, val)

    def _deps(self, eng, reads, writes):
        toks = []
        for k in reads:
            t = self.lastw.get(k)
            if t is not None:
                toks.append(t)
        for k in writes:
            t = self.lastw.get(k)
            if t is not None and t[3] != eng:
                toks.append(t)
            for t in self.readers.get(k, ()):
                if t[3] != eng:
                    toks.append(t)
        for t in toks:
            self._wait(eng, t)

    def _update(self, tok, reads, writes):
        for k in writes:
            self.lastw[k] = tok
            self.readers[k] = []
        for k in reads:
            if k in writes:
                continue
            lst = self.readers.setdefault(k, [])
            if tok[3] != 'dma':
                lst[:] = [t for t in lst if t[3] != tok[3]]
            lst.append(tok)

    def op(self, eng, fn, reads=(), writes=()):
        self._deps(eng, reads, writes)
        ins = fn(self.engs[eng])
        self.cnt[eng] += 1
        ins.then_inc(self.sem[eng], 1)
        tok = (self.sem[eng], 'e_' + eng, self.cnt[eng], eng)
        self._update(tok, reads, writes)
        self.n_ins += 1
        return tok

    def dma(self, q, out, in_, reads=(), writes=()):
        self._deps(q, reads, writes)
        j = self.ring_i[q] % self.NR
        self.ring_i[q] += 1
        sem = self.ring[q][j]
        sid = 'r_%s%d' % (q, j)
        prev = 16 * self.ring_cnt[q][j]
        if prev > 0:
            self._wait(q, (sem, sid, prev, 'dma'))
        ins = self.engs[q].dma_start(out=out, in_=in_)
        ins.then_inc(sem, 16)
        self.ring_cnt[q][j] += 1
        tok = (sem, sid, 16 * self.ring_cnt[q][j], 'dma')
        self._update(tok, reads, writes)
        self.n_ins += 1
        return tok

    def barrier(self):
        for e in self.engs:
            for f in self.engs:
                if f != e and self.cnt[f] > 0:
                    self._wait(e, (self.sem[f], 'e_' + f, self.cnt[f], f))
            for q in self.ring:
                for j in range(self.NR):
                    if self.ring_cnt[q][j] > 0:
                        self._wait(e, (self.ring[q][j], 'r_%s%d' % (q, j), 16 * self.ring_cnt[q][j], 'dma'))
        self.lastw.clear()
        self.readers.clear()


class Ctx:
    pass


def build(seq_lens, layers, dbg=False):
    T = sum(seq_lens)
    Lmax = max(seq_lens)
    nc = bass.Bass("TRN2", target_bir_lowering=False)
    P = Prog(nc)
    c = Ctx()
    c.nc, c.P, c.T, c.Lmax = nc, P, T, Lmax

    def din(name, shape, dt=F32):
        return nc.dram_tensor(name, list(shape), dt, kind="ExternalInput").ap()

    def dscr(name, shape, dt):
        return nc.dram_tensor(name, list(shape), dt, kind=("ExternalOutput" if dbg else "Internal")).ap()

    n_m = sum(1 for k, _ in layers if k == 'm')
    n_r = sum(1 for k, _ in layers if k == 'r')
    n_a = sum(1 for k, _ in layers if k == 'a')
    nl = len(layers)
    xin = din("xin", [T, D])
    yout = nc.dram_tensor("yout", [T, D], F32, kind="ExternalOutput").ap()
    ln_g = din("ln_g", [nl, D])
    ln_b = din("ln_b", [nl, D])
    ident_f = din("ident_f", [128, 128])
    rope_t = din("rope_t", [Lmax, 4, 32])
    W = {}
    if n_a:
        W['a_w_in'] = din("a_w_in", [n_a, D, A_IN])
        W['a_q_norm'] = din("a_q_norm", [n_a, 128])
        W['a_k_norm'] = din("a_k_norm", [n_a, 128])
        W['a_w_out'] = din("a_w_out", [n_a, D, D])
    if n_m:
        W['m_w_in'] = din("m_w_in", [n_m, D, M_IN])
        W['m_w_out'] = din("m_w_out", [n_m, M_DI, D])
        W['m_convw_t'] = din("m_convw_t", [n_m, 128, 32, 5])
        W['m_convb_t'] = din("m_convb_t", [n_m, 128, 32])
        W['m_alog_c'] = din("m_alog_c", [n_m, 64, 1])
        W['m_dtb_c'] = din("m_dtb_c", [n_m, 64, 1])
        W['m_d_rep'] = din("m_d_rep", [n_m, 2048])
        W['m_norm_w'] = din("m_norm_w", [n_m, 2048])
        segmask = din("segmask", [64, Lmax])
        sel_c = din("sel_c", [64, 16, 4])
        negm_c = din("negm_c", [128, 2, 512])
    if n_r:
        W['r_w_in'] = din("r_w_in", [n_r, D, R_IN])
        W['r_w_out'] = din("r_w_out", [n_r, D, D])
        W['r_mu_t'] = din("r_mu_t", [n_r, 128, 34])
        W['r_wup_t'] = din("r_wup_t", [n_r, 128, 1024])
        W['r_aup_t'] = din("r_aup_t", [n_r, 128, 1024])
        W['r_w0_t'] = din("r_w0_t", [n_r, 128, 2, 8])
        W['r_a0_t'] = din("r_a0_t", [n_r, 128, 2, 8])
        for nm in ('r_kk_t', 'r_ka_t', 'r_rk_t', 'r_lnw_t', 'r_lnb_t'):
            W[nm] = din(nm, [n_r, 128, 8])
        blk_c = din("blk_c", [128, 128])
        mask64_c = din("mask64_c", [128, 1024])
        rmask_c = din("rmask_c", [128, 2, 128])
        rmaskn_c = din("rmaskn_c", [64, 2, 64])
    c.W = W

    xres = [dscr("xres%d" % i, [Lmax, D], F32) for i in range(2)]
    xT_d = dscr("xT_d", [128, 8, Lmax], BF16)
    yT_d = dscr("yT_d", [128, 16, Lmax], BF16)
    wb = {}
    if n_a:
        wb['a_w_in'] = dscr("wb_a_w_in", [n_a, 128, 8, A_IN], BF16)
        wb['a_w_out'] = dscr("wb_a_w_out", [n_a, 128, 8, D], BF16)
        qT_d = dscr("qT_d", [128, A_H, Lmax], BF16)
        kT_d = dscr("kT_d", [128, A_KV, Lmax], BF16)
        v_d = dscr("v_d", [Lmax, A_KV * 128], BF16)
        gT_d = dscr("gT_d", [128, 8, Lmax], BF16)

    if n_m:
        wb['m_w_in'] = dscr("wb_m_w_in", [n_m, 128, 8, M_IN], BF16)
        wb['m_w_out'] = dscr("wb_m_w_out", [n_m, 128, 16, D], BF16)
        xbcT_d = dscr("xbcT_d", [128, 32, Lmax], BF16)
        xtm_d = dscr("xtm_d", [Lmax, 2048], BF16)
        btm_d = dscr("btm_d", [Lmax, 1024], BF16)
        z_d = dscr("z_d", [Lmax, 2048], F32)
        cs_d = dscr("cs_d", [128, Lmax], F32)
        nbw_d = dscr("nbw_d", [Lmax, 128], F32)
        ecs_d = dscr("ecs_d", [Lmax, 64], F32)
        dec_d = dscr("dec_d", [128, Lmax // 128, 64], F32)
        hb_d = dscr("hb_d", [Lmax // 128, 128, 2048], BF16)

    if n_r:
        wb['r_w_in'] = dscr("wb_r_w_in", [n_r, 128, 8, R_IN], BF16)
        wb['r_w_out'] = dscr("wb_r_w_out", [n_r, 128, 8, D], BF16)
        u_d = dscr("u_d", [128, 34, Lmax], F32)
        rop_d = dscr("rop_d", [64, 16, 2, Lmax // 64, 4, 64], F32)
        gb_d = dscr("gb_d", [128, 8, 2, Lmax], F32)
        wc_d = dscr("wc_d", [64, 16, 2, Lmax // 64], F32)
        ybt_d = dscr("ybt_d", [Lmax, 1024], F32)

    ps = [nc.alloc_psum_tensor("ps%d" % i, [128, 512], F32) for i in range(8)]
    psk = ["ps%d" % i for i in range(8)]
    identf = nc.alloc_sbuf_tensor("identf", [128, 128], F32)
    identb = nc.alloc_sbuf_tensor("identb", [128, 128], BF16)
    onesb = nc.alloc_sbuf_tensor("onesb", [128, 128], BF16)
    onesf = nc.alloc_sbuf_tensor("onesf", [128, 128], F32)
    P.dma('sp', identf[:], ident_f, writes=['identf'])
    P.op('dve', lambda e: e.tensor_copy(out=identb[:], in_=identf[:]), reads=['identf'], writes=['identb'])
    P.op('dve', lambda e: e.memset(onesb[:], 1.0), writes=['onesb'])
    P.op('dve', lambda e: e.memset(onesf[:], 1.0), writes=['onesf'])

    rr = {'ev': 0, 'uid': 0}

    def SB(name, shape, dt):
        rr['uid'] += 1
        return nc.sbuf_tensor("%s_%d" % (name, rr['uid']), shape, dt)

    def evac_eng():
        rr['ev'] += 1
        return 'act' if rr['ev'] % 2 else 'dve'

    def copy(eng, out, in_, reads, writes):
        if eng == 'act':
            if out.dtype == F32R:
                return P.op('act', lambda e: e.activation(out=out, in_=in_, func=AF.Identity), reads, writes)
            return P.op('act', lambda e: e.copy(out=out, in_=in_), reads, writes)
        return P.op(eng, lambda e: e.tensor_copy(out=out, in_=in_), reads, writes)

    def cast_weight(src, dst, din_, F):
        with SB("cw_f", [128, 2, F], F32) as wf, SB("cw_b", [128, 2, F], BF16) as wbf:
            for dc in range(din_ // 128):
                b = dc % 2
                P.dma('sp', wf[:, b, :], src[dc * 128:(dc + 1) * 128, :], writes=[('cwf', b)])
                eng = ('dve', 'act', 'pool')[dc % 3]
                copy(eng, wbf[:, b, :], wf[:, b, :], [('cwf', b)], [('cwb', b)])
                P.dma('pool', dst[:, dc, :], wbf[:, b, :], reads=[('cwb', b)], writes=[('wbd', id(dst))])
            P.barrier()

    ai = 0
    for k, j in layers:
        if k == 'm':
            cast_weight(W['m_w_in'][j], wb['m_w_in'][j], D, M_IN)
            cast_weight(W['m_w_out'][j], wb['m_w_out'][j], M_DI, D)
        if k == 'r':
            cast_weight(W['r_w_in'][j], wb['r_w_in'][j], D, R_IN)
            cast_weight(W['r_w_out'][j], wb['r_w_out'][j], D, D)
        if k == 'a':
            cast_weight(W['a_w_in'][j], wb['a_w_in'][j], D, A_IN)
            cast_weight(W['a_w_out'][j], wb['a_w_out'][j], D, D)

    def transpose_to_xT(xt_tile_ap, xkey, tok0, tb):
        pass

    def phase_prep(t0, L):
        with SB("pp_x", [128, 2, D], F32) as xt, SB("pp_t", [128, 2, 8, 128], BF16) as tt:
            for it in range(L // 128):
                b = it % 2
                P.dma('sp', xt[:, b, :], xin[t0 + it * 128:t0 + (it + 1) * 128, :], writes=[('ppx', b)])
                for hf in range(2):
                    pk = psk[(it * 2 + hf) % 8]
                    pt = ps[(it * 2 + hf) % 8]
                    for i in range(4):
                        dc = hf * 4 + i
                        P.op('pe', lambda e, dc=dc, i=i, pt=pt: e.transpose(out=pt[:, i * 128:(i + 1) * 128], in_=xt[:, b, dc * 128:(dc + 1) * 128], identity=identf[:]),
                             reads=[('ppx', b), 'identf'], writes=[pk])
                    copy(evac_eng(), tt[:, b, hf * 4:(hf + 1) * 4, :], pt[:, :].rearrange("p (c t) -> p c t", c=4), [pk], [('ppt', b, hf)])
                P.dma('pool', xT_d[:, :, it * 128:(it + 1) * 128], tt[:, b, :, :], reads=[('ppt', b, 0), ('ppt', b, 1)], writes=['xT_d'])
            P.barrier()

    def phase_out(li, t0, L, cin_chunks, wout_b, x_src, x_dst, last):
        CC = cin_chunks
        with SB("po_w", [128, CC, D], BF16) as wo, \
                SB("po_g", [128, D], F32) as gbc, SB("po_b", [128, D], F32) as bbc, \
                SB("po_y", [128, 2, CC, 128], BF16) as yt, \
                SB("po_x", [128, 2, D], F32) as xr, \
                SB("po_z", [128, 2, D], F32) as zt, \
                SB("po_st", [128, 2, 16], F32) as st, \
                SB("po_t", [128, 2, 8, 128], BF16) as tt:
            P.dma('sp', wo[:], wout_b, writes=['po_w'])
            P.dma('sp', gbc[:], ln_g[li].unsqueeze(0).broadcast_to([128, D]), writes=['po_g'])
            P.dma('sp', bbc[:], ln_b[li].unsqueeze(0).broadcast_to([128, D]), writes=['po_b'])
            for it in range(L // 128):
                b = it % 2
                P.dma('sp', yt[:, b, :, :], yT_d[:, 0:CC, it * 128:(it + 1) * 128], reads=['yT_d'], writes=[('poy', b)])
                P.dma('sp', xr[:, b, :], x_src[it * 128:(it + 1) * 128, :], reads=['xres_src'], writes=[('pox', b)])
                pb = [(it * 4 + n) % 8 for n in range(4)]
                for nb in range(2):
                    for cc in range(CC):
                        P.op('pe', lambda e, nb=nb, cc=cc: e.matmul(ps[pb[nb]][:, :], yt[:, b, cc, :], wo[:, cc, nb * 512:(nb + 1) * 512], start=(cc == 0), stop=(cc == CC - 1)),
                             reads=[('poy', b), 'po_w'], writes=[psk[pb[nb]]])
                for nb in range(2):
                    P.op('dve', lambda e, nb=nb: e.scalar_tensor_tensor(out=zt[:, b, nb * 512:(nb + 1) * 512], in0=xr[:, b, nb * 512:(nb + 1) * 512], scalar=ALPHA,
                                                                          in1=ps[pb[nb]][:, :], op0=ALU.mult, op1=ALU.add),
                         reads=[('pox', b), psk[pb[nb]]], writes=[('poz', b, nb)])
                    P.op('dve', lambda e, nb=nb: e.bn_stats(out=st[:, b, nb * 6:(nb + 1) * 6], in_=zt[:, b, nb * 512:(nb + 1) * 512]),
                         reads=[('poz', b, nb)], writes=[('post', b, nb)])
                P.op('dve', lambda e: e.bn_aggr(out=st[:, b, 12:14], in_=st[:, b, 0:12]), reads=[('post', b, 0), ('post', b, 1)], writes=[('pomv', b)])
                P.op('dve', lambda e: e.tensor_scalar(out=st[:, b, 14:15], in0=st[:, b, 13:14], scalar1=LN_EPS, scalar2=None, op0=ALU.add), reads=[('pomv', b)], writes=[('pors', b)])
                P.op('act', lambda e: e.activation(out=st[:, b, 14:15], in_=st[:, b, 14:15], func=AF.Sqrt), reads=[('pors', b)], writes=[('pors', b)])
                P.op('dve', lambda e: e.reciprocal(out=st[:, b, 15:16], in_=st[:, b, 14:15]), reads=[('pors', b)], writes=[('pors2', b)])
                P.op('dve', lambda e: e.tensor_scalar(out=zt[:, b, :], in0=zt[:, b, :], scalar1=st[:, b, 12:13], scalar2=st[:, b, 15:16], op0=ALU.subtract, op1=ALU.mult),
                     reads=[('poz', b, 0), ('poz', b, 1), ('pomv', b), ('pors2', b)], writes=[('poz', b, 0), ('poz', b, 1)])
                P.op('pool', lambda e: e.tensor_tensor(out=zt[:, b, :], in0=zt[:, b, :], in1=gbc[:], op=ALU.mult), reads=[('poz', b, 0), ('poz', b, 1), 'po_g'], writes=[('poz', b, 0), ('poz', b, 1)])
                P.op('pool', lambda e: e.tensor_tensor(out=zt[:, b, :], in0=zt[:, b, :], in1=bbc[:], op=ALU.add), reads=[('poz', b, 0), ('poz', b, 1), 'po_b'], writes=[('poz', b, 0), ('poz', b, 1)])
                P.dma('pool', x_dst[it * 128:(it + 1) * 128, :], zt[:, b, :], reads=[('poz', b, 0), ('poz', b, 1)], writes=['xres_dst'])
                if not last:
                    for hf in range(2):
                        pi = pb[2 + hf]
                        for i in range(4):
                            dc = hf * 4 + i
                            P.op('pe', lambda e, dc=dc, i=i, pi=pi: e.transpose(out=ps[pi][:, i * 128:(i + 1) * 128], in_=zt[:, b, dc * 128:(dc + 1) * 128], identity=identf[:]),
                                 reads=[('poz', b, 0), ('poz', b, 1), 'identf'], writes=[psk[pi]])
                        copy(evac_eng(), tt[:, b, hf * 4:(hf + 1) * 4, :], ps[pi][:, :].rearrange("p (c t) -> p c t", c=4), [psk[pi]], [('pot', b, hf)])
                    P.dma('pool', xT_d[:, :, it * 128:(it + 1) * 128], tt[:, b, :, :], reads=[('pot', b, 0), ('pot', b, 1)], writes=['xT_d'])
            P.barrier()

    def phase_attn(j, L):
        NT = L // 128
        scale = A_HD ** -0.5
        win = wb['a_w_in'][j]
        with SB("a1_xT", [128, 8, L], BF16) as xT, \
                SB("a1_w", [128, 8, 1536], BF16) as wq, \
                SB("a1_wg", [128, 8, 1024], BF16) as wg, \
                SB("a1_gq", [128, 2, 128], F32) as gqk, \
                SB("a1_rope", [128, 2, 4, 32], F32) as rp, \
                SB("a1_q", [128, 2, 1280], F32) as qf, \
                SB("a1_sq", [128, 1280], F32) as sq, \
                SB("a1_ss", [128, 2, 16], F32) as ss, \
                SB("a1_qr", [128, 2, 1280], BF16) as qr, \
                SB("a1_v", [128, 2, 256], BF16) as vb, \
                SB("a1_qT", [128, 2, 10, 128], BF16) as qTt, \
                SB("a1_g", [128, 2, 512], BF16) as gt:
            P.dma('sp', xT[:], xT_d[:, :, 0:L], reads=['xT_d'], writes=['a1_xT'])
            P.dma('sp', wq[:], win[:, :, 0:1536], writes=['a1_w'])
            P.dma('sp', wg[:], win[:, :, 1536:2560], writes=['a1_wg'])
            P.dma('sp', gqk[:, 0, :], W['a_q_norm'][j].unsqueeze(0).broadcast_to([128, 128]), writes=['a1_gq'])
            P.dma('sp', gqk[:, 1, :], W['a_k_norm'][j].unsqueeze(0).broadcast_to([128, 128]), writes=['a1_gq'])
            for it in range(NT):
                b = it % 2
                P.dma('sp', rp[:, b, :, :], rope_t[it * 128:(it + 1) * 128, :, :], writes=[('a1rp', b)])
                pb = [(it * 3 + n) % 6 for n in range(3)]
                for n in range(3):
                    for dc in range(8):
                        P.op('pe', lambda e, n=n, dc=dc: e.matmul(ps[pb[n]][:, :], xT[:, dc, it * 128:(it + 1) * 128], wq[:, dc, n * 512:(n + 1) * 512], start=(dc == 0), stop=(dc == 7)),
                             reads=['a1_xT', 'a1_w'], writes=[psk[pb[n]]])
                P.op('act', lambda e: e.copy(out=qf[:, b, 0:512], in_=ps[pb[0]][:, :]), reads=[psk[pb[0]]], writes=[('a1q', b)])
                P.op('act', lambda e: e.copy(out=qf[:, b, 512:1024], in_=ps[pb[1]][:, :]), reads=[psk[pb[1]]], writes=[('a1q', b)])
                P.op('act', lambda e: e.copy(out=qf[:, b, 1024:1280], in_=ps[pb[2]][:, 0:256]), reads=[psk[pb[2]]], writes=[('a1q', b)])
                P.op('act', lambda e: e.copy(out=vb[:, b, :], in_=ps[pb[2]][:, 256:512]), reads=[psk[pb[2]]], writes=[('a1v', b)])
                P.dma('pool', v_d[it * 128:(it + 1) * 128, :], vb[:, b, :], reads=[('a1v', b)], writes=['v_d'])
                P.op('dve', lambda e: e.tensor_tensor(out=sq[:], in0=qf[:, b, :], in1=qf[:, b, :], op=ALU.mult), reads=[('a1q', b)], writes=['a1sq'])
                P.op('dve', lambda e: e.tensor_reduce(out=ss[:, b, 0:10], in_=sq[:].rearrange("p (h d) -> p h d", h=10), op=ALU.add, axis=AX.X), reads=['a1sq'], writes=[('a1ss', b)])
                P.op('dve', lambda e: e.tensor_scalar(out=ss[:, b, 0:10], in0=ss[:, b, 0:10], scalar1=1.0 / 128, scalar2=QK_EPS, op0=ALU.mult, op1=ALU.add), reads=[('a1ss', b)], writes=[('a1ss', b)])
                P.op('act', lambda e: e.activation(out=ss[:, b, 0:10], in_=ss[:, b, 0:10], func=AF.Sqrt), reads=[('a1ss', b)], writes=[('a1ss', b)])
                P.op('dve', lambda e: e.reciprocal(out=ss[:, b, 0:10], in_=ss[:, b, 0:10]), reads=[('a1ss', b)], writes=[('a1ss', b)])
                q3 = qf[:, b, :].rearrange("p (h d) -> p h d", h=10)
                P.op('dve', lambda e: e.tensor_tensor(out=q3, in0=q3, in1=ss[:, b, 0:10].unsqueeze(2).broadcast_to([128, 10, 128]), op=ALU.mult), reads=[('a1q', b), ('a1ss', b)], writes=[('a1q', b)])
                P.op('pool', lambda e: e.tensor_tensor(out=q3[:, 0:8, :], in0=q3[:, 0:8, :], in1=gqk[:, 0:1, :].broadcast_to([128, 8, 128]), op=ALU.mult), reads=[('a1q', b), 'a1_gq'], writes=[('a1q', b)])
                P.op('pool', lambda e: e.tensor_tensor(out=q3[:, 8:10, :], in0=q3[:, 8:10, :], in1=gqk[:, 1:2, :].broadcast_to([128, 2, 128]), op=ALU.mult), reads=[('a1q', b), 'a1_gq'], writes=[('a1q', b)])
                q5 = qf[:, b, :].rearrange("p (h a f) -> p h a f", h=10, a=2)
                o5 = qr[:, b, :].rearrange("p (h a f) -> p h a f", h=10, a=2)
                s5 = sq[:].rearrange("p (h a f) -> p h a f", h=10, a=2)
                cosb = rp[:, b, 0:4:2, :]
                sinb = rp[:, b, 1:4:2, :]
                cb4 = cosb.unsqueeze(1).broadcast_to([128, 10, 2, 32])
                sb4 = sinb.unsqueeze(1).broadcast_to([128, 10, 2, 32])
                P.op('dve', lambda e: e.tensor_tensor(out=s5[:, :, :, 0:32], in0=q5[:, :, :, 0:32], in1=cb4, op=ALU.mult),
                     reads=[('a1q', b), ('a1rp', b)], writes=['a1sq'])
                P.op('pool', lambda e: e.tensor_tensor(out=s5[:, :, :, 32:64], in0=q5[:, :, :, 32:64], in1=cb4, op=ALU.mult),
                     reads=[('a1q', b), ('a1rp', b)], writes=['a1sq'])
                P.op('dve', lambda e: e.tensor_tensor(out=q5[:, :, :, 0:32], in0=q5[:, :, :, 0:32], in1=sb4, op=ALU.mult),
                     reads=[('a1q', b), ('a1rp', b), 'a1sq'], writes=[('a1q', b)])
                P.op('pool', lambda e: e.tensor_tensor(out=q5[:, :, :, 32:64], in0=q5[:, :, :, 32:64], in1=sb4, op=ALU.mult),
                     reads=[('a1q', b), ('a1rp', b), 'a1sq'], writes=[('a1q', b)])
                P.op('dve', lambda e: e.tensor_tensor(out=o5[:, :, :, 0:32], in0=s5[:, :, :, 0:32], in1=q5[:, :, :, 32:64], op=ALU.subtract),
                     reads=[('a1q', b), 'a1sq'], writes=[('a1qr', b)])
                P.op('dve', lambda e: e.tensor_tensor(out=o5[:, :, :, 32:64], in0=s5[:, :, :, 32:64], in1=q5[:, :, :, 0:32], op=ALU.add),
                     reads=[('a1q', b), 'a1sq'], writes=[('a1qr', b)])
                for h in range(10):
                    pi = 6 + (h // 8)
                    pv = ps[pi][:, :].bitcast(BF16)
                    P.op('pe', lambda e, h=h, pv=pv: e.transpose(out=pv[:, (h % 8) * 128:(h % 8 + 1) * 128], in_=qr[:, b, h * 128:(h + 1) * 128], identity=identb[:]),
                         reads=[('a1qr', b), 'identb'], writes=[psk[pi]])
                P.op('act', lambda e: e.copy(out=qTt[:, b, 0:8, :], in_=ps[6][:, :].bitcast(BF16).rearrange("p (h t) -> p h t", h=8)), reads=[psk[6]], writes=[('a1qT', b)])
                P.op('dve', lambda e: e.tensor_copy(out=qTt[:, b, 8:10, :], in_=ps[7][:, :].bitcast(BF16)[:, 0:256].rearrange("p (h t) -> p h t", h=2)), reads=[psk[7]], writes=[('a1kT', b)])
                P.dma('pool', qT_d[:, :, it * 128:(it + 1) * 128], qTt[:, b, 0:8, :], reads=[('a1qT', b)], writes=['qT_d'])
                P.dma('pool', kT_d[:, :, it * 128:(it + 1) * 128], qTt[:, b, 8:10, :], reads=[('a1kT', b)], writes=['kT_d'])
            k = 0
            for fc in range(8):
                for tbk in range(L // 512):
                    b = k % 2
                    pi = k % 6
                    k += 1
                    for dc in range(8):
                        P.op('pe', lambda e, dc=dc, pi=pi: e.matmul(ps[pi][:, :], wg[:, dc, fc * 128:(fc + 1) * 128], xT[:, dc, tbk * 512:(tbk + 1) * 512], start=(dc == 0), stop=(dc == 7)),
                             reads=['a1_xT', 'a1_wg'], writes=[psk[pi]])
                    P.op('act', lambda e, pi=pi: e.activation(out=gt[:, b, :], in_=ps[pi][:, :], func=AF.Silu), reads=[psk[pi]], writes=[('a1g', b)])
                    P.dma('pool', gT_d[:, fc, tbk * 512:(tbk + 1) * 512], gt[:, b, :], reads=[('a1g', b)], writes=['gT_d'])
            P.barrier()
        with SB("a2_kT", [128, A_KV, L], BF16) as kT, \
                SB("a2_v", [128, NT, 256], BF16) as V, \
                SB("a2_q", [128, 2, 512], BF16) as qb, \
                SB("a2_g", [128, 2, 512], BF16) as gb, \
                SB("a2_p", [128, 3, 512], BF16) as pT, \
                SB("a2_r", [128, 2, 512], F32) as rc, \
                SB("a2_o", [128, 2, 512], BF16) as ob, \
                SB("a2_m", [128, 8], F32) as mm, SB("a2_racc", [128, 2, 2, 512], F32) as racc:
            P.dma('sp', kT[:], kT_d[:, :, 0:L], reads=['kT_d'], writes=['a2_kT'])
            P.dma('sp', V[:], v_d[0:L, :].rearrange("(n p) c -> p n c", p=128), reads=['v_d'], writes=['a2_v'])
            P.dma('sp', rc[:, 0, 0:128], W['a_q_norm'][j].unsqueeze(0).broadcast_to([128, 128]), writes=[('a2r', 0)])
            P.dma('sp', rc[:, 0, 128:256], W['a_k_norm'][j].unsqueeze(0).broadcast_to([128, 128]), writes=[('a2r', 0)])
            P.op('dve', lambda e: e.tensor_reduce(out=mm[:, 0:2], in_=rc[:, 0, 0:256].rearrange("p (a d) -> p a d", a=2), op=ALU.max, axis=AX.X, apply_absolute_value=True),
                 reads=[('a2r', 0)], writes=['a2m'])
            P.op('dve', lambda e: e.tensor_tensor(out=mm[:, 2:3], in0=mm[:, 0:1], in1=mm[:, 1:2], op=ALU.mult), reads=['a2m'], writes=['a2m2'])
            P.op('dve', lambda e: e.tensor_scalar(out=mm[:, 3:4], in0=mm[:, 2:3], scalar1=-math.sqrt(128.0), scalar2=None, op0=ALU.mult), reads=['a2m2'], writes=['a2m3'])
            k = 0
            for h in range(A_H):
                kv = h // (A_H // A_KV)
                for qbk in range(L // 512):
                    b = k % 2
                    k += 1
                    P.dma('sp', qb[:, b, :], qT_d[:, h, qbk * 512:(qbk + 1) * 512], reads=['qT_d'], writes=[('a2q', b)])
                    P.dma('sp', gb[:, b, :], gT_d[:, h, qbk * 512:(qbk + 1) * 512], reads=['gT_d'], writes=[('a2g', b)])
                    po = 4 + 2 * b
                    pr = 5 + 2 * b

                    def qk(st):
                        pi = st % 4
                        P.op('pe', lambda e: e.matmul(ps[pi][:, :], kT[:, kv, st * 128:(st + 1) * 128], qb[:, b, :], start=True, stop=True),
                             reads=['a2_kT', ('a2q', b)], writes=[psk[pi]])
                    qk(0)
                    for st in range(NT):
                        if st + 1 < NT:
                            qk(st + 1)
                        pi = st % 4
                        pb3 = st % 3
                        P.op('act', lambda e: e.activation(out=pT[:, pb3, :], in_=ps[pi][:, :], func=AF.Exp, scale=scale, bias=mm[:, 3:4]),
                             reads=[psk[pi], 'a2m3'], writes=[('a2p', pb3)])
                        P.op('pe', lambda e: e.matmul(ps[po][:, :], V[:, st, kv * 128:(kv + 1) * 128], pT[:, pb3, :], start=(st == 0), stop=(st == NT - 1)),
                             reads=['a2_v', ('a2p', pb3)], writes=[psk[po]])
                        par = st % 2
                        reng = 'dve' if par == 0 else 'pool'
                        if st < 2:
                            P.op(reng, lambda e: e.tensor_copy(out=racc[:, b, par, :], in_=pT[:, pb3, :]), reads=[('a2p', pb3)], writes=[('racc', b, par)])
                        else:
                            P.op(reng, lambda e: e.tensor_tensor(out=racc[:, b, par, :], in0=racc[:, b, par, :], in1=pT[:, pb3, :], op=ALU.add), reads=[('a2p', pb3), ('racc', b, par)], writes=[('racc', b, par)])
                    P.op('pe', lambda e: e.matmul(ps[pr][:, :], onesf[:], racc[:, b, 0, :], start=True, stop=False), reads=['onesf', ('racc', b, 0)], writes=[psk[pr]])
                    P.op('pe', lambda e: e.matmul(ps[pr][:, :], onesf[:], racc[:, b, 1, :], start=False, stop=True), reads=['onesf', ('racc', b, 1)], writes=[psk[pr]])
                    P.op('dve', lambda e: e.reciprocal(out=rc[:, b, :], in_=ps[pr][:, :]), reads=[psk[pr]], writes=[('a2r', b)])
                    P.op('dve', lambda e: e.tensor_tensor(out=rc[:, b, :], in0=ps[po][:, :], in1=rc[:, b, :], op=ALU.mult), reads=[psk[po], ('a2r', b)], writes=[('a2r', b)])
                    P.op('pool', lambda e: e.tensor_tensor(out=ob[:, b, :], in0=rc[:, b, :], in1=gb[:, b, :], op=ALU.mult), reads=[('a2r', b), ('a2g', b)], writes=[('a2o', b)])
                    P.dma('pool', yT_d[:, h, qbk * 512:(qbk + 1) * 512], ob[:, b, :], reads=[('a2o', b)], writes=['yT_d'])
            P.barrier()

    def phase_mamba(j, L):
        NT = L // 128
        NB = L // 512
        win = wb['m_w_in'][j]
        with SB("m1_xT", [128, 8, L], BF16) as xT, SB("m1_w", [128, 2, 8, 128], BF16) as wsl, \
                SB("m1_cw", [128, 32, 5], F32) as cw, SB("m1_cb", [128, 32], F32) as cb, \
                SB("m1_rb", [128, L + 4], F32) as rb, SB("m1_acc", [128, L], F32) as acc, \
                SB("m1_ob", [128, 2, L], BF16) as ob, SB("m1_tt", [128, 2, 8, 128], BF16) as tt, SB("m1_cp", [128, 2, L], F32) as cp:
            P.dma('sp', xT[:], xT_d[:, :, 0:L], writes=['m1_xT'])
            P.dma('sp', cw[:], W['m_convw_t'][j], writes=['m1_cw'])
            P.dma('sp', cb[:], W['m_convb_t'][j], writes=['m1_cw'])
            P.op('dve', lambda e: e.memset(rb[:, 0:2], 0.0), writes=['m1_rb'])
            P.op('dve', lambda e: e.memset(rb[:, L + 2:L + 4], 0.0), writes=['m1_rb'])
            k = 0
            kt = 0
            gsz = min(8, NT)
            for fc in range(32):
                wbuf = fc % 2
                P.dma('sp', wsl[:, wbuf], win[:, :, 2048 + fc * 128:2048 + (fc + 1) * 128], writes=[('m1w', wbuf)])
                for tb in range(NB):
                    pi = k % 4
                    k += 1
                    for dc in range(8):
                        P.op('pe', lambda e: e.matmul(ps[pi][:, :], wsl[:, wbuf, dc, :], xT[:, dc, tb * 512:(tb + 1) * 512], start=(dc == 0), stop=(dc == 7)),
                             reads=['m1_xT', ('m1w', wbuf)], writes=[psk[pi]])
                    P.op('act', lambda e: e.copy(out=rb[:, 2 + tb * 512:2 + (tb + 1) * 512], in_=ps[pi][:, :]), reads=[psk[pi]], writes=['m1_rb'])
                P.op('dve', lambda e: e.tensor_scalar(out=acc[:], in0=rb[:, 0:L], scalar1=cw[:, fc, 0:1], scalar2=cb[:, fc:fc + 1], op0=ALU.mult, op1=ALU.add),
                     reads=['m1_rb', 'm1_cw'], writes=['m1_acc'])
                for kk in (3, 4):
                    P.op('act', lambda e: e.activation(out=cp[:, kk - 3, :], in_=rb[:, kk:kk + L], func=AF.Copy, scale=cw[:, fc, kk:kk + 1]), reads=['m1_rb', 'm1_cw'], writes=[('m1cp', kk - 3)])
                P.op('pool', lambda e: e.tensor_tensor(out=cp[:, 0, :], in0=cp[:, 0, :], in1=cp[:, 1, :], op=ALU.add), reads=[('m1cp', 0), ('m1cp', 1)], writes=[('m1cp', 0)])
                for kk in range(1, 3):
                    P.op('dve', lambda e: e.scalar_tensor_tensor(out=acc[:], in0=rb[:, kk:kk + L], scalar=cw[:, fc, kk:kk + 1], in1=acc[:], op0=ALU.mult, op1=ALU.add),
                         reads=['m1_rb', 'm1_cw', 'm1_acc'], writes=['m1_acc'])
                P.op('dve', lambda e: e.tensor_tensor(out=acc[:], in0=acc[:], in1=cp[:, 0, :], op=ALU.add), reads=['m1_acc', ('m1cp', 0)], writes=['m1_acc'])
                obb = fc % 2
                P.op('act', lambda e: e.activation(out=ob[:, obb, :], in_=acc[:], func=AF.Silu), reads=['m1_acc'], writes=[('m1ob', obb)])
                P.dma('pool', xbcT_d[:, fc, 0:L], ob[:, obb, :], reads=[('m1ob', obb)], writes=['xbcT_d'])
                if fc < 24:
                    for t8 in range(NT // gsz):
                        pi = 4 + (kt % 4)
                        tb2 = kt % 2
                        kt += 1
                        pv = ps[pi][:, :].bitcast(BF16)
                        for i in range(gsz):
                            tk = t8 * gsz + i
                            P.op('pe', lambda e: e.transpose(out=pv[:, i * 128:(i + 1) * 128], in_=ob[:, obb, tk * 128:(tk + 1) * 128], identity=identb[:]),
                                 reads=[('m1ob', obb), 'identb'], writes=[psk[pi]])
                        copy(evac_eng(), tt[:, tb2, 0:gsz, :], pv[:, 0:gsz * 128].rearrange("p (n c) -> p n c", n=gsz), [psk[pi]], [('m1tt', tb2)])
                        if fc < 16:
                            dst = xtm_d[t8 * gsz * 128:(t8 + 1) * gsz * 128, fc * 128:(fc + 1) * 128]
                        else:
                            dst = btm_d[t8 * gsz * 128:(t8 + 1) * gsz * 128, (fc - 16) * 128:(fc - 15) * 128]
                        P.dma('pool', dst.rearrange("(n p) c -> p n c", p=128), tt[:, tb2, 0:gsz, :], reads=[('m1tt', tb2)], writes=['xtm_d'])
            P.barrier()
        with SB("m1b_xT", [128, 8, L], BF16) as xT, SB("m1b_wz", [128, 8, 2048], BF16) as wz, SB("m1b_z", [128, 2, 2048], F32) as zt:
            P.dma('sp', xT[:], xT_d[:, :, 0:L], writes=['m1_xT'])
            P.dma('sp', wz[:], win[:, :, 0:2048], writes=['m1_wz'])
            for it in range(NT):
                b = it % 2
                for n in range(4):
                    pi = (it * 4 + n) % 8
                    for dc in range(8):
                        P.op('pe', lambda e: e.matmul(ps[pi][:, :], xT[:, dc, it * 128:(it + 1) * 128], wz[:, dc, n * 512:(n + 1) * 512], start=(dc == 0), stop=(dc == 7)),
                             reads=['m1_xT', 'm1_wz'], writes=[psk[pi]])
                    P.op('act', lambda e: e.activation(out=zt[:, b, n * 512:(n + 1) * 512], in_=ps[pi][:, :], func=AF.Silu), reads=[psk[pi]], writes=[('m1z', b)])
                P.dma('pool', z_d[it * 128:(it + 1) * 128, :], zt[:, b, :], reads=[('m1z', b)], writes=['z_d'])
            P.barrier()
        with SB("m1c_xT", [128, 8, L], BF16) as xT, SB("m1c_wdt", [128, 8, 64], BF16) as wdt, \
                SB("m1c_x", [64, L], F32) as xr, SB("m1c_t1", [64, L], F32) as t1, SB("m1c_t2", [64, L], F32) as t2, \
                SB("m1c_cs", [64, L], F32) as cs, SB("m1c_vv", [64, L], F32) as vv, SB("m1c_msk", [64, L], F32) as msk, \
                SB("m1c_col", [64, 8], F32) as col, SB("m1c_dcol", [64, NT], F32) as dcol, SB("m1c_xd", [64, NT, 64], F32) as xd, \
                SB("m1c_st", [128, 2, 512], F32) as stg:
            P.dma('sp', xT[:], xT_d[:, :, 0:L], writes=['m1_xT'])
            P.dma('sp', wdt[:], win[:, :, 6144:6208], writes=['m1_wdt'])
            P.dma('sp', msk[:], segmask[:, 0:L], writes=['msk'])
            P.dma('sp', col[:, 0:1], W['m_alog_c'][j], writes=['col0'])
            P.dma('sp', col[:, 2:3], W['m_dtb_c'][j], writes=['col2'])
            P.op('act', lambda e: e.activation(out=col[:, 1:2], in_=col[:, 0:1], func=AF.Exp), reads=['col0'], writes=['col1'])
            P.op('dve', lambda e: e.tensor_scalar(out=col[:, 3:4], in0=col[:, 1:2], scalar1=-1.0, scalar2=None, op0=ALU.mult), reads=['col1'], writes=['col3'])
            for tb in range(NB):
                pi = tb % 4
                for dc in range(8):
                    P.op('pe', lambda e: e.matmul(ps[pi][0:64, :], wdt[:, dc, :], xT[:, dc, tb * 512:(tb + 1) * 512], start=(dc == 0), stop=(dc == 7)),
                         reads=['m1_xT', 'm1_wdt'], writes=[psk[pi]])
                P.op('dve', lambda e: e.tensor_scalar(out=xr[:, tb * 512:(tb + 1) * 512], in0=ps[pi][0:64, :], scalar1=col[:, 2:3], scalar2=None, op0=ALU.add),
                     reads=[psk[pi], 'col2'], writes=['xr'])
            P.op('act', lambda e: e.activation(out=t1[:], in_=xr[:], func=AF.Abs), reads=['xr'], writes=['t1'])
            P.op('act', lambda e: e.activation(out=t1[:], in_=t1[:], func=AF.Exp, scale=-1.0), reads=['t1'], writes=['t1'])
            P.op('act', lambda e: e.activation(out=t1[:], in_=t1[:], func=AF.Ln, bias=1.0), reads=['t1'], writes=['t1'])
            P.op('dve', lambda e: e.tensor_scalar(out=xr[:], in0=xr[:], scalar1=0.0, scalar2=None, op0=ALU.max), reads=['xr'], writes=['xr'])
            P.op('dve', lambda e: e.tensor_tensor(out=xr[:], in0=xr[:], in1=t1[:], op=ALU.add), reads=['xr', 't1'], writes=['xr'])
            P.op('dve', lambda e: e.tensor_scalar(out=vv[:], in0=xr[:], scalar1=col[:, 3:4], scalar2=None, op0=ALU.mult), reads=['xr', 'col3'], writes=['vv'])
            P.op('dve', lambda e: e.tensor_tensor_scan(out=cs[:], data0=msk[:], data1=vv[:], initial=0.0, op0=ALU.mult, op1=ALU.add), reads=['msk', 'vv'], writes=['cs'])
            cs3 = cs[:].rearrange("p (c t) -> p c t", t=128)
            t13 = t1[:].rearrange("p (c t) -> p c t", t=128)
            t23 = t2[:].rearrange("p (c t) -> p c t", t=128)
            P.op('dve', lambda e: e.tensor_tensor(out=t13[32:64], in0=cs3[32:64, :, 127:128].broadcast_to([32, NT, 128]), in1=cs3[32:64], op=ALU.subtract), reads=['cs', 't1'], writes=['t1'])
            P.op('dve', lambda e: e.tensor_tensor(out=cs[32:64, :], in0=t1[32:64, :], in1=vv[32:64, :], op=ALU.add), reads=['t1', 'vv', 'cs'], writes=['cs'])
            P.op('dve', lambda e: e.tensor_copy(out=msk[:].bitcast(F32R), in_=cs[:]), reads=['cs', 'msk'], writes=['msk'])
            P.op('dve', lambda e: e.tensor_tensor(out=vv[:], in0=cs[:], in1=msk[:], op=ALU.subtract), reads=['cs', 'msk', 'vv'], writes=['vv'])
            P.dma('pool', cs_d[0:64, 0:L], msk[:], reads=['msk'], writes=['cs_d'])
            P.dma('pool', cs_d[64:128, 0:L], vv[:], reads=['vv'], writes=['cs_d'])
            P.op('act', lambda e: e.activation(out=t1[:], in_=xr[:], func=AF.Ln), reads=['xr', 't1'], writes=['t1'])
            P.op('dve', lambda e: e.tensor_tensor(out=t1[:], in0=t1[:], in1=cs[:], op=ALU.subtract), reads=['t1', 'cs'], writes=['t1'])
            P.op('dve', lambda e: e.tensor_tensor(out=t23[0:32], in0=cs3[0:32, :, 127:128].broadcast_to([32, NT, 128]), in1=cs3[0:32], op=ALU.subtract), reads=['cs'], writes=['t2'])
            P.op('dve', lambda e: e.tensor_tensor(out=t23[32:64], in0=cs3[32:64, :, 0:1].broadcast_to([32, NT, 128]), in1=cs3[32:64], op=ALU.subtract), reads=['cs'], writes=['t2'])
            P.op('act', lambda e: e.activation(out=t2[:], in_=t2[:], func=AF.Exp), reads=['t2'], writes=['t2'])
            P.op('dve', lambda e: e.tensor_tensor(out=t2[:], in0=t2[:], in1=xr[:], op=ALU.mult), reads=['t2', 'xr'], writes=['t2'])
            P.op('act', lambda e: e.activation(out=xr[:], in_=cs[:], func=AF.Exp), reads=['cs', 'xr', 't2'], writes=['xr'])
            c8e = min(8, NT)
            for c8 in range(NT // c8e):
                pi = 4 + c8 % 2
                bq = c8 % 2
                for i in range(c8e):
                    cc = c8 * c8e + i
                    P.op('pe', lambda e: e.transpose(out=ps[pi][:, i * 64:(i + 1) * 64], in_=xr[:, cc * 128:(cc + 1) * 128], identity=identf[0:64, 0:64]), reads=['xr', 'identf'], writes=[psk[pi]])
                copy(evac_eng(), stg[:, bq, 0:c8e * 64], ps[pi][:, 0:c8e * 64], [psk[pi]], [('stg', bq)])
                P.dma('pool', ecs_d[c8 * c8e * 128:(c8 + 1) * c8e * 128, :].rearrange("(n p) c -> p n c", p=128), stg[:, bq, 0:c8e * 64].rearrange("p (n c) -> p n c", c=64),
                      reads=[('stg', bq)], writes=['ecs_d'])
            P.op('act', lambda e: e.activation(out=dcol[0:32, :], in_=cs3[0:32, :, 127], func=AF.Exp), reads=['cs'], writes=['dcol'])
            P.op('act', lambda e: e.activation(out=dcol[32:64, :], in_=cs3[32:64, :, 0], func=AF.Exp), reads=['cs'], writes=['dcol'])
            P.op('dve', lambda e: e.tensor_tensor(out=xd[:], in0=dcol[:, :].unsqueeze(2).broadcast_to([64, NT, 64]), in1=identf[0:64, 0:64].unsqueeze(1).broadcast_to([64, NT, 64]), op=ALU.mult),
                 reads=['dcol', 'identf'], writes=['xd'])
            c8n = min(8, NT)
            for c8 in range(NT // c8n):
                pi = 4 + c8 % 2
                b = c8 % 2
                P.op('pe', lambda e: e.matmul(ps[pi][:, 0:c8n * 64], onesf[0:64, :], xd[:, c8 * c8n:(c8 + 1) * c8n, :], start=True, stop=True), reads=['onesf', 'xd'], writes=[psk[pi]])
                copy(evac_eng(), stg[:, b, 0:c8n * 64], ps[pi][:, 0:c8n * 64], [psk[pi]], [('stg', b)])
                P.dma('pool', dec_d[:, c8 * c8n:(c8 + 1) * c8n, :], stg[:, b, 0:c8n * 64].rearrange("p (c k) -> p c k", k=64), reads=[('stg', b)], writes=['dec_d'])
            c4n = min(4, NT)
            for c4 in range(NT // c4n):
                pi = 6 + c4 % 2
                b = c4 % 2
                for i in range(c4n):
                    cc = c4 * c4n + i
                    P.op('pe', lambda e: e.transpose(out=ps[pi][:, i * 128:i * 128 + 64], in_=t1[:, cc * 128:(cc + 1) * 128], identity=identf[0:64, 0:64]), reads=['t1', 'identf'], writes=[psk[pi]])
                    P.op('pe', lambda e: e.transpose(out=ps[pi][:, i * 128 + 64:(i + 1) * 128], in_=t2[:, cc * 128:(cc + 1) * 128], identity=identf[0:64, 0:64]), reads=['t2', 'identf'], writes=[psk[pi]])
                copy(evac_eng(), stg[:, b, 0:c4n * 128], ps[pi][:, 0:c4n * 128], [psk[pi]], [('stg', b)])
                P.dma('pool', nbw_d[c4 * c4n * 128:(c4 + 1) * c4n * 128, :].rearrange("(n p) c -> p n c", p=128), stg[:, b, 0:c4n * 128].rearrange("p (n c) -> p n c", c=128),
                      reads=[('stg', b)], writes=['nbw_d'])
            P.barrier()
        with ExitStack() as es:
            hf = es.enter_context(SB("m2_hf", [128, 2048], F32))
            hfb = es.enter_context(SB("m2_hfb", [128, 2, 2048], BF16))
            decb = es.enter_context(SB("m2_dec", [128, NT, 64], F32))
            selt = es.enter_context(SB("m2_sel", [64, 16, 4], F32))
            ngf = es.enter_context(SB("m2_ngf", [128, 2, 512], F32))
            ng = es.enter_context(SB("m2_ng", [128, 2, 512], BF16))
            Dbc = es.enter_context(SB("m2_D", [128, 2048], F32))
            nwbc = es.enter_context(SB("m2_nw", [128, 2048], F32))
            bcf = es.enter_context(SB("m2_bc", [128, 2, 16, 128], BF16))
            xtm = es.enter_context(SB("m2_x", [128, 2, 2048], BF16))
            btm = es.enter_context(SB("m2_bt", [128, 2, 1024], BF16))
            nbw = es.enter_context(SB("m2_nbw", [128, 2, 128], F32))
            csc = es.enter_context(SB("m2_cs", [128, 2, 128], F32))
            csr = es.enter_context(SB("m2_csr", [128, 2, 128], F32))
            zt = es.enter_context(SB("m2_z", [128, 2, 2048], F32))
            hbt = es.enter_context(SB("m2_hb", [128, 2, 2048], BF16))
            xdg = es.enter_context(SB("m2_xd", [64, 2, 512], F32))
            ecs = es.enter_context(SB("m2_ecs", [128, 2, 64], F32))
            gg = es.enter_context(SB("m2_g", [128, 2, 2, 512], F32))
            selbig = es.enter_context(SB("m2_selbig", [128, 64, 128], F32))
            mt = es.enter_context(SB("m2_mt", [128, 2, 512], BF16))
            yo = es.enter_context(SB("m2_yo", [128, 2, 2048], F32))
            xd = es.enter_context(SB("m2_xd", [128, 2048], F32))
            xw = es.enter_context(SB("m2_xw", [128, 2, 256], BF16))
            tmpt = es.enter_context(SB("m2_tmp", [128, 2, 256], F32))
            y2 = es.enter_context(SB("m2_y", [128, 1024], F32))
            y3 = es.enter_context(SB("m2_y3", [128, 1024], F32))
            yb = es.enter_context(SB("m2_yb", [128, 1024], BF16))
            ss = es.enter_context(SB("m2_ss", [128, 8], F32))
            ytt = es.enter_context(SB("m2_yt", [128, 2, 8, 128], BF16))
            P.dma('sp', decb[:], dec_d[:, 0:NT, :], reads=['dec_d'], writes=['decb'])
            P.dma('sp', selt[:], sel_c, writes=['selt'])
            P.op('dve', lambda e: e.tensor_copy(out=selbig[0:64].bitcast(F32R), in_=identf[0:64, 0:64].unsqueeze(2).broadcast_to([64, 64, 128])), reads=['identf'], writes=['selbig'])
            P.op('dve', lambda e: e.tensor_copy(out=selbig[64:128].bitcast(F32R), in_=identf[64:128, 64:128].unsqueeze(2).broadcast_to([64, 64, 128])), reads=['identf'], writes=['selbig'])
            P.dma('sp', ngf[:], negm_c, writes=['ngf'])
            P.op('dve', lambda e: e.tensor_copy(out=ng[:], in_=ngf[:]), reads=['ngf'], writes=['ng'])
            P.dma('sp', Dbc[:], W['m_d_rep'][j].unsqueeze(0).broadcast_to([128, 2048]), writes=['Dbc'])
            P.dma('sp', nwbc[:], W['m_norm_w'][j].unsqueeze(0).broadcast_to([128, 2048]), writes=['nwbc'])

            def state_update_g(c, d, b, hstate, hkey, pbank, g):
                xb = (c * 8 + g) % 2
                P.op('pool', lambda e: e.tensor_tensor(out=xw[:, xb, :].rearrange("p (h q) -> p h q", h=4), in0=xtm[:, b, g * 256:(g + 1) * 256].rearrange("p (h q) -> p h q", h=4),
                                                        in1=nbw[:, b, 64 + d * 32 + g * 4:64 + d * 32 + g * 4 + 4].unsqueeze(2).broadcast_to([128, 4, 64]), op=ALU.mult),
                     reads=[('xtm', b), ('nbw', b)], writes=[('xw', xb)])
                P.op('pe', lambda e: e.matmul(ps[pbank][:, 0:256], btm[:, b, g * 128:(g + 1) * 128], xw[:, xb, :], start=True, stop=True), reads=[('btm', b), ('xw', xb)], writes=[psk[pbank]])
                P.op('dve', lambda e: e.tensor_tensor(out=tmpt[:, g % 2, :].rearrange("p (h q) -> p h q", h=4), in0=hstate[:, g * 256:(g + 1) * 256].rearrange("p (h q) -> p h q", h=4),
                                                       in1=decb[:, c, d * 32 + g * 4:d * 32 + g * 4 + 4].unsqueeze(2).broadcast_to([128, 4, 64]), op=ALU.mult),
                     reads=[(hkey, g), 'decb'], writes=[('tmpt', g % 2)])
                P.op('dve', lambda e: e.tensor_tensor(out=hstate[:, g * 256:(g + 1) * 256], in0=tmpt[:, g % 2, :], in1=ps[pbank][:, 0:256], op=ALU.add), reads=[('tmpt', g % 2), psk[pbank]], writes=[(hkey, g)])

            def state_update(c, d, b, hstate, hkey, pbank):
                for g in range(8):
                    state_update_g(c, d, b, hstate, hkey, pbank, g)

            hfkeys = [('hf', g) for g in range(8)]

            P.op('dve', lambda e: e.memset(hf[:], 0.0), writes=hfkeys)
            P.op('dve', lambda e: e.memset(hfb[:], 0.0), writes=[('hfb', 0), ('hfb', 1)])
            for ci, c in enumerate(range(NT - 1, -1, -1)):
                b = ci % 2
                P.dma('sp', xtm[:, b, :], xtm_d[c * 128:(c + 1) * 128, :], writes=[('xtm', b)])
                P.dma('sp', btm[:, b, :], btm_d[c * 128:(c + 1) * 128, :], writes=[('btm', b)])
                P.dma('sp', nbw[:, b, :], nbw_d[c * 128:(c + 1) * 128, :], writes=[('nbw', b)])
                P.dma('pool', hb_d[c], hfb[:, b, :], reads=[('hfb', b)], writes=['hb_d'])
                if c > 0:
                    state_update(c, 1, b, hf, 'hf', 6)
                    P.op('pool', lambda e: e.tensor_copy(out=hfb[:, 1 - b, :], in_=hf[:]), reads=hfkeys, writes=[('hfb', 1 - b)])
            P.barrier()
            P.op('dve', lambda e: e.memset(hf[:], 0.0), writes=hfkeys)
            P.op('dve', lambda e: e.memset(hfb[:], 0.0), writes=[('hfb', 0), ('hfb', 1)])
            for c in range(NT):
                b = c % 2
                P.dma('sp', bcf[:, b], xbcT_d[:, 16:32, c * 128:(c + 1) * 128], writes=[('bcf', b)])
                P.dma('sp', xtm[:, b, :], xtm_d[c * 128:(c + 1) * 128, :], writes=[('xtm', b)])
                P.dma('sp', btm[:, b, :], btm_d[c * 128:(c + 1) * 128, :], writes=[('btm', b)])
                P.dma('sp', nbw[:, b, :], nbw_d[c * 128:(c + 1) * 128, :], writes=[('nbw', b)])
                P.dma('sp', csc[:, b, :], cs_d[:, c * 128:(c + 1) * 128], writes=[('csc', b)])
                P.op('pool', lambda e: e.tensor_copy(out=csr[:, b, :].bitcast(F32R), in_=csc[:, b, :]), reads=[('csc', b)], writes=[('csr', b)])
                P.dma('sp', zt[:, b, :], z_d[c * 128:(c + 1) * 128, :], writes=[('zt', b)])
                P.dma('sp', hbt[:, b, :], hb_d[c], writes=[('hbt', b)])
                P.dma('sp', ecs[:, b, :], ecs_d[c * 128:(c + 1) * 128, :], writes=[('ecs', b)])
                def cbmm(half):
                    for gi in range(4):
                        g = half * 4 + gi
                        P.op('pe', lambda e: e.matmul(ps[0][:, gi * 128:(gi + 1) * 128], bcf[:, b, g, :], bcf[:, b, 8 + g, :], start=True, stop=True), reads=[('bcf', b)], writes=[psk[0]])

                def front_pe_act(g):
                    gp = g % 2
                    for d in range(2):
                        pB = 3 if d == 0 else 6
                        for hh in range(4):
                            krow = d * 32 + g * 4 + hh
                            P.op('pe', lambda e: e.matmul(ps[pB][:, hh * 128:(hh + 1) * 128], selbig[:, krow, :].bitcast(F32R), csr[:, b, :].bitcast(F32R), start=(hh == 0), stop=False), reads=['selbig', ('csr', b)], writes=[psk[pB]])
                        P.op('pe', lambda e: e.matmul(ps[pB][:, :], identb[:], ng[:, d, :], start=False, stop=True), reads=['identb', 'ng'], writes=[psk[pB]])
                        for hh in range(4):
                            hcol = d * 32 + g * 4 + hh
                            P.op('act', lambda e: e.activation(out=gg[:, gp, d, hh * 128:(hh + 1) * 128], in_=ps[pB][:, hh * 128:(hh + 1) * 128], func=AF.Exp, bias=nbw[:, b, hcol:hcol + 1]),
                                 reads=[psk[pB], ('nbw', b)], writes=[('gg', gp, d)])
                        hsrc, hk = (hfb, ('hfb', b)) if d == 0 else (hbt, ('hbt', b))
                        P.op('pe', lambda e: e.matmul(ps[1 + d][:, 0:256], bcf[:, b, 8 + g, :], hsrc[:, b, g * 256:(g + 1) * 256], start=True, stop=True), reads=[('bcf', b), hk], writes=[psk[1 + d]])

                def front_dve(g):
                    for d in range(2):
                        P.op('dve', lambda e: e.tensor_tensor(out=yo[:, d, g * 256:(g + 1) * 256].rearrange("p (h q) -> p h q", h=4), in0=ps[1 + d][:, 0:256].rearrange("p (h q) -> p h q", h=4),
                                                               in1=ecs[:, b, d * 32 + g * 4:d * 32 + g * 4 + 4].unsqueeze(2).broadcast_to([128, 4, 64]), op=ALU.mult),
                             reads=[psk[1 + d], ('ecs', b)], writes=[('yo', d, g // 4)])

                def back_dve(g):
                    gp = g % 2
                    gi = g % 4
                    P.op('dve', lambda e: e.tensor_tensor(out=gg[:, gp, 0, :], in0=gg[:, gp, 0, :], in1=gg[:, gp, 1, :], op=ALU.add), reads=[('gg', gp, 0), ('gg', gp, 1)], writes=[('gg', gp, 0)])
                    P.op('dve', lambda e: e.tensor_tensor(out=mt[:, gp, :].rearrange("p (h q) -> p h q", h=4), in0=gg[:, gp, 0, :].rearrange("p (h q) -> p h q", h=4),
                                                           in1=ps[0][:, gi * 128:(gi + 1) * 128].unsqueeze(1).broadcast_to([128, 4, 128]), op=ALU.mult),
                         reads=[('gg', gp, 0), psk[0]], writes=[('mt', gp)])

                def back_pe(g):
                    gp = g % 2
                    gi = g % 4
                    py = 4 + gi // 2
                    for hh in range(4):
                        h = g * 4 + hh
                        oc = (gi % 2) * 256 + hh * 64
                        P.op('pe', lambda e: e.matmul(ps[py][:, oc:oc + 64], mt[:, gp, hh * 128:(hh + 1) * 128], xtm[:, b, h * 64:(h + 1) * 64], start=True, stop=True),
                             reads=[('mt', gp), ('xtm', b)], writes=[psk[py]])

                def state_g(g):
                    xb = (c * 8 + g) % 2
                    P.op('pool', lambda e: e.tensor_tensor(out=xw[:, xb, :].rearrange("p (h q) -> p h q", h=4), in0=xtm[:, b, g * 256:(g + 1) * 256].rearrange("p (h q) -> p h q", h=4),
                                                            in1=nbw[:, b, 64 + g * 4:64 + g * 4 + 4].unsqueeze(2).broadcast_to([128, 4, 64]), op=ALU.mult),
                         reads=[('xtm', b), ('nbw', b)], writes=[('xw', xb)])
                    P.op('pe', lambda e: e.matmul(ps[7][:, 0:256], btm[:, b, g * 128:(g + 1) * 128], xw[:, xb, :], start=True, stop=True), reads=[('btm', b), ('xw', xb)], writes=[psk[7]])
                    P.op('dve', lambda e: e.tensor_tensor(out=tmpt[:, g % 2, :].rearrange("p (h q) -> p h q", h=4), in0=hf[:, g * 256:(g + 1) * 256].rearrange("p (h q) -> p h q", h=4),
                                                           in1=decb[:, c, g * 4:g * 4 + 4].unsqueeze(2).broadcast_to([128, 4, 64]), op=ALU.mult),
                         reads=[('hf', g), 'decb'], writes=[('tmpt', g % 2)])
                    P.op('dve', lambda e: e.tensor_tensor(out=hf[:, g * 256:(g + 1) * 256], in0=tmpt[:, g % 2, :], in1=ps[7][:, 0:256], op=ALU.add), reads=[('tmpt', g % 2), psk[7]], writes=[('hf', g)])
                    P.op('dve', lambda e: e.tensor_copy(out=hfb[:, 1 - b, g * 256:(g + 1) * 256], in_=hf[:, g * 256:(g + 1) * 256]), reads=[('hf', g)], writes=[('hfb', 1 - b)])

                def epilogue(half):
                    c0 = half * 1024
                    P.op('pool', lambda e: e.tensor_tensor(out=yo[:, 0, c0:c0 + 1024], in0=yo[:, 0, c0:c0 + 1024], in1=yo[:, 1, c0:c0 + 1024], op=ALU.add), reads=[('yo', 0, half), ('yo', 1, half)], writes=[('yo', 0, half)])
                    for q in range(2):
                        P.op('dve', lambda e: e.tensor_tensor(out=y2[:, q * 512:(q + 1) * 512], in0=xd[:, c0 + q * 512:c0 + (q + 1) * 512], in1=ps[4 + q][:, :], op=ALU.add), reads=['xd', psk[4 + q]], writes=['y2'])
                    P.op('dve', lambda e: e.tensor_tensor(out=y2[:], in0=y2[:], in1=yo[:, 0, c0:c0 + 1024], op=ALU.add), reads=['y2', ('yo', 0, half)], writes=['y2'])
                    P.op('dve', lambda e: e.tensor_tensor(out=y2[:], in0=y2[:], in1=zt[:, b, c0:c0 + 1024], op=ALU.mult), reads=['y2', ('zt', b)], writes=['y2'])
                    P.op('dve', lambda e: e.tensor_tensor(out=y3[:], in0=y2[:], in1=y2[:], op=ALU.mult), reads=['y2', 'y3'], writes=['y3'])
                    P.op('dve', lambda e: e.tensor_reduce(out=ss[:, 0:4], in_=y3[:].rearrange("p (g c) -> p g c", g=4), op=ALU.add, axis=AX.X), reads=['y3'], writes=['ss'])
                    P.op('dve', lambda e: e.tensor_scalar(out=ss[:, 0:4], in0=ss[:, 0:4], scalar1=1.0 / 256, scalar2=1e-5, op0=ALU.mult, op1=ALU.add), reads=['ss'], writes=['ss'])
                    P.op('act', lambda e: e.activation(out=ss[:, 0:4], in_=ss[:, 0:4], func=AF.Ln), reads=['ss'], writes=['ss'])
                    P.op('act', lambda e: e.activation(out=ss[:, 4:8], in_=ss[:, 0:4], func=AF.Exp, scale=-0.5), reads=['ss'], writes=['ss2'])
                    P.op('dve', lambda e: e.tensor_tensor(out=y2[:].rearrange("p (g c) -> p g c", g=4), in0=y2[:].rearrange("p (g c) -> p g c", g=4), in1=ss[:, 4:8].unsqueeze(2).broadcast_to([128, 4, 256]), op=ALU.mult),
                         reads=['y2', 'ss2'], writes=['y2'])
                    P.op('pool', lambda e: e.tensor_tensor(out=yb[:], in0=y2[:], in1=nwbc[:, c0:c0 + 1024], op=ALU.mult), reads=['y2', 'nwbc'], writes=['yb'])
                    pv = ps[7][:, :].bitcast(BF16)
                    for i in range(8):
                        P.op('pe', lambda e: e.transpose(out=pv[:, i * 128:(i + 1) * 128], in_=yb[:, i * 128:(i + 1) * 128], identity=identb[:]), reads=['yb', 'identb'], writes=[psk[7]])
                    yb2 = (c * 2 + half) % 2
                    copy('dve', ytt[:, yb2, :, :], pv.rearrange("p (n t) -> p n t", n=8), [psk[7]], [('ytt', yb2)])
                    P.dma('pool', yT_d[:, half * 8:(half + 1) * 8, c * 128:(c + 1) * 128], ytt[:, yb2, :, :], reads=[('ytt', yb2)], writes=['yT_d'])

                for q in range(2):
                    P.op('pool', lambda e: e.tensor_tensor(out=xd[:, q * 1024:(q + 1) * 1024], in0=xtm[:, b, q * 1024:(q + 1) * 1024], in1=Dbc[:, q * 1024:(q + 1) * 1024], op=ALU.mult), reads=[('xtm', b), 'Dbc'], writes=['xd'])
                cbmm(0)
                front_pe_act(0)
                front_dve(0)
                for g in range(8):
                    if g + 1 < 8:
                        front_pe_act(g + 1)
                    back_dve(g)
                    if g + 1 < 8:
                        front_dve(g + 1)
                    back_pe(g)
                    if c < NT - 1:
                        state_g(g)
                    if g == 3:
                        cbmm(1)
                    if g % 4 == 3:
                        epilogue(g // 4)
            P.barrier()

    def phase_rwkv(j, L):
        NC = L // 64
        NB = L // 512
        Lh = min(L, 1024)
        NH = L // Lh
        NCh = Lh // 64
        NBh = Lh // 512
        LAM = math.exp(-0.5)
        win = wb['r_w_in'][j]
        with SB("r1_xT", [128, 8, L], BF16) as xT, SB("r1_w", [128, 2, 8, 128], BF16) as wsl, \
                SB("r1_rb", [128, L + 2], F32) as rb, SB("r1_tmp", [128, L], F32) as tmp, \
                SB("r1_o", [128, 2, L], F32) as orow, SB("r1_mu", [128, 3, 34], F32) as mu:
            P.dma('sp', xT[:], xT_d[:, :, 0:L], writes=['r1_xT'])
            P.dma('sp', mu[:, 0, :], W['r_mu_t'][j], writes=['mu0'])
            P.op('dve', lambda e: e.tensor_scalar(out=mu[:, 1, :], in0=mu[:, 0, :], scalar1=0.5, scalar2=None, op0=ALU.mult), reads=['mu0'], writes=['mu1'])
            P.op('dve', lambda e: e.tensor_scalar(out=mu[:, 2, :], in0=mu[:, 0, :], scalar1=-1.0, scalar2=1.0, op0=ALU.mult, op1=ALU.add), reads=['mu0'], writes=['mu2'])
            P.op('dve', lambda e: e.memset(rb[:, 0:1], 0.0), writes=['rb'])
            P.op('dve', lambda e: e.memset(rb[:, L + 1:L + 2], 0.0), writes=['rb'])
            k = 0
            for fc in range(34):
                wbuf = fc % 2
                P.dma('sp', wsl[:, wbuf], win[:, :, fc * 128:(fc + 1) * 128], writes=[('r1w', wbuf)])
                for tb in range(NB):
                    pi = k % 6
                    k += 1
                    for dc in range(8):
                        P.op('pe', lambda e: e.matmul(ps[pi][:, :], wsl[:, wbuf, dc, :], xT[:, dc, tb * 512:(tb + 1) * 512], start=(dc == 0), stop=(dc == 7)),
                             reads=['r1_xT', ('r1w', wbuf)], writes=[psk[pi]])
                    P.op('act', lambda e: e.copy(out=rb[:, 1 + tb * 512:1 + (tb + 1) * 512], in_=ps[pi][:, :]), reads=[psk[pi]], writes=['rb'])
                P.op('dve', lambda e: e.tensor_tensor(out=tmp[:], in0=rb[:, 0:L], in1=rb[:, 2:L + 2], op=ALU.add), reads=['rb'], writes=['tmp'])
                P.op('dve', lambda e: e.tensor_scalar(out=tmp[:], in0=tmp[:], scalar1=mu[:, 1, fc:fc + 1], scalar2=None, op0=ALU.mult), reads=['tmp', 'mu1'], writes=['tmp'])
                ob_ = fc % 2
                P.op('dve', lambda e: e.scalar_tensor_tensor(out=orow[:, ob_, :], in0=rb[:, 1:L + 1], scalar=mu[:, 2, fc:fc + 1], in1=tmp[:], op0=ALU.mult, op1=ALU.add),
                     reads=['rb', 'tmp', 'mu2'], writes=[('orow', ob_)])
                P.dma('pool', u_d[:, fc, 0:L], orow[:, ob_, :], reads=[('orow', ob_)], writes=['u_d'])
            P.barrier()
        if RW_STAGE < 2:
            return
        with ExitStack() as es:
            A_ = lambda n, sh, dt: es.enter_context(SB(n, sh, dt))
            wupf = A_("r2_wupf", [128, 2, 1024], F32)
            wupb = A_("r2_wupb", [128, 2, 1024], BF16)
            cols = A_("r2_cols", [128, 9, 8], F32)
            blk = A_("r2_blk", [128, 128], F32)
            m64 = A_("r2_m64", [128, Lh], F32)
            lwt = A_("r2_lwt", [128, Lh], BF16)
            lat = A_("r2_lat", [128, Lh], BF16)
            rows = {n: A_("r2_" + n, [128, Lh], F32) for n in ('r', 'k', 'v', 'sg0', 'sg1', 'a0', 'a1', 'kk', 't1', 't2', 't4', 'bs', 'e0', 'e1', 'ei')}
            opn = A_("r2_opn", [128, 2, NCh, 4, 64], F32)
            gbs = A_("r2_gbs", [128, 2, Lh], F32)
            wcs = A_("r2_wcs", [128, 2, NCh], F32)
            P.dma('sp', wupf[:, 0, :], W['r_wup_t'][j], writes=['wupf'])
            P.dma('sp', wupf[:, 1, :], W['r_aup_t'][j], writes=['wupf'])
            P.op('dve', lambda e: e.tensor_copy(out=wupb[:], in_=wupf[:]), reads=['wupf'], writes=['wupb'])
            P.dma('sp', cols[:, 0:2, :], W['r_w0_t'][j], writes=['cols'])
            P.dma('sp', cols[:, 2:4, :], W['r_a0_t'][j], writes=['cols'])
            P.dma('sp', cols[:, 4, :], W['r_kk_t'][j], writes=['cols'])
            P.dma('sp', cols[:, 5, :], W['r_ka_t'][j], writes=['cols'])
            P.dma('sp', cols[:, 7, :], W['r_rk_t'][j], writes=['cols'])
            P.op('dve', lambda e: e.tensor_scalar(out=cols[:, 6, :], in0=cols[:, 5, :], scalar1=-1.0, scalar2=1.0, op0=ALU.mult, op1=ALU.add), reads=['cols'], writes=['cols6'])
            P.dma('sp', blk[:], blk_c, writes=['blk'])
            P.dma('sp', m64[:], mask64_c[:, 0:Lh], writes=['m64'])
            R = rows
            kq = 0
            for hf_ in range(NH):
                t0h = hf_ * Lh
                P.dma('sp', R['t1'][:], u_d[:, 32, t0h:t0h + Lh], writes=['t1'])
                P.op('act', lambda e: e.activation(out=lwt[:], in_=R['t1'][:], func=AF.Tanh), reads=['t1'], writes=['lwt'])
                P.dma('sp', R['t2'][:], u_d[:, 33, t0h:t0h + Lh], writes=['t2'])
                P.op('act', lambda e: e.copy(out=lat[:], in_=R['t2'][:]), reads=['t2'], writes=['lat'])
                for cc in range(8):
                    P.dma('sp', R['r'][:], u_d[:, cc, t0h:t0h + Lh], writes=['r'])
                    P.dma('sp', R['k'][:], u_d[:, 8 + cc, t0h:t0h + Lh], writes=['k'])
                    P.dma('sp', R['v'][:], u_d[:, 16 + cc, t0h:t0h + Lh], writes=['v'])
                    P.dma('sp', R['t4'][:], u_d[:, 24 + cc, t0h:t0h + Lh], writes=['t4'])
                    P.op('act', lambda e: e.activation(out=gbs[:, 0, :], in_=R['t4'][:], func=AF.Silu), reads=['t4'], writes=['gbs0'])
                    for tb in range(NBh):
                        sl = slice(tb * 512, (tb + 1) * 512)
                        for d in range(2):
                            P.op('pe', lambda e: e.matmul(ps[d][:, :], wupb[d * 64:(d + 1) * 64, 0, cc * 128:(cc + 1) * 128], lwt[d * 64:(d + 1) * 64, sl], start=True, stop=True),
                                 reads=['wupb', 'lwt'], writes=[psk[d]])
                            P.op('act', lambda e: e.activation(out=R['sg%d' % d][:, sl], in_=ps[d][:, :], func=AF.Sigmoid, bias=cols[:, d, cc:cc + 1]), reads=[psk[d], 'cols'], writes=['sg%d' % d])
                            P.op('pe', lambda e: e.matmul(ps[2 + d][:, :], wupb[d * 64:(d + 1) * 64, 1, cc * 128:(cc + 1) * 128], lat[d * 64:(d + 1) * 64, sl], start=True, stop=True),
                                 reads=['wupb', 'lat'], writes=[psk[2 + d]])
                            P.op('act', lambda e: e.activation(out=R['a%d' % d][:, sl], in_=ps[2 + d][:, :], func=AF.Sigmoid, bias=cols[:, 2 + d, cc:cc + 1]), reads=[psk[2 + d], 'cols'], writes=['a%d' % d])
                    P.op('dve', lambda e: e.tensor_scalar(out=R['kk'][:], in0=R['k'][:], scalar1=cols[:, 4, cc:cc + 1], scalar2=None, op0=ALU.mult), reads=['k', 'cols'], writes=['kk'])
                    P.op('dve', lambda e: e.tensor_tensor(out=R['t1'][:], in0=R['kk'][:], in1=R['kk'][:], op=ALU.mult), reads=['kk', 't1'], writes=['t1'])
                    for tb in range(NBh):
                        sl = slice(tb * 512, (tb + 1) * 512)
                        pi = 4 + tb % 2
                        P.op('pe', lambda e: e.matmul(ps[pi][:, :], blk[:], R['t1'][:, sl], start=True, stop=True), reads=['blk', 't1'], writes=[psk[pi]])
                        P.op('act', lambda e: e.activation(out=R['t2'][:, sl], in_=ps[pi][:, :], func=AF.Sqrt), reads=[psk[pi], 't2'], writes=['t2'])
                    P.op('dve', lambda e: e.tensor_scalar(out=R['t2'][:], in0=R['t2'][:], scalar1=1e-12, scalar2=None, op0=ALU.max), reads=['t2'], writes=['t2'])
                    P.op('dve', lambda e: e.reciprocal(out=R['t2'][:], in_=R['t2'][:]), reads=['t2'], writes=['t2'])
                    P.op('dve', lambda e: e.tensor_tensor(out=R['kk'][:], in0=R['kk'][:], in1=R['t2'][:], op=ALU.mult), reads=['kk', 't2'], writes=['kk'])
                    for d in range(2):
                        sg = R['sg%d' % d]
                        a_ = R['a%d' % d]
                        ob2 = kq % 2
                        kq += 1
                        P.op('dve', lambda e: e.tensor_scalar(out=R['t1'][:], in0=a_[:], scalar1=cols[:, 5, cc:cc + 1], scalar2=cols[:, 6, cc:cc + 1], op0=ALU.mult, op1=ALU.add), reads=['a%d' % d, 'cols', 'cols6', 't1'], writes=['t1'])
                        P.op('dve', lambda e: e.tensor_tensor(out=R['t1'][:], in0=R['t1'][:], in1=R['k'][:], op=ALU.mult), reads=['t1', 'k'], writes=['t1'])
                        if d == 0:
                            P.op('dve', lambda e: e.scalar_tensor_tensor(out=R['bs'][:], in0=R['t1'][:], scalar=cols[:, 7, cc:cc + 1], in1=R['r'][:], op0=ALU.mult, op1=ALU.mult), reads=['t1', 'r', 'cols', 'bs'], writes=['bs'])
                        else:
                            P.op('dve', lambda e: e.scalar_tensor_tensor(out=R['t2'][:], in0=R['t1'][:], scalar=cols[:, 7, cc:cc + 1], in1=R['r'][:], op0=ALU.mult, op1=ALU.mult), reads=['t1', 'r', 'cols', 't2'], writes=['t2'])
                            P.op('dve', lambda e: e.tensor_tensor(out=R['bs'][:], in0=R['bs'][:], in1=R['t2'][:], op=ALU.add), reads=['bs', 't2'], writes=['bs'])
                        P.op('dve', lambda e: e.tensor_tensor_scan(out=R['t2'][:], data0=m64[:], data1=sg[:], initial=0.0, op0=ALU.mult, op1=ALU.add), reads=['m64', 'sg%d' % d, 't2'], writes=['t2'])
                        t23 = R['t2'][:].rearrange("p (c t) -> p c t", t=64)
                        t43 = R['t4'][:].rearrange("p (c t) -> p c t", t=64)
                        if d == 1:
                            P.op('dve', lambda e: e.tensor_tensor(out=t43, in0=t23[:, :, 63:64].broadcast_to([128, NCh, 64]), in1=t23, op=ALU.subtract), reads=['t2', 't4'], writes=['t4'])
                            P.op('dve', lambda e: e.tensor_tensor(out=R['t2'][:], in0=R['t4'][:], in1=sg[:], op=ALU.add), reads=['t4', 'sg%d' % d], writes=['t2'])
                        P.op('act', lambda e: e.activation(out=R['e1'][:], in_=R['t2'][:], func=AF.Exp, scale=-LAM), reads=['t2', 'e1'], writes=['e1'])
                        P.op('act', lambda e: e.activation(out=R['ei'][:], in_=R['t2'][:], func=AF.Exp, scale=LAM), reads=['t2', 'ei'], writes=['ei'])
                        P.op('dve', lambda e: e.tensor_tensor(out=R['t4'][:], in0=R['t2'][:], in1=sg[:], op=ALU.subtract), reads=['t2', 'sg%d' % d, 't4'], writes=['t4'])
                        P.op('act', lambda e: e.activation(out=R['e0'][:], in_=R['t4'][:], func=AF.Exp, scale=-LAM), reads=['t4', 'e0'], writes=['e0'])
                        e13 = R['e1'][:].rearrange("p (c t) -> p c t", t=64)
                        ecol = 63 if d == 0 else 0
                        P.op('act', lambda e: e.copy(out=wcs[:, d, :], in_=e13[:, :, ecol]), reads=['e1'], writes=['wcs'])
                        def o3(kind):
                            return opn[:, ob2, :, kind, :]
                        def v3(n):
                            return R[n][:].rearrange("p (c t) -> p c t", t=64)
                        P.op('dve', lambda e: e.tensor_tensor(out=R['t4'][:], in0=R['kk'][:], in1=a_[:], op=ALU.mult), reads=['kk', 'a%d' % d, 't4'], writes=['t4'])
                        P.op('dve', lambda e: e.tensor_tensor(out=o3(0), in0=v3('t4'), in1=v3('ei'), op=ALU.mult), reads=['t4', 'ei'], writes=[('opn', ob2)])
                        P.op('pool', lambda e: e.tensor_tensor(out=o3(1), in0=v3('t1'), in1=v3('ei'), op=ALU.mult), reads=['t1', 'ei'], writes=[('opn', ob2)])
                        P.op('dve', lambda e: e.scalar_tensor_tensor(out=o3(2), in0=v3('kk'), scalar=-1.0, in1=v3('e0'), op0=ALU.mult, op1=ALU.mult), reads=['kk', 'e0'], writes=[('opn', ob2)])
                        P.op('pool', lambda e: e.tensor_tensor(out=o3(3), in0=v3('r'), in1=v3('e1'), op=ALU.mult), reads=['r', 'e1'], writes=[('opn', ob2)])
                        for hh in range(2):
                            P.dma('pool', rop_d[:, 2 * cc + hh, d, hf_ * NCh:(hf_ + 1) * NCh, :, :], opn[hh * 64:(hh + 1) * 64, ob2], reads=[('opn', ob2)], writes=['rop_d'])
                    for tb in range(NBh):
                        sl = slice(tb * 512, (tb + 1) * 512)
                        pi = 6 + tb % 2
                        P.op('pe', lambda e: e.matmul(ps[pi][:, :], blk[:], R['bs'][:, sl], start=True, stop=True), reads=['blk', 'bs'], writes=[psk[pi]])
                        P.op('dve', lambda e: e.tensor_tensor(out=gbs[:, 1, sl], in0=ps[pi][:, :], in1=R['v'][:, sl], op=ALU.mult), reads=[psk[pi], 'v'], writes=['gbs1'])
                    P.dma('pool', gb_d[:, cc, :, t0h:t0h + Lh], gbs[:], reads=['gbs0', 'gbs1'], writes=['gb_d'])
                    for hh in range(2):
                        P.dma('pool', wc_d[:, 2 * cc + hh, :, hf_ * NCh:(hf_ + 1) * NCh], wcs[hh * 64:(hh + 1) * 64], reads=['wcs'], writes=['wc_d'])
            P.barrier()
        if RW_STAGE < 3:
            return
        with ExitStack() as es:
            A_ = lambda n, sh, dt: es.enter_context(SB(n, sh, dt))
            fr = lambda ap: ap.bitcast(F32R)
            mk = A_("r3_mk", [128, 2, 128], F32)
            mkn = A_("r3_mkn", [64, 2, 64], F32)
            wcall = A_("r3_wc", [64, 16, 2, NC], F32)
            lncol = A_("r3_ln", [128, 2, 8], F32)
            rop = A_("r3_rop", [128, 2, 16, 4, 64], F32)
            vf = A_("r3_vf", [128, 2, 8, 128], F32)
            ropr = A_("r3_ropr", [128, 2, 16, 4, 64], F32)
            Hsr = A_("r3_Hsr", [128, 16, 64], F32)
            AT = A_("r3_AT", [128, 16, 128], F32)
            PT = A_("r3_PT", [128, 2, 16, 2, 64], F32)
            Qm = A_("r3_Q", [128, 2, 16, 64], F32)
            Z = A_("r3_Z", [128, 16, 64], F32)
            Zx = A_("r3_Zx", [128, 16, 64], F32)
            Xs = A_("r3_X", [128, 16, 64], F32)
            BKs = A_("r3_BK", [128, 16, 64], F32)
            Hs = A_("r3_H", [128, 16, 64], F32)
            Ht = A_("r3_Ht", [64, 16, 64], F32)
            ysb = A_("r3_y", [128, 2, 1024], F32)
            ybl = A_("r3_yb", [64, 2, 1024], F32)
            ysq = A_("r3_ysq", [64, 1024], F32)
            gst = A_("r3_gst", [64, 4, 16], F32)
            gbt = A_("r3_gb", [128, 2, 8, 2, 64], F32)
            ofm = A_("r3_ofm", [128, 512], F32)
            ofb = A_("r3_ofb", [128, 2, 8, 64], BF16)
            P.dma('sp', mk[:], rmask_c, writes=['mk'])
            P.dma('sp', mkn[:], rmaskn_c, writes=['mkn'])
            P.dma('sp', wcall[:], wc_d[:, :, :, 0:NC], writes=['wcall'])
            P.dma('sp', lncol[:, 0, :], W['r_lnw_t'][j], writes=['lncol'])
            P.dma('sp', lncol[:, 1, :], W['r_lnb_t'][j], writes=['lncol'])
            P.op('dve', lambda e: e.memset(vf[:], 0.0), writes=[('vf', 0), ('vf', 1)])
            P.op('dve', lambda e: e.memset(rop[:, 0].rearrange("p h k t -> p (h k t)"), 0.0), writes=[('rop', 0)])
            P.op('dve', lambda e: e.memset(rop[:, 1].rearrange("p h k t -> p (h k t)"), 0.0), writes=[('rop', 1)])
            P.op('dve', lambda e: e.memset(ropr[:, 0].rearrange("p h k t -> p (h k t)"), 0.0), writes=[('ropr', 0)])
            P.op('dve', lambda e: e.memset(ropr[:, 1].rearrange("p h k t -> p (h k t)"), 0.0), writes=[('ropr', 1)])
            P.op('dve', lambda e: e.memset(Hsr[:], 0.0), writes=['Hsr'])
            if RW_X != 2:
                P.op('dve', lambda e: e.memset(PT[:].rearrange("p a h k t -> p (a h k t)"), 0.0), writes=[('PT', 0, 0), ('PT', 0, 1), ('PT', 1, 0), ('PT', 1, 1)])
                P.op('dve', lambda e: e.memset(Qm[:], 0.0), writes=[('Q', 0, 0), ('Q', 0, 1), ('Q', 1, 0), ('Q', 1, 1)])
                P.op('dve', lambda e: e.memset(Zx[:], 0.0), writes=['Zx'])
                P.op('dve', lambda e: e.memset(Xs[:], 0.0), writes=[('Xs', 0), ('Xs', 1)])
                P.op('dve', lambda e: e.memset(ysb[:], 0.0), writes=[('ysb', 0), ('ysb', 1)])

            def hview(bank_pair, q, w_):
                return ps[bank_pair + q][0:64, :].rearrange("p (h t) -> p h t", t=w_)

            def prefetch(si, c, d):
                b = si % 2
                P.dma('sp', rop[0:64, b], rop_d[:, :, d, c, :, :], writes=[('rop', b)])
                P.op('pool', lambda e: e.tensor_copy(out=fr(ropr[0:64, b].rearrange("p h k t -> p (h k t)")), in_=rop[0:64, b].rearrange("p h k t -> p (h k t)")), reads=[('rop', b)], writes=[('ropr', b)])
                P.dma('sp', vf[:, b, :, 64:128], u_d[:, 16:24, c * 64:(c + 1) * 64], writes=[('vf', b)])

            def step(si, c, d, nxt_c=None):
                b = si % 2
                for cc in range(8):
                    pi = 6 + cc // 4
                    P.op('pe', lambda e: e.transpose(out=ps[pi][:, (cc % 4) * 128:(cc % 4 + 1) * 128], in_=vf[:, b, cc, :], identity=identf[:]), reads=[('vf', b), 'identf'], writes=[psk[pi]])
                for q in range(2):
                    src = ps[6 + q][64:128, :].rearrange("p (h v) -> p h v", v=64)
                    copy('act', fr(Z[64:128, q * 8:(q + 1) * 8, :]), src, [psk[6 + q]], ['Zv'])
                    copy('pool', fr(Zx[64:128, q * 8:(q + 1) * 8, :]), Z[64:128, q * 8:(q + 1) * 8, :], ['Zv'], ['Zx'])
                if RW_SUB < 2:
                    return b
                for h in range(16):
                    pi = h // 4
                    P.op('pe', lambda e: e.matmul(ps[pi][:, (h % 4) * 128:(h % 4 + 1) * 128], fr(ropr[:, b, h, 0:2, :].rearrange("p a t -> p (a t)")),
                                                  fr(ropr[:, b, h, 2:4, :].rearrange("p a t -> p (a t)")), start=True, stop=True), reads=[('ropr', b)], writes=[psk[pi]])
                for q in range(4):
                    P.op('dve', lambda e: e.tensor_tensor(out=fr(AT[:, q * 4:(q + 1) * 4, :]), in0=ps[q][:, :].rearrange("p (h t) -> p h t", t=128),
                                                           in1=mk[:, d:d + 1, :].broadcast_to([128, 4, 128]), op=ALU.mult), reads=[psk[q], 'mk'], writes=['AT'])
                if nxt_c is not None:
                    prefetch(si + 1, nxt_c, d)
                for h in range(16):
                    pi = 4 + h // 8
                    P.op('pe', lambda e: e.matmul(ps[pi][0:64, (h % 8) * 64:(h % 8 + 1) * 64], fr(ropr[:, b, h, 2, :]), fr(ropr[:, b, h, 0, :]), start=True, stop=True),
                         reads=[('ropr', b)], writes=[psk[pi]])
                for q in range(2):
                    P.op('dve', lambda e: e.tensor_tensor(out=fr(Qm[0:64, 0, q * 8:(q + 1) * 8, :]), in0=hview(4, q, 64),
                                                           in1=mkn[:, d:d + 1, :].broadcast_to([64, 8, 64]), op=ALU.mult), reads=[psk[4 + q], 'mkn'], writes=[('Q', 0, q)])
                P.op('act', lambda e: e.activation(out=fr(PT[0:64, 0, :, 0, :]), in_=AT[0:64, :, 0:64], func=AF.Identity), reads=['AT'], writes=[('PT', 0, 0), ('PT', 0, 1)])
                P.op('dve', lambda e: e.tensor_tensor(out=fr(PT[0:64, 1, :, 1, :]), in0=AT[0:64, :, 0:64], in1=identf[0:64, 0:64].unsqueeze(1).broadcast_to([64, 16, 64]), op=ALU.add),
                     reads=['AT', 'identf'], writes=[('PT', 1, 0), ('PT', 1, 1)])
                if RW_SUB < 4:
                    return b
                for h in range(16):
                    pi = h // 4
                    P.op('pe', lambda e: e.transpose(out=ps[pi][:, (h % 4) * 128:(h % 4 + 1) * 128], in_=rop[:, b, h, 0:2, :].rearrange("p a t -> p (a t)"), identity=identf[:]),
                         reads=[('rop', b), 'identf'], writes=[psk[pi]])
                for q in range(4):
                    copy('act' if q % 2 == 0 else 'dve', fr(BKs[:, q * 4:(q + 1) * 4, :]), ps[q][:, :].rearrange("p (h k) -> p h k", k=128)[:, :, 0:64], [psk[q]], ['BKs'])
                if RW_STAGE < 4:
                    return b
                for q in range(2):
                    for h in range(q * 8, q * 8 + 8):
                        P.op('pe', lambda e: e.matmul(ps[0 + q][0:64, (h % 8) * 64:(h % 8 + 1) * 64], fr(Qm[:, 0, h, :]), fr(PT[:, 0, h, 0, :]), start=True, stop=True),
                             reads=[('Q', 0, q), ('PT', 0, q)], writes=[psk[0 + q]])
                    for h in range(q * 8, q * 8 + 8):
                        P.op('pe', lambda e: e.matmul(ps[4 + q][0:64, (h % 8) * 64:(h % 8 + 1) * 64], fr(PT[:, 0, h, 0, :]), fr(Qm[:, 0, h, :]), start=True, stop=True),
                             reads=[('Q', 0, q), ('PT', 0, q)], writes=[psk[4 + q]])
                    copy('act', fr(PT[0:64, 1, q * 8:(q + 1) * 8, 0, :]), hview(0, q, 64), [psk[0 + q]], [('PT', 1, q)])
                    copy('dve', fr(Qm[0:64, 1, q * 8:(q + 1) * 8, :]), hview(4, q, 64), [psk[4 + q]], [('Q', 1, q)])
                cur = 1
                for lv in range(1, 5):
                    nxt = 1 - cur
                    for q in range(2):
                        for h in range(q * 8, q * 8 + 8):
                            pi = 2 * q + (h % 8) // 4
                            P.op('pe', lambda e: e.matmul(ps[pi][0:64, (h % 4) * 128:(h % 4 + 1) * 128], fr(Qm[:, cur, h, :]), fr(PT[:, cur, h, :, :].rearrange("p k t -> p (k t)")), start=True, stop=True),
                                 reads=[('Q', cur, q), ('PT', cur, q)], writes=[psk[pi]])
                        for h in range(q * 8, q * 8 + 8):
                            P.op('pe', lambda e: e.matmul(ps[4 + q][0:64, (h % 8) * 64:(h % 8 + 1) * 64], fr(PT[:, cur, h, 0, :]), fr(Qm[:, cur, h, :]), start=True, stop=True),
                                 reads=[('Q', cur, q), ('PT', cur, q)], writes=[psk[4 + q]])
                        for hb in range(2):
                            pi = 2 * q + hb
                            h0 = q * 8 + hb * 4
                            pv3 = ps[pi][0:64, :].rearrange("p (h k t) -> p h k t", k=2, t=64)
                            if lv < 4:
                                P.op('act', lambda e: e.activation(out=fr(PT[0:64, nxt, h0:h0 + 4, 0, :]), in_=pv3[:, :, 0, :], func=AF.Identity), reads=[psk[pi]], writes=[('PT', nxt, q)])
                            P.op('dve', lambda e: e.tensor_tensor(out=fr(PT[0:64, nxt, h0:h0 + 4, 1, :]), in0=pv3[:, :, 1, :], in1=PT[0:64, cur, h0:h0 + 4, 1, :], op=ALU.add),
                                 reads=[psk[pi], ('PT', cur, q)], writes=[('PT', nxt, q)])
                        copy('act' if lv == 4 else 'dve', fr(Qm[0:64, nxt, q * 8:(q + 1) * 8, :]), hview(4, q, 64), [psk[4 + q]], [('Q', nxt, q)])
                    cur = nxt
                nxt = 1 - cur
                for q in range(2):
                    for h in range(q * 8, q * 8 + 8):
                        P.op('pe', lambda e: e.matmul(ps[0 + q][0:64, (h % 8) * 64:(h % 8 + 1) * 64], fr(Qm[:, cur, h, :]), fr(PT[:, cur, h, 1, :]), start=True, stop=True),
                             reads=[('Q', cur, q), ('PT', cur, q)], writes=[psk[0 + q]])
                    P.op('dve', lambda e: e.tensor_tensor(out=fr(PT[0:64, nxt, q * 8:(q + 1) * 8, 1, :]), in0=hview(0, q, 64), in1=PT[0:64, cur, q * 8:(q + 1) * 8, 1, :], op=ALU.add),
                         reads=[psk[0 + q], ('PT', cur, q)], writes=[('PT', nxt, q)])
                cur = nxt
                if RW_STAGE < 5:
                    return b
                for q in range(2):
                    for h in range(q * 8, q * 8 + 8):
                        oc = (h % 8) * 64
                        P.op('pe', lambda e: e.matmul(ps[0 + q][0:64, oc:oc + 64], fr(ropr[:, b, h, 2, :]), fr(Hsr[:, h, :]), start=True, stop=False), reads=[('ropr', b), 'Hsr'], writes=[psk[0 + q]])
                        P.op('pe', lambda e: e.matmul(ps[0 + q][0:64, oc:oc + 64], fr(AT[:, h, 0:64]), fr(Zx[:, h, :]), start=False, stop=True), reads=['AT', 'Zx'], writes=[psk[0 + q]])
                    copy('act' if q == 0 else 'dve', fr(Xs[0:64, q * 8:(q + 1) * 8, :]), hview(0, q, 64), [psk[q]], [('Xs', q)])
                for q in range(2):
                    for h in range(q * 8, q * 8 + 8):
                        oc = (h % 8) * 64
                        P.op('pe', lambda e: e.matmul(ps[2 + q][0:64, oc:oc + 64], fr(PT[:, cur, h, 1, :]), fr(Xs[:, h, :]), start=True, stop=True), reads=[('PT', cur, q), ('Xs', q)], writes=[psk[2 + q]])
                    copy('act' if q == 0 else 'dve', fr(Z[0:64, q * 8:(q + 1) * 8, :]), hview(2, q, 64), [psk[2 + q]], [('Zu', q)])
                for q in range(2):
                    for h in range(q * 8, q * 8 + 8):
                        oc = (h % 8) * 64
                        P.op('pe', lambda e: e.matmul(ps[4 + q][0:64, oc:oc + 64], fr(ropr[:, b, h, 3, :]), fr(Hsr[:, h, :]), start=True, stop=False), reads=[('ropr', b), 'Hsr'], writes=[psk[4 + q]])
                        P.op('pe', lambda e: e.matmul(ps[4 + q][0:64, oc:oc + 64], fr(AT[:, h, 64:128]), fr(Z[:, h, :]), start=False, stop=True), reads=['AT', ('Zu', q), 'Zv'], writes=[psk[4 + q]])
                if RW_STAGE < 6:
                    return b
                for h in range(16):
                    pi = 6 + h // 8
                    oc = (h % 8) * 64
                    P.op('pe', lambda e: e.matmul(ps[pi][0:64, oc:oc + 64], fr(BKs[:, h, :]), fr(Z[:, h, :]), start=True, stop=True), reads=['BKs', ('Zu', h // 8), 'Zv'], writes=[psk[pi]])
                for q in range(2):
                    P.op('dve', lambda e: e.tensor_tensor(out=Ht[:, q * 8:(q + 1) * 8, :], in0=Hs[0:64, q * 8:(q + 1) * 8, :], in1=hview(6, q, 64), op=ALU.add), reads=['Hs', psk[6 + q]], writes=['Ht'])
                P.op('dve', lambda e: e.tensor_tensor(out=Hs[0:64], in0=Ht[:], in1=wcall[:, :, d, c:c + 1].broadcast_to([64, 16, 64]), op=ALU.mult), reads=['Ht', 'wcall'], writes=['Hs'])
                P.op('pool', lambda e: e.tensor_copy(out=fr(Hsr[0:64]), in_=Hs[0:64]), reads=['Hs'], writes=['Hsr'])
                return b

            P.op('dve', lambda e: e.memset(Hs[:], 0.0), writes=['Hs'])
            P.op('dve', lambda e: e.memset(Hsr[:], 0.0), writes=['Hsr'])
            seqB = list(range(NC - 1, -1, -1))
            prefetch(0, seqB[0], 1)
            for si, c in enumerate(seqB):
                b = step(si, c, 1, seqB[si + 1] if si + 1 < NC else None)
                if RW_STAGE < 9:
                    continue
                for q in range(2):
                    copy('act' if q == 0 else 'dve', ysb[0:64, b, q * 512:(q + 1) * 512], ps[4 + q][0:64, :], [psk[4 + q]], [('ysb', b)])
                P.dma('pool', ybt_d[c * 64:(c + 1) * 64, :], ysb[0:64, b, :], reads=[('ysb', b)], writes=['ybt_d'])
            P.barrier()
            P.op('dve', lambda e: e.memset(Hs[:], 0.0), writes=['Hs'])
            P.op('dve', lambda e: e.memset(Hsr[:], 0.0), writes=['Hsr'])
            prefetch(0, 0, 0)
            for si, c in enumerate(range(NC)):
                b = si % 2
                P.dma('sp', ybl[:, b, :], ybt_d[c * 64:(c + 1) * 64, :], writes=[('ybl', b)])
                P.dma('sp', gbt[:, b], gb_d[:, :, :, c * 64:(c + 1) * 64], writes=[('gbt', b)])
                step(si, c, 0, c + 1 if c + 1 < NC else None)
                if RW_STAGE < 9:
                    continue
                for q in range(2):
                    P.op('dve', lambda e: e.tensor_tensor(out=ysb[0:64, b, q * 512:(q + 1) * 512], in0=ybl[:, b, q * 512:(q + 1) * 512], in1=ps[4 + q][0:64, :], op=ALU.add),
                         reads=[('ybl', b), psk[4 + q]], writes=[('ysb', b)])
                y3 = ysb[0:64, b, :].rearrange("p (h v) -> p h v", v=64)
                P.op('dve', lambda e: e.tensor_reduce(out=gst[:, 0, :], in_=y3, op=ALU.add, axis=AX.X), reads=[('ysb', b)], writes=['gst0'])
                P.op('pool', lambda e: e.tensor_tensor(out=ysq[:], in0=ysb[0:64, b, :], in1=ysb[0:64, b, :], op=ALU.mult), reads=[('ysb', b)], writes=['ysq'])
                P.op('dve', lambda e: e.tensor_reduce(out=gst[:, 1, :], in_=ysq[:].rearrange("p (h v) -> p h v", v=64), op=ALU.add, axis=AX.X), reads=['ysq'], writes=['gst1'])
                P.op('dve', lambda e: e.tensor_scalar(out=gst[:, 0, :], in0=gst[:, 0, :], scalar1=1.0 / 64, scalar2=None, op0=ALU.mult), reads=['gst0'], writes=['gst0'])
                P.op('dve', lambda e: e.tensor_tensor(out=gst[:, 2, :], in0=gst[:, 0, :], in1=gst[:, 0, :], op=ALU.mult), reads=['gst0'], writes=['gst2'])
                P.op('dve', lambda e: e.scalar_tensor_tensor(out=gst[:, 1, :], in0=gst[:, 1, :], scalar=1.0 / 64, in1=gst[:, 2, :], op0=ALU.mult, op1=ALU.subtract), reads=['gst1', 'gst2'], writes=['gst1'])
                P.op('dve', lambda e: e.tensor_scalar(out=gst[:, 1, :], in0=gst[:, 1, :], scalar1=64e-5, scalar2=None, op0=ALU.add), reads=['gst1'], writes=['gst1'])
                P.op('act', lambda e: e.activation(out=gst[:, 1, :], in_=gst[:, 1, :], func=AF.Sqrt), reads=['gst1'], writes=['gst1'])
                P.op('dve', lambda e: e.reciprocal(out=gst[:, 3, :], in_=gst[:, 1, :]), reads=['gst1'], writes=['gst3'])
                P.op('dve', lambda e: e.tensor_tensor(out=y3, in0=y3, in1=gst[:, 0, :].unsqueeze(2).broadcast_to([64, 16, 64]), op=ALU.subtract), reads=[('ysb', b), 'gst0'], writes=[('ysb', b)])
                P.op('dve', lambda e: e.tensor_tensor(out=y3, in0=y3, in1=gst[:, 3, :].unsqueeze(2).broadcast_to([64, 16, 64]), op=ALU.mult), reads=[('ysb', b), 'gst3'], writes=[('ysb', b)])
                for cc in range(8):
                    pi = 6 + cc // 4
                    P.op('pe', lambda e: e.transpose(out=ps[pi][:, (cc % 4) * 128:(cc % 4 + 1) * 128], in_=ysb[:, b, cc * 128:(cc + 1) * 128], identity=identf[:]), reads=[('ysb', b), 'identf'], writes=[psk[pi]])
                o3_ = ofm[:].rearrange("p (c t) -> p c t", t=64)
                for q in range(2):
                    P.op('dve', lambda e: e.tensor_tensor(out=o3_[:, q * 4:(q + 1) * 4, :], in0=ps[6 + q][:, :].rearrange("p (c t) -> p c t", t=128)[:, :, 0:64],
                                                           in1=lncol[:, 0, q * 4:(q + 1) * 4].unsqueeze(2).broadcast_to([128, 4, 64]), op=ALU.mult), reads=[psk[6 + q], 'lncol'], writes=['ofm'])
                P.op('pool', lambda e: e.tensor_tensor(out=o3_, in0=o3_, in1=lncol[:, 1, :].unsqueeze(2).broadcast_to([128, 8, 64]), op=ALU.add), reads=['ofm', 'lncol'], writes=['ofm'])
                P.op('pool', lambda e: e.tensor_tensor(out=o3_, in0=o3_, in1=gbt[:, b, :, 1, :], op=ALU.add), reads=['ofm', ('gbt', b)], writes=['ofm'])
                P.op('pool', lambda e: e.tensor_tensor(out=ofb[:, b], in0=o3_, in1=gbt[:, b, :, 0, :], op=ALU.mult), reads=['ofm', ('gbt', b)], writes=[('ofb', b)])
                P.dma('pool', yT_d[:, 0:8, c * 64:(c + 1) * 64], ofb[:, b], reads=[('ofb', b)], writes=['yT_d'])
            P.barrier()

    t0 = 0
    for L in seq_lens:
        phase_prep(t0, L)
        P.lastw['xres_src'] = None
        for li, (kind, j) in enumerate(layers):
            last = (li == nl - 1)
            x_src = xin[t0:t0 + L, :] if li == 0 else xres[(li - 1) % 2][0:L, :]
            x_dst = yout[t0:t0 + L, :] if last else xres[li % 2][0:L, :]
            if kind == 'm':
                phase_mamba(j, L)
                phase_out(li, t0, L, 16, wb['m_w_out'][j], x_src, x_dst, last)
            if kind == 'r':
                phase_rwkv(j, L)
                phase_out(li, t0, L, 8, wb['r_w_out'][j], x_src, x_dst, last)
            if kind == 'a':
                phase_attn(j, L)
                phase_out(li, t0, L, 8, wb['a_w_out'][j], x_src, x_dst, last)
        t0 += L
    P.barrier()
    return nc


def host_consts(Lmax):
    t = np.arange(Lmax)
    row = (t // 64).astype(np.float32)
    col = (t % 64).astype(np.float32)
    inv = (10000.0 ** (-np.arange(0, 64, 2, dtype=np.float32) / 64)).astype(np.float32)
    ar = row[:, None] * inv[None]
    ac = col[:, None] * inv[None]
    rope = np.stack([np.cos(ar), np.sin(ar), np.cos(ac), np.sin(ac)], axis=1).astype(np.float32)
    segmask = np.ones((64, Lmax), np.float32)
    segmask[:, ::128] = 0.0
    sel = np.zeros((64, 16, 4), np.float32)
    for d in range(2):
        for g in range(8):
            for hh in range(4):
                sel[d * 32 + g * 4 + hh, d * 8 + g, hh] = 1.0
    s_ = np.arange(128)[:, None]
    q_ = np.arange(128)[None, :]
    nf = np.where(q_ < s_, -30000.0, 0.0).astype(np.float32)
    nb_ = np.where(q_ > s_, -30000.0, 0.0).astype(np.float32)
    negm = np.stack([np.tile(nf, (1, 4)), np.tile(nb_, (1, 4))], axis=1).astype(np.float32)
    blk = np.zeros((128, 128), np.float32)
    blk[:64, :64] = 1.0
    blk[64:, 64:] = 1.0
    mask64 = np.ones((128, 1024), np.float32)
    mask64[:, ::64] = 0.0
    s2 = np.arange(64)[:, None]
    t2 = np.arange(64)[None, :]
    rmask = np.zeros((128, 2, 128), np.float32)
    for rk in range(2):
        rmask[rk * 64:(rk + 1) * 64, 0, 0:64] = (s2 < t2)
        rmask[rk * 64:(rk + 1) * 64, 0, 64:128] = (s2 <= t2)
        rmask[rk * 64:(rk + 1) * 64, 1, 0:64] = (s2 > t2)
        rmask[rk * 64:(rk + 1) * 64, 1, 64:128] = (s2 >= t2)
    rmaskn = np.zeros((64, 2, 64), np.float32)
    rmaskn[:, 0, :] = (t2 < s2)
    rmaskn[:, 1, :] = (t2 > s2)
    return {"ident_f": np.eye(128, dtype=np.float32), "rope_t": rope, "segmask": segmask, "sel_c": sel, "negm_c": negm,
            "blk_c": blk, "mask64_c": mask64, "rmask_c": rmask, "rmaskn_c": rmaskn}


def rwkv_host_layout(inp):
    nb = inp['r_mu'].shape[0]
    def t8(a):
        return np.ascontiguousarray(a.reshape(nb, 8, 128).transpose(0, 2, 1)).astype(np.float32)
    def t82(a):
        return np.ascontiguousarray(a.reshape(nb, 2, 8, 128).transpose(0, 3, 1, 2)).astype(np.float32)
    return {
        'r_w_in': inp['r_w_in'], 'r_w_out': inp['r_w_out'],
        'r_mu_t': np.ascontiguousarray(inp['r_mu'].reshape(nb, 34, 128).transpose(0, 2, 1)).astype(np.float32),
        'r_wup_t': np.ascontiguousarray(inp['r_w_up'].reshape(nb, 128, 1024)), 'r_aup_t': np.ascontiguousarray(inp['r_a_up'].reshape(nb, 128, 1024)),
        'r_w0_t': t82(inp['r_w0']), 'r_a0_t': t82(inp['r_a0']),
        'r_kk_t': t8(inp['r_k_k']), 'r_ka_t': t8(inp['r_k_a']), 'r_rk_t': t8(inp['r_r_k'].reshape(nb, 1024)),
        'r_lnw_t': t8(inp['r_ln_w']), 'r_lnb_t': t8(inp['r_ln_b']),
    }


def mamba_host_layout(inp):
    na = inp['m_conv_w'].shape[0]
    cw = np.ascontiguousarray(inp['m_conv_w'].reshape(na, 5, 32, 128).transpose(0, 3, 2, 1)).astype(np.float32)
    cb = np.ascontiguousarray(inp['m_conv_b'].reshape(na, 32, 128).transpose(0, 2, 1)).astype(np.float32)
    return {
        'm_w_in': inp['m_w_in'], 'm_w_out': inp['m_w_out'], 'm_convw_t': cw, 'm_convb_t': cb,
        'm_alog_c': np.ascontiguousarray(inp['m_a_log'].reshape(na, 64, 1)), 'm_dtb_c': np.ascontiguousarray(inp['m_dt_bias'].reshape(na, 64, 1)),
        'm_d_rep': np.ascontiguousarray(np.repeat(inp['m_d'], 64, axis=1)), 'm_norm_w': inp['m_norm_w'],
    }


SEQS = [4096, 4096, 2048]
LAYERS = [('m', 0), ('r', 0), ('a', 0), ('m', 1)]


def run(seq_lens, layers, xs, weights, n_cores=8, dbg=False):
    nc = build(seq_lens, layers, dbg)
    hc = host_consts(max(seq_lens))
    in_maps = []
    for c in range(n_cores):
        m = {"xin": np.ascontiguousarray(xs[c], dtype=np.float32)}
        m.update(hc)
        if not any(k == 'm' for k, _ in layers):
            for kk in ("segmask", "sel_c", "negm_c"):
                m.pop(kk, None)
        if not any(k == 'r' for k, _ in layers):
            for kk in ("blk_c", "mask64_c", "rmask_c", "rmaskn_c"):
                m.pop(kk, None)
        m.update(weights)
        in_maps.append(m)
    res = run_bass_kernel_spmd(nc, in_maps, core_ids=list(range(n_cores)))
    if dbg:
        return res.results
    return [r["yout"] for r in res.results]


def kernel(**inputs):
    inp = {k: np.asarray(v) for k, v in inputs.items()}
    layers = LAYERS
    xp, xs = inp['x_prompt'], inp['x_sample']
    xs_core = [np.concatenate([xp[2 * c], xp[2 * c + 1], xs[c]], axis=0) for c in range(8)]
    w = {'ln_g': inp['ln_g'], 'ln_b': inp['ln_b']}
    w.update(mamba_host_layout(inp))
    for k in ('a_w_in', 'a_q_norm', 'a_k_norm', 'a_w_out'):
        w[k] = inp[k]
    w.update(rwkv_host_layout(inp))
    ys = run(SEQS, layers, xs_core, w)
    y_prompt = np.stack([ys[c // 2][(c % 2) * 4096:(c % 2 + 1) * 4096] for c in range(16)], axis=0)
    y_sample = np.stack([ys[c][8192:10240] for c in range(8)], axis=0)
    return (y_prompt.astype(np.float32), y_sample.astype(np.float32))
```

```python
import math
from contextlib import ExitStack
import numpy as np
import ml_dtypes
import concourse.bass as bass
import concourse.mybir as mybir
from concourse.bass_utils import run_bass_kernel_spmd

F32 = mybir.dt.float32
F32R = mybir.dt.float32r
BF16 = mybir.dt.bfloat16
AF = mybir.ActivationFunctionType
ALU = mybir.AluOpType
AX = mybir.AxisListType

D = 1024
DEPTH = 4
LN_EPS = 1e-5
ALPHA = (2.0 * DEPTH) ** 0.25
A_HD = 128
A_H = 8
A_KV = 2
A_IN = 2560
QK_EPS = 1e-6
M_DI = 2048
M_H = 32
M_P = 64
M_N = 128
M_G = 8
M_CONVCH = 4096
M_IN = 6208
R_IN = 4352
R_H = 16
import os
RW_STAGE = int(os.environ.get('RW_STAGE', '9'))
RW_SUB = int(os.environ.get('RW_SUB', '9'))
RW_X = int(os.environ.get('RW_X', '0'))


class Prog:
    def __init__(self, nc):
        self.nc = nc
        self.engs = {'pe': nc.tensor, 'act': nc.scalar, 'dve': nc.vector, 'pool': nc.gpsimd, 'sp': nc.sync}
        self.sem = {k: nc.alloc_semaphore(name='s_' + k) for k in self.engs}
        self.cnt = {k: 0 for k in self.engs}
        self.waited = {k: {} for k in self.engs}
        self.NR = 6
        self.ring = {q: [nc.alloc_semaphore(name='d_%s%d' % (q, i)) for i in range(self.NR)] for q in ('sp', 'pool', 'act')}
        self.ring_cnt = {q: [0] * self.NR for q in self.ring}
        self.ring_i = {q: 0 for q in self.ring}
        self.lastw = {}
        self.readers = {}
        self.n_ins = 0

    def _wait(self, eng, tok):
        sem, sid, val, _ = tok
        w = self.waited[eng]
        if w.get(sid, 0) >= val:
            return
        w[sid] = val
        self.engs[eng].wait_ge(sem, val)

    def _deps(self, eng, reads, writes):
        toks = []
        for k in reads:
            t = self.lastw.get(k)
            if t is not None:
                toks.append(t)
        for k in writes:
            t = self.lastw.get(k)
            if t is not None and t[3] != eng:
                toks.append(t)
            for t in self.readers.get(k, ()):
                if t[3] != eng:
                    toks.append(t)
        for t in toks:
            self._wait(eng, t)

    def _update(self, tok, reads, writes):
        for k in writes:
            self.lastw[k] = tok
            self.readers[k] = []
        for k in reads:
            if k in writes:
                continue
            lst = self.readers.setdefault(k, [])
            if tok[3] != 'dma':
                lst[:] = [t for t in lst if t[3] != tok[3]]
            lst.append(tok)

    def op(self, eng, fn, reads=(), writes=()):
        self._deps(eng, reads, writes)
        ins = fn(self.engs[eng])
        self.cnt[eng] += 1
        ins.then_inc(self.sem[eng], 1)
        tok = (self.sem[eng], 'e_' + eng, self.cnt[eng], eng)
        self._update(tok, reads, writes)
        self.n_ins += 1
        return tok

    def dma(self, q, out, in_, reads=(), writes=()):
        self._deps(q, reads, writes)
        j = self.ring_i[q] % self.NR
        self.ring_i[q] += 1
        sem = self.ring[q][j]
        sid = 'r_%s%d' % (q, j)
        prev = 16 * self.ring_cnt[q][j]
        if prev > 0:
            self._wait(q, (sem, sid, prev, 'dma'))
        ins = self.engs[q].dma_start(out=out, in_=in_)
        ins.then_inc(sem, 16)
        self.ring_cnt[q][j] += 1
        tok = (sem, sid, 16 * self.ring_cnt[q][j], 'dma')
        self._update(tok, reads, writes)
        self.n_ins += 1
        return tok

    def barrier(self):
        for e in self.engs:
            for f in self.engs:
                if f != e and self.cnt[f] > 0:
                    self._wait(e, (self.sem[f], 'e_' + f, self.cnt[f], f))
            for q in self.ring:
                for j in range(self.NR):
                    if self.ring_cnt[q][j] > 0:
                        self._wait(e, (self.ring[q][j], 'r_%s%d' % (q, j), 16 * self.ring_cnt[q][j], 'dma'))
        self.lastw.clear()
        self.readers.clear()


class Ctx:
    pass


def build(seq_lens, layers, dbg=False):
    T = sum(seq_lens)
    Lmax = max(seq_lens)
    nc = bass.Bass("TRN2", target_bir_lowering=False)
    P = Prog(nc)
    c = Ctx()
    c.nc, c.P, c.T, c.Lmax = nc, P, T, Lmax

    def din(name, shape, dt=F32):
        return nc.dram_tensor(name, list(shape), dt, kind="ExternalInput").ap()

    def dscr(name, shape, dt):
        return nc.dram_tensor(name, list(shape), dt, kind=("ExternalOutput" if dbg else "Internal")).ap()

    n_m = sum(1 for k, _ in layers if k == 'm')
    n_r = sum(1 for k, _ in layers if k == 'r')
    n_a = sum(1 for k, _ in layers if k == 'a')
    nl = len(layers)
    xin = din("xin", [T, D])
    yout = nc.dram_tensor("yout", [T, D], F32, kind="ExternalOutput").ap()
    ln_g = din("ln_g", [nl, D])
    ln_b = din("ln_b", [nl, D])
    ident_f = din("ident_f", [128, 128])
    rope_t = din("rope_t", [Lmax, 4, 32])
    W = {}
    if n_a:
        W['a_w_in'] = din("a_w_in", [n_a, D, A_IN])
        W['a_q_norm'] = din("a_q_norm", [n_a, 128])
        W['a_k_norm'] = din("a_k_norm", [n_a, 128])
        W['a_w_out'] = din("a_w_out", [n_a, D, D])
    if n_m:
        W['m_w_in'] = din("m_w_in", [n_m, D, M_IN])
        W['m_w_out'] = din("m_w_out", [n_m, M_DI, D])
        W['m_convw_t'] = din("m_convw_t", [n_m, 128, 32, 5])
        W['m_convb_t'] = din("m_convb_t", [n_m, 128, 32])
        W['m_alog_c'] = din("m_alog_c", [n_m, 64, 1])
        W['m_dtb_c'] = din("m_dtb_c", [n_m, 64, 1])
        W['m_d_rep'] = din("m_d_rep", [n_m, 2048])
        W['m_norm_w'] = din("m_norm_w", [n_m, 2048])
        segmask = din("segmask", [64, Lmax])
        sel_c = din("sel_c", [64, 16, 4])
        negm_c = din("negm_c", [128, 2, 512])
    if n_r:
        W['r_w_in'] = din("r_w_in", [n_r, D, R_IN])
        W['r_w_out'] = din("r_w_out", [n_r, D, D])
        W['r_mu_t'] = din("r_mu_t", [n_r, 128, 34])
        W['r_wup_t'] = din("r_wup_t", [n_r, 128, 1024])
        W['r_aup_t'] = din("r_aup_t", [n_r, 128, 1024])
        W['r_w0_t'] = din("r_w0_t", [n_r, 128, 2, 8])
        W['r_a0_t'] = din("r_a0_t", [n_r, 128, 2, 8])
        for nm in ('r_kk_t', 'r_ka_t', 'r_rk_t', 'r_lnw_t', 'r_lnb_t'):
            W[nm] = din(nm, [n_r, 128, 8])
        blk_c = din("blk_c", [128, 128])
        mask64_c = din("mask64_c", [128, 1024])
        rmask_c = din("rmask_c", [128, 2, 128])
        rmaskn_c = din("rmaskn_c", [64, 2, 64])
    c.W = W

    xres = [dscr("xres%d" % i, [Lmax, D], F32) for i in range(2)]
    xT_d = dscr("xT_d", [128, 8, Lmax], BF16)
    yT_d = dscr("yT_d", [128, 16, Lmax], BF16)
    wb = {}
    if n_a:
        wb['a_w_in'] = dscr("wb_a_w_in", [n_a, 128, 8, A_IN], BF16)
        wb['a_w_out'] = dscr("wb_a_w_out", [n_a, 128, 8, D], BF16)
        qT_d = dscr("qT_d", [128, A_H, Lmax], BF16)
        kT_d = dscr("kT_d", [128, A_KV, Lmax], BF16)
        v_d = dscr("v_d", [Lmax, A_KV * 128], BF16)
        gT_d = dscr("gT_d", [128, 8, Lmax], BF16)

    if n_m:
        wb['m_w_in'] = dscr("wb_m_w_in", [n_m, 128, 8, M_IN], BF16)
        wb['m_w_out'] = dscr("wb_m_w_out", [n_m, 128, 16, D], BF16)
        xbcT_d = dscr("xbcT_d", [128, 32, Lmax], BF16)
        xtm_d = dscr("xtm_d", [Lmax, 2048], BF16)
        btm_d = dscr("btm_d", [Lmax, 1024], BF16)
        z_d = dscr("z_d", [Lmax, 2048], F32)
        cs_d = dscr("cs_d", [128, Lmax], F32)
        nbw_d = dscr("nbw_d", [Lmax, 128], F32)
        ecs_d = dscr("ecs_d", [Lmax, 64], F32)
        dec_d = dscr("dec_d", [128, Lmax // 128, 64], F32)
        hb_d = dscr("hb_d", [Lmax // 128, 128, 2048], BF16)

    if n_r:
        wb['r_w_in'] = dscr("wb_r_w_in", [n_r, 128, 8, R_IN], BF16)
        wb['r_w_out'] = dscr("wb_r_w_out", [n_r, 128, 8, D], BF16)
        u_d = dscr("u_d", [128, 34, Lmax], F32)
        rop_d = dscr("rop_d", [64, 16, 2, Lmax // 64, 4, 64], F32)
        gb_d = dscr("gb_d", [128, 8, 2, Lmax], F32)
        wc_d = dscr("wc_d", [64, 16, 2, Lmax // 64], F32)
        ybt_d = dscr("ybt_d", [Lmax, 1024], F32)

    ps = [nc.alloc_psum_tensor("ps%d" % i, [128, 512], F32) for i in range(8)]
    psk = ["ps%d" % i for i in range(8)]
    identf = nc.alloc_sbuf_tensor("identf", [128, 128], F32)
    identb = nc.alloc_sbuf_tensor("identb", [128, 128], BF16)
    onesb = nc.alloc_sbuf_tensor("onesb", [128, 128], BF16)
    onesf = nc.alloc_sbuf_tensor("onesf", [128, 128], F32)
    P.dma('sp', identf[:], ident_f, writes=['identf'])
    P.op('dve', lambda e: e.tensor_copy(out=identb[:], in_=identf[:]), reads=['identf'], writes=['identb'])
    P.op('dve', lambda e: e.memset(onesb[:], 1.0), writes=['onesb'])
    P.op('dve', lambda e: e.memset(onesf[:], 1.0), writes=['onesf'])

    rr = {'ev': 0, 'uid': 0}

    def SB(name, shape, dt):
        rr['uid'] += 1
        return nc.sbuf_tensor("%s_%d" % (name, rr['uid']), shape, dt)

    def evac_eng():
        rr['ev'] += 1
        return 'act' if rr['ev'] % 2 else 'dve'

    def copy(eng, out, in_, reads, writes):
        if eng == 'act':
            if out.dtype == F32R:
                return P.op('act', lambda e: e.activation(out=out, in_=in_, func=AF.Identity), reads, writes)
            return P.op('act', lambda e: e.copy(out=out, in_=in_), reads, writes)
        return P.op(eng, lambda e: e.tensor_copy(out=out, in_=in_), reads, writes)

    def cast_weight(src, dst, din_, F):
        with SB("cw_f", [128, 2, F], F32) as wf, SB("cw_b", [128, 2, F], BF16) as wbf:
            for dc in range(din_ // 128):
                b = dc % 2
                P.dma('sp', wf[:, b, :], src[dc * 128:(dc + 1) * 128, :], writes=[('cwf', b)])
                eng = ('dve', 'act', 'pool')[dc % 3]
                copy(eng, wbf[:, b, :], wf[:, b, :], [('cwf', b)], [('cwb', b)])
                P.dma('pool', dst[:, dc, :], wbf[:, b, :], reads=[('cwb', b)], writes=[('wbd', id(dst))])
            P.barrier()

    ai = 0
    for k, j in layers:
        if k == 'm':
            cast_weight(W['m_w_in'][j], wb['m_w_in'][j], D, M_IN)
            cast_weight(W['m_w_out'][j], wb['m_w_out'][j], M_DI, D)
        if k == 'r':
            cast_weight(W['r_w_in'][j], wb['r_w_in'][j], D, R_IN)
            cast_weight(W['r_w_out'][j], wb['r_w_out'][j], D, D)
        if k == 'a':
            cast_weight(W['a_w_in'][j], wb['a_w_in'][j], D, A_IN)
            cast_weight(W['a_w_out'][j], wb['a_w_out'][j], D, D)

    def transpose_to_xT(xt_tile_ap, xkey, tok0, tb):
        pass

    def phase_prep(t0, L):
        with SB("pp_x", [128, 2, D], F32) as xt, SB("pp_t", [128, 2, 8, 128], BF16) as tt:
            for it in range(L // 128):
                b = it % 2
                P.dma('sp', xt[:, b, :], xin[t0 + it * 128:t0 + (it + 1) * 128, :], writes=[('ppx', b)])
                for hf in range(2):
                    pk = psk[(it * 2 + hf) % 8]
                    pt = ps[(it * 2 + hf) % 8]
                    for i in range(4):
                        dc = hf * 4 + i
                        P.op('pe', lambda e, dc=dc, i=i, pt=pt: e.transpose(out=pt[:, i * 128:(i + 1) * 128], in_=xt[:, b, dc * 128:(dc + 1) * 128], identity=identf[:]),
                             reads=[('ppx', b), 'identf'], writes=[pk])
                    copy(evac_eng(), tt[:, b, hf * 4:(hf + 1) * 4, :], pt[:, :].rearrange("p (c t) -> p c t", c=4), [pk], [('ppt', b, hf)])
                P.dma('pool', xT_d[:, :, it * 128:(it + 1) * 128], tt[:, b, :, :], reads=[('ppt', b, 0), ('ppt', b, 1)], writes=['xT_d'])
            P.barrier()

    def phase_out(li, t0, L, cin_chunks, wout_b, x_src, x_dst, last):
        CC = cin_chunks
        with SB("po_w", [128, CC, D], BF16) as wo, \
                SB("po_g", [128, D], F32) as gbc, SB("po_b", [128, D], F32) as bbc, \
                SB("po_y", [128, 2, CC, 128], BF16) as yt, \
                SB("po_x", [128, 2, D], F32) as xr, \
                SB("po_z", [128, 2, D], F32) as zt, \
                SB("po_st", [128, 2, 16], F32) as st, \
                SB("po_t", [128, 2, 8, 128], BF16) as tt:
            P.dma('sp', wo[:], wout_b, writes=['po_w'])
            P.dma('sp', gbc[:], ln_g[li].unsqueeze(0).broadcast_to([128, D]), writes=['po_g'])
            P.dma('sp', bbc[:], ln_b[li].unsqueeze(0).broadcast_to([128, D]), writes=['po_b'])
            def partA(it):
                b = it % 2
                P.dma('sp', yt[:, b, :, :], yT_d[:, 0:CC, it * 128:(it + 1) * 128], reads=['yT_d'], writes=[('poy', b)])
                P.dma('sp', xr[:, b, :], x_src[it * 128:(it + 1) * 128, :], reads=['xres_src'], writes=[('pox', b)])
                pb = [(it * 4 + n) % 8 for n in range(4)]
                for nb in range(2):
                    for cc in range(CC):
                        P.op('pe', lambda e, nb=nb, cc=cc: e.matmul(ps[pb[nb]][:, :], yt[:, b, cc, :], wo[:, cc, nb * 512:(nb + 1) * 512], start=(cc == 0), stop=(cc == CC - 1)),
                             reads=[('poy', b), 'po_w'], writes=[psk[pb[nb]]])
                for nb in range(2):
                    P.op('dve', lambda e, nb=nb: e.scalar_tensor_tensor(out=zt[:, b, nb * 512:(nb + 1) * 512], in0=xr[:, b, nb * 512:(nb + 1) * 512], scalar=ALPHA,
                                                                          in1=ps[pb[nb]][:, :], op0=ALU.mult, op1=ALU.add),
                         reads=[('pox', b), psk[pb[nb]]], writes=[('poz', b, nb)])
                    P.op('dve', lambda e, nb=nb: e.bn_stats(out=st[:, b, nb * 6:(nb + 1) * 6], in_=zt[:, b, nb * 512:(nb + 1) * 512]),
                         reads=[('poz', b, nb)], writes=[('post', b, nb)])
                P.op('dve', lambda e: e.bn_aggr(out=st[:, b, 12:14], in_=st[:, b, 0:12]), reads=[('post', b, 0), ('post', b, 1)], writes=[('pomv', b)])
                P.op('dve', lambda e: e.tensor_scalar(out=st[:, b, 14:15], in0=st[:, b, 13:14], scalar1=LN_EPS, scalar2=None, op0=ALU.add), reads=[('pomv', b)], writes=[('pors', b)])
                P.op('act', lambda e: e.activation(out=st[:, b, 14:15], in_=st[:, b, 14:15], func=AF.Sqrt), reads=[('pors', b)], writes=[('pors', b)])

            def partB(it):
                b = it % 2
                pb = [(it * 4 + n) % 8 for n in range(4)]
                P.op('dve', lambda e: e.reciprocal(out=st[:, b, 15:16], in_=st[:, b, 14:15]), reads=[('pors', b)], writes=[('pors2', b)])
                P.op('dve', lambda e: e.tensor_scalar(out=zt[:, b, :], in0=zt[:, b, :], scalar1=st[:, b, 12:13], scalar2=st[:, b, 15:16], op0=ALU.subtract, op1=ALU.mult),
                     reads=[('poz', b, 0), ('poz', b, 1), ('pomv', b), ('pors2', b)], writes=[('poz', b, 0), ('poz', b, 1)])
                P.op('pool', lambda e: e.tensor_tensor(out=zt[:, b, :], in0=zt[:, b, :], in1=gbc[:], op=ALU.mult), reads=[('poz', b, 0), ('poz', b, 1), 'po_g'], writes=[('poz', b, 0), ('poz', b, 1)])
                P.op('pool', lambda e: e.tensor_tensor(out=zt[:, b, :], in0=zt[:, b, :], in1=bbc[:], op=ALU.add), reads=[('poz', b, 0), ('poz', b, 1), 'po_b'], writes=[('poz', b, 0), ('poz', b, 1)])
                P.dma('pool', x_dst[it * 128:(it + 1) * 128, :], zt[:, b, :], reads=[('poz', b, 0), ('poz', b, 1)], writes=['xres_dst'])
                if not last:
                    for hf in range(2):
                        pi = pb[2 + hf]
                        for i in range(4):
                            dc = hf * 4 + i
                            P.op('pe', lambda e, dc=dc, i=i, pi=pi: e.transpose(out=ps[pi][:, i * 128:(i + 1) * 128], in_=zt[:, b, dc * 128:(dc + 1) * 128], identity=identf[:]),
                                 reads=[('poz', b, 0), ('poz', b, 1), 'identf'], writes=[psk[pi]])
                        copy(evac_eng(), tt[:, b, hf * 4:(hf + 1) * 4, :], ps[pi][:, :].rearrange("p (c t) -> p c t", c=4), [psk[pi]], [('pot', b, hf)])
                    P.dma('pool', xT_d[:, :, it * 128:(it + 1) * 128], tt[:, b, :, :], reads=[('pot', b, 0), ('pot', b, 1)], writes=['xT_d'])
            NTo = L // 128
            partA(0)
            for it in range(NTo):
                if it + 1 < NTo:
                    partA(it + 1)
                partB(it)
            P.barrier()

    def phase_attn(j, L):
        NT = L // 128
        scale = A_HD ** -0.5
        win = wb['a_w_in'][j]
        with SB("a1_xT", [128, 8, L], BF16) as xT, \
                SB("a1_w", [128, 8, 1536], BF16) as wq, \
                SB("a1_wg", [128, 8, 1024], BF16) as wg, \
                SB("a1_gq", [128, 2, 128], F32) as gqk, \
                SB("a1_rope", [128, 2, 4, 32], F32) as rp, \
                SB("a1_q", [128, 2, 1280], F32) as qf, \
                SB("a1_sq", [128, 1280], F32) as sq, \
                SB("a1_ss", [128, 2, 16], F32) as ss, \
                SB("a1_qr", [128, 2, 1280], BF16) as qr, \
                SB("a1_v", [128, 2, 256], BF16) as vb, \
                SB("a1_qT", [128, 2, 10, 128], BF16) as qTt, \
                SB("a1_g", [128, 2, 512], BF16) as gt:
            P.dma('sp', xT[:], xT_d[:, :, 0:L], reads=['xT_d'], writes=['a1_xT'])
            P.dma('sp', wq[:], win[:, :, 0:1536], writes=['a1_w'])
            P.dma('sp', wg[:], win[:, :, 1536:2560], writes=['a1_wg'])
            P.dma('sp', gqk[:, 0, :], W['a_q_norm'][j].unsqueeze(0).broadcast_to([128, 128]), writes=['a1_gq'])
            P.dma('sp', gqk[:, 1, :], W['a_k_norm'][j].unsqueeze(0).broadcast_to([128, 128]), writes=['a1_gq'])
            for it in range(NT):
                b = it % 2
                P.dma('sp', rp[:, b, :, :], rope_t[it * 128:(it + 1) * 128, :, :], writes=[('a1rp', b)])
                pb = [(it * 3 + n) % 6 for n in range(3)]
                for n in range(3):
                    for dc in range(8):
                        P.op('pe', lambda e, n=n, dc=dc: e.matmul(ps[pb[n]][:, :], xT[:, dc, it * 128:(it + 1) * 128], wq[:, dc, n * 512:(n + 1) * 512], start=(dc == 0), stop=(dc == 7)),
                             reads=['a1_xT', 'a1_w'], writes=[psk[pb[n]]])
                P.op('act', lambda e: e.copy(out=qf[:, b, 0:512], in_=ps[pb[0]][:, :]), reads=[psk[pb[0]]], writes=[('a1q', b)])
                P.op('act', lambda e: e.copy(out=qf[:, b, 512:1024], in_=ps[pb[1]][:, :]), reads=[psk[pb[1]]], writes=[('a1q', b)])
                P.op('act', lambda e: e.copy(out=qf[:, b, 1024:1280], in_=ps[pb[2]][:, 0:256]), reads=[psk[pb[2]]], writes=[('a1q', b)])
                P.op('act', lambda e: e.copy(out=vb[:, b, :], in_=ps[pb[2]][:, 256:512]), reads=[psk[pb[2]]], writes=[('a1v', b)])
                P.dma('pool', v_d[it * 128:(it + 1) * 128, :], vb[:, b, :], reads=[('a1v', b)], writes=['v_d'])
                P.op('dve', lambda e: e.tensor_tensor(out=sq[:], in0=qf[:, b, :], in1=qf[:, b, :], op=ALU.mult), reads=[('a1q', b)], writes=['a1sq'])
                P.op('dve', lambda e: e.tensor_reduce(out=ss[:, b, 0:10], in_=sq[:].rearrange("p (h d) -> p h d", h=10), op=ALU.add, axis=AX.X), reads=['a1sq'], writes=[('a1ss', b)])
                P.op('dve', lambda e: e.tensor_scalar(out=ss[:, b, 0:10], in0=ss[:, b, 0:10], scalar1=1.0 / 128, scalar2=QK_EPS, op0=ALU.mult, op1=ALU.add), reads=[('a1ss', b)], writes=[('a1ss', b)])
                P.op('act', lambda e: e.activation(out=ss[:, b, 0:10], in_=ss[:, b, 0:10], func=AF.Sqrt), reads=[('a1ss', b)], writes=[('a1ss', b)])
                P.op('dve', lambda e: e.reciprocal(out=ss[:, b, 0:10], in_=ss[:, b, 0:10]), reads=[('a1ss', b)], writes=[('a1ss', b)])
                q3 = qf[:, b, :].rearrange("p (h d) -> p h d", h=10)
                P.op('dve', lambda e: e.tensor_tensor(out=q3, in0=q3, in1=ss[:, b, 0:10].unsqueeze(2).broadcast_to([128, 10, 128]), op=ALU.mult), reads=[('a1q', b), ('a1ss', b)], writes=[('a1q', b)])
                P.op('pool', lambda e: e.tensor_tensor(out=q3[:, 0:8, :], in0=q3[:, 0:8, :], in1=gqk[:, 0:1, :].broadcast_to([128, 8, 128]), op=ALU.mult), reads=[('a1q', b), 'a1_gq'], writes=[('a1q', b)])
                P.op('pool', lambda e: e.tensor_tensor(out=q3[:, 8:10, :], in0=q3[:, 8:10, :], in1=gqk[:, 1:2, :].broadcast_to([128, 2, 128]), op=ALU.mult), reads=[('a1q', b), 'a1_gq'], writes=[('a1q', b)])
                q5 = qf[:, b, :].rearrange("p (h a f) -> p h a f", h=10, a=2)
                o5 = qr[:, b, :].rearrange("p (h a f) -> p h a f", h=10, a=2)
                s5 = sq[:].rearrange("p (h a f) -> p h a f", h=10, a=2)
                cosb = rp[:, b, 0:4:2, :]
                sinb = rp[:, b, 1:4:2, :]
                cb4 = cosb.unsqueeze(1).broadcast_to([128, 10, 2, 32])
                sb4 = sinb.unsqueeze(1).broadcast_to([128, 10, 2, 32])
                P.op('dve', lambda e: e.tensor_tensor(out=s5[:, :, :, 0:32], in0=q5[:, :, :, 0:32], in1=cb4, op=ALU.mult),
                     reads=[('a1q', b), ('a1rp', b)], writes=['a1sq'])
                P.op('pool', lambda e: e.tensor_tensor(out=s5[:, :, :, 32:64], in0=q5[:, :, :, 32:64], in1=cb4, op=ALU.mult),
                     reads=[('a1q', b), ('a1rp', b)], writes=['a1sq'])
                P.op('dve', lambda e: e.tensor_tensor(out=q5[:, :, :, 0:32], in0=q5[:, :, :, 0:32], in1=sb4, op=ALU.mult),
                     reads=[('a1q', b), ('a1rp', b), 'a1sq'], writes=[('a1q', b)])
                P.op('pool', lambda e: e.tensor_tensor(out=q5[:, :, :, 32:64], in0=q5[:, :, :, 32:64], in1=sb4, op=ALU.mult),
                     reads=[('a1q', b), ('a1rp', b), 'a1sq'], writes=[('a1q', b)])
                P.op('dve', lambda e: e.tensor_tensor(out=o5[:, :, :, 0:32], in0=s5[:, :, :, 0:32], in1=q5[:, :, :, 32:64], op=ALU.subtract),
                     reads=[('a1q', b), 'a1sq'], writes=[('a1qr', b)])
                P.op('dve', lambda e: e.tensor_tensor(out=o5[:, :, :, 32:64], in0=s5[:, :, :, 32:64], in1=q5[:, :, :, 0:32], op=ALU.add),
                     reads=[('a1q', b), 'a1sq'], writes=[('a1qr', b)])
                for h in range(10):
                    pi = 6 + (h // 8)
                    pv = ps[pi][:, :].bitcast(BF16)
                    P.op('pe', lambda e, h=h, pv=pv: e.transpose(out=pv[:, (h % 8) * 128:(h % 8 + 1) * 128], in_=qr[:, b, h * 128:(h + 1) * 128], identity=identb[:]),
                         reads=[('a1qr', b), 'identb'], writes=[psk[pi]])
                P.op('act', lambda e: e.copy(out=qTt[:, b, 0:8, :], in_=ps[6][:, :].bitcast(BF16).rearrange("p (h t) -> p h t", h=8)), reads=[psk[6]], writes=[('a1qT', b)])
                P.op('dve', lambda e: e.tensor_copy(out=qTt[:, b, 8:10, :], in_=ps[7][:, :].bitcast(BF16)[:, 0:256].rearrange("p (h t) -> p h t", h=2)), reads=[psk[7]], writes=[('a1kT', b)])
                P.dma('pool', qT_d[:, :, it * 128:(it + 1) * 128], qTt[:, b, 0:8, :], reads=[('a1qT', b)], writes=['qT_d'])
                P.dma('pool', kT_d[:, :, it * 128:(it + 1) * 128], qTt[:, b, 8:10, :], reads=[('a1kT', b)], writes=['kT_d'])
            k = 0
            for fc in range(8):
                for tbk in range(L // 512):
                    b = k % 2
                    pi = k % 6
                    k += 1
                    for dc in range(8):
                        P.op('pe', lambda e, dc=dc, pi=pi: e.matmul(ps[pi][:, :], wg[:, dc, fc * 128:(fc + 1) * 128], xT[:, dc, tbk * 512:(tbk + 1) * 512], start=(dc == 0), stop=(dc == 7)),
                             reads=['a1_xT', 'a1_wg'], writes=[psk[pi]])
                    P.op('act', lambda e, pi=pi: e.activation(out=gt[:, b, :], in_=ps[pi][:, :], func=AF.Silu), reads=[psk[pi]], writes=[('a1g', b)])
                    P.dma('pool', gT_d[:, fc, tbk * 512:(tbk + 1) * 512], gt[:, b, :], reads=[('a1g', b)], writes=['gT_d'])
            P.barrier()
        with SB("a2_kT", [128, A_KV, L], BF16) as kT, \
                SB("a2_v", [128, NT, 256], BF16) as V, \
                SB("a2_q", [128, 2, 512], BF16) as qb, \
                SB("a2_g", [128, 2, 512], BF16) as gb, \
                SB("a2_p", [128, 3, 512], BF16) as pT, \
                SB("a2_r", [128, 2, 512], F32) as rc, \
                SB("a2_o", [128, 2, 512], BF16) as ob, \
                SB("a2_m", [128, 8], F32) as mm, SB("a2_racc", [128, 2, 2, 512], F32) as racc:
            P.dma('sp', kT[:], kT_d[:, :, 0:L], reads=['kT_d'], writes=['a2_kT'])
            P.dma('sp', V[:], v_d[0:L, :].rearrange("(n p) c -> p n c", p=128), reads=['v_d'], writes=['a2_v'])
            P.dma('sp', rc[:, 0, 0:128], W['a_q_norm'][j].unsqueeze(0).broadcast_to([128, 128]), writes=[('a2r', 0)])
            P.dma('sp', rc[:, 0, 128:256], W['a_k_norm'][j].unsqueeze(0).broadcast_to([128, 128]), writes=[('a2r', 0)])
            P.op('dve', lambda e: e.tensor_reduce(out=mm[:, 0:2], in_=rc[:, 0, 0:256].rearrange("p (a d) -> p a d", a=2), op=ALU.max, axis=AX.X, apply_absolute_value=True),
                 reads=[('a2r', 0)], writes=['a2m'])
            P.op('dve', lambda e: e.tensor_tensor(out=mm[:, 2:3], in0=mm[:, 0:1], in1=mm[:, 1:2], op=ALU.mult), reads=['a2m'], writes=['a2m2'])
            P.op('dve', lambda e: e.tensor_scalar(out=mm[:, 3:4], in0=mm[:, 2:3], scalar1=-math.sqrt(128.0), scalar2=None, op0=ALU.mult), reads=['a2m2'], writes=['a2m3'])
            k = 0
            for h in range(A_H):
                kv = h // (A_H // A_KV)
                for qbk in range(L // 512):
                    b = k % 2
                    k += 1
                    P.dma('sp', qb[:, b, :], qT_d[:, h, qbk * 512:(qbk + 1) * 512], reads=['qT_d'], writes=[('a2q', b)])
                    P.dma('sp', gb[:, b, :], gT_d[:, h, qbk * 512:(qbk + 1) * 512], reads=['gT_d'], writes=[('a2g', b)])
                    po = 4 + 2 * b
                    pr = 5 + 2 * b

                    def qk(st):
                        pi = st % 4
                        P.op('pe', lambda e: e.matmul(ps[pi][:, :], kT[:, kv, st * 128:(st + 1) * 128], qb[:, b, :], start=True, stop=True),
                             reads=['a2_kT', ('a2q', b)], writes=[psk[pi]])
                    qk(0)
                    for st in range(NT):
                        if st + 1 < NT:
                            qk(st + 1)
                        pi = st % 4
                        pb3 = st % 3
                        P.op('act', lambda e: e.activation(out=pT[:, pb3, :], in_=ps[pi][:, :], func=AF.Exp, scale=scale, bias=mm[:, 3:4]),
                             reads=[psk[pi], 'a2m3'], writes=[('a2p', pb3)])
                        P.op('pe', lambda e: e.matmul(ps[po][:, :], V[:, st, kv * 128:(kv + 1) * 128], pT[:, pb3, :], start=(st == 0), stop=(st == NT - 1)),
                             reads=['a2_v', ('a2p', pb3)], writes=[psk[po]])
                        par = st % 2
                        reng = 'dve' if par == 0 else 'pool'
                        if st < 2:
                            P.op(reng, lambda e: e.tensor_copy(out=racc[:, b, par, :], in_=pT[:, pb3, :]), reads=[('a2p', pb3)], writes=[('racc', b, par)])
                        else:
                            P.op(reng, lambda e: e.tensor_tensor(out=racc[:, b, par, :], in0=racc[:, b, par, :], in1=pT[:, pb3, :], op=ALU.add), reads=[('a2p', pb3), ('racc', b, par)], writes=[('racc', b, par)])
                    P.op('pe', lambda e: e.matmul(ps[pr][:, :], onesf[:], racc[:, b, 0, :], start=True, stop=False), reads=['onesf', ('racc', b, 0)], writes=[psk[pr]])
                    P.op('pe', lambda e: e.matmul(ps[pr][:, :], onesf[:], racc[:, b, 1, :], start=False, stop=True), reads=['onesf', ('racc', b, 1)], writes=[psk[pr]])
                    P.op('dve', lambda e: e.reciprocal(out=rc[:, b, :], in_=ps[pr][:, :]), reads=[psk[pr]], writes=[('a2r', b)])
                    P.op('dve', lambda e: e.tensor_tensor(out=rc[:, b, :], in0=ps[po][:, :], in1=rc[:, b, :], op=ALU.mult), reads=[psk[po], ('a2r', b)], writes=[('a2r', b)])
                    P.op('pool', lambda e: e.tensor_tensor(out=ob[:, b, :], in0=rc[:, b, :], in1=gb[:, b, :], op=ALU.mult), reads=[('a2r', b), ('a2g', b)], writes=[('a2o', b)])
                    P.dma('pool', yT_d[:, h, qbk * 512:(qbk + 1) * 512], ob[:, b, :], reads=[('a2o', b)], writes=['yT_d'])
            P.barrier()

    def phase_mamba(j, L):
        NT = L // 128
        NB = L // 512
        win = wb['m_w_in'][j]
        with SB("m1_xT", [128, 8, L], BF16) as xT, SB("m1_w", [128, 2, 8, 128], BF16) as wsl, \
                SB("m1_cw", [128, 32, 5], F32) as cw, SB("m1_cb", [128, 32], F32) as cb, \
                SB("m1_rb", [128, L + 4], F32) as rb, SB("m1_acc", [128, L], F32) as acc, \
                SB("m1_ob", [128, 2, L], BF16) as ob, SB("m1_tt", [128, 2, 8, 128], BF16) as tt:
            P.dma('sp', xT[:], xT_d[:, :, 0:L], writes=['m1_xT'])
            P.dma('sp', cw[:], W['m_convw_t'][j], writes=['m1_cw'])
            P.dma('sp', cb[:], W['m_convb_t'][j], writes=['m1_cw'])
            P.op('dve', lambda e: e.memset(rb[:, 0:2], 0.0), writes=['m1_rb'])
            P.op('dve', lambda e: e.memset(rb[:, L + 2:L + 4], 0.0), writes=['m1_rb'])
            k = 0
            kt = 0
            gsz = min(8, NT)
            for fc in range(32):
                wbuf = fc % 2
                P.dma('sp', wsl[:, wbuf], win[:, :, 2048 + fc * 128:2048 + (fc + 1) * 128], writes=[('m1w', wbuf)])
                for tb in range(NB):
                    pi = k % 4
                    k += 1
                    for dc in range(8):
                        P.op('pe', lambda e: e.matmul(ps[pi][:, :], wsl[:, wbuf, dc, :], xT[:, dc, tb * 512:(tb + 1) * 512], start=(dc == 0), stop=(dc == 7)),
                             reads=['m1_xT', ('m1w', wbuf)], writes=[psk[pi]])
                    P.op('act', lambda e: e.copy(out=rb[:, 2 + tb * 512:2 + (tb + 1) * 512], in_=ps[pi][:, :]), reads=[psk[pi]], writes=['m1_rb'])
                P.op('dve', lambda e: e.tensor_scalar(out=acc[:], in0=rb[:, 0:L], scalar1=cw[:, fc, 0:1], scalar2=cb[:, fc:fc + 1], op0=ALU.mult, op1=ALU.add),
                     reads=['m1_rb', 'm1_cw'], writes=['m1_acc'])
                for kk in range(1, 5):
                    P.op('dve', lambda e: e.scalar_tensor_tensor(out=acc[:], in0=rb[:, kk:kk + L], scalar=cw[:, fc, kk:kk + 1], in1=acc[:], op0=ALU.mult, op1=ALU.add),
                         reads=['m1_rb', 'm1_cw', 'm1_acc'], writes=['m1_acc'])
                obb = fc % 2
                P.op('act', lambda e: e.activation(out=ob[:, obb, :], in_=acc[:], func=AF.Silu), reads=['m1_acc'], writes=[('m1ob', obb)])
                P.dma('pool', xbcT_d[:, fc, 0:L], ob[:, obb, :], reads=[('m1ob', obb)], writes=['xbcT_d'])
                if fc < 24:
                    for t8 in range(NT // gsz):
                        pi = 4 + (kt % 4)
                        tb2 = kt % 2
                        kt += 1
                        pv = ps[pi][:, :].bitcast(BF16)
                        for i in range(gsz):
                            tk = t8 * gsz + i
                            P.op('pe', lambda e: e.transpose(out=pv[:, i * 128:(i + 1) * 128], in_=ob[:, obb, tk * 128:(tk + 1) * 128], identity=identb[:]),
                                 reads=[('m1ob', obb), 'identb'], writes=[psk[pi]])
                        copy(evac_eng(), tt[:, tb2, 0:gsz, :], pv[:, 0:gsz * 128].rearrange("p (n c) -> p n c", n=gsz), [psk[pi]], [('m1tt', tb2)])
                        if fc < 16:
                            dst = xtm_d[t8 * gsz * 128:(t8 + 1) * gsz * 128, fc * 128:(fc + 1) * 128]
                        else:
                            dst = btm_d[t8 * gsz * 128:(t8 + 1) * gsz * 128, (fc - 16) * 128:(fc - 15) * 128]
                        P.dma('pool', dst.rearrange("(n p) c -> p n c", p=128), tt[:, tb2, 0:gsz, :], reads=[('m1tt', tb2)], writes=['xtm_d'])
            P.barrier()
        with SB("m1b_xT", [128, 8, L], BF16) as xT, SB("m1b_wz", [128, 8, 2048], BF16) as wz, SB("m1b_z", [128, 2, 2048], F32) as zt:
            P.dma('sp', xT[:], xT_d[:, :, 0:L], writes=['m1_xT'])
            P.dma('sp', wz[:], win[:, :, 0:2048], writes=['m1_wz'])
            for it in range(NT):
                b = it % 2
                for n in range(4):
                    pi = (it * 4 + n) % 8
                    for dc in range(8):
                        P.op('pe', lambda e: e.matmul(ps[pi][:, :], xT[:, dc, it * 128:(it + 1) * 128], wz[:, dc, n * 512:(n + 1) * 512], start=(dc == 0), stop=(dc == 7)),
                             reads=['m1_xT', 'm1_wz'], writes=[psk[pi]])
                    P.op('act', lambda e: e.activation(out=zt[:, b, n * 512:(n + 1) * 512], in_=ps[pi][:, :], func=AF.Silu), reads=[psk[pi]], writes=[('m1z', b)])
                P.dma('pool', z_d[it * 128:(it + 1) * 128, :], zt[:, b, :], reads=[('m1z', b)], writes=['z_d'])
            P.barrier()
        with SB("m1c_xT", [128, 8, L], BF16) as xT, SB("m1c_wdt", [128, 8, 64], BF16) as wdt, \
                SB("m1c_x", [64, L], F32) as xr, SB("m1c_t1", [64, L], F32) as t1, SB("m1c_t2", [64, L], F32) as t2, \
                SB("m1c_cs", [64, L], F32) as cs, SB("m1c_vv", [64, L], F32) as vv, SB("m1c_msk", [64, L], F32) as msk, \
                SB("m1c_col", [64, 8], F32) as col, SB("m1c_dcol", [64, NT], F32) as dcol, SB("m1c_xd", [64, NT, 64], F32) as xd, \
                SB("m1c_st", [128, 2, 512], F32) as stg:
            P.dma('sp', xT[:], xT_d[:, :, 0:L], writes=['m1_xT'])
            P.dma('sp', wdt[:], win[:, :, 6144:6208], writes=['m1_wdt'])
            P.dma('sp', msk[:], segmask[:, 0:L], writes=['msk'])
            P.dma('sp', col[:, 0:1], W['m_alog_c'][j], writes=['col0'])
            P.dma('sp', col[:, 2:3], W['m_dtb_c'][j], writes=['col2'])
            P.op('act', lambda e: e.activation(out=col[:, 1:2], in_=col[:, 0:1], func=AF.Exp), reads=['col0'], writes=['col1'])
            P.op('dve', lambda e: e.tensor_scalar(out=col[:, 3:4], in0=col[:, 1:2], scalar1=-1.0, scalar2=None, op0=ALU.mult), reads=['col1'], writes=['col3'])
            for tb in range(NB):
                pi = tb % 4
                for dc in range(8):
                    P.op('pe', lambda e: e.matmul(ps[pi][0:64, :], wdt[:, dc, :], xT[:, dc, tb * 512:(tb + 1) * 512], start=(dc == 0), stop=(dc == 7)),
                         reads=['m1_xT', 'm1_wdt'], writes=[psk[pi]])
                P.op('dve', lambda e: e.tensor_scalar(out=xr[:, tb * 512:(tb + 1) * 512], in0=ps[pi][0:64, :], scalar1=col[:, 2:3], scalar2=None, op0=ALU.add),
                     reads=[psk[pi], 'col2'], writes=['xr'])
            P.op('act', lambda e: e.activation(out=t1[:], in_=xr[:], func=AF.Abs), reads=['xr'], writes=['t1'])
            P.op('act', lambda e: e.activation(out=t1[:], in_=t1[:], func=AF.Exp, scale=-1.0), reads=['t1'], writes=['t1'])
            P.op('act', lambda e: e.activation(out=t1[:], in_=t1[:], func=AF.Ln, bias=1.0), reads=['t1'], writes=['t1'])
            P.op('dve', lambda e: e.tensor_scalar(out=xr[:], in0=xr[:], scalar1=0.0, scalar2=None, op0=ALU.max), reads=['xr'], writes=['xr'])
            P.op('dve', lambda e: e.tensor_tensor(out=xr[:], in0=xr[:], in1=t1[:], op=ALU.add), reads=['xr', 't1'], writes=['xr'])
            P.op('dve', lambda e: e.tensor_scalar(out=vv[:], in0=xr[:], scalar1=col[:, 3:4], scalar2=None, op0=ALU.mult), reads=['xr', 'col3'], writes=['vv'])
            P.op('dve', lambda e: e.tensor_tensor_scan(out=cs[:], data0=msk[:], data1=vv[:], initial=0.0, op0=ALU.mult, op1=ALU.add), reads=['msk', 'vv'], writes=['cs'])
            cs3 = cs[:].rearrange("p (c t) -> p c t", t=128)
            t13 = t1[:].rearrange("p (c t) -> p c t", t=128)
            t23 = t2[:].rearrange("p (c t) -> p c t", t=128)
            P.op('dve', lambda e: e.tensor_tensor(out=t13[32:64], in0=cs3[32:64, :, 127:128].broadcast_to([32, NT, 128]), in1=cs3[32:64], op=ALU.subtract), reads=['cs', 't1'], writes=['t1'])
            P.op('dve', lambda e: e.tensor_tensor(out=cs[32:64, :], in0=t1[32:64, :], in1=vv[32:64, :], op=ALU.add), reads=['t1', 'vv', 'cs'], writes=['cs'])
            P.op('dve', lambda e: e.tensor_copy(out=msk[:].bitcast(F32R), in_=cs[:]), reads=['cs', 'msk'], writes=['msk'])
            P.op('dve', lambda e: e.tensor_tensor(out=vv[:], in0=cs[:], in1=msk[:], op=ALU.subtract), reads=['cs', 'msk', 'vv'], writes=['vv'])
            P.dma('pool', cs_d[0:64, 0:L], msk[:], reads=['msk'], writes=['cs_d'])
            P.dma('pool', cs_d[64:128, 0:L], vv[:], reads=['vv'], writes=['cs_d'])
            P.op('act', lambda e: e.activation(out=t1[:], in_=xr[:], func=AF.Ln), reads=['xr', 't1'], writes=['t1'])
            P.op('dve', lambda e: e.tensor_tensor(out=t1[:], in0=t1[:], in1=cs[:], op=ALU.subtract), reads=['t1', 'cs'], writes=['t1'])
            P.op('dve', lambda e: e.tensor_tensor(out=t23[0:32], in0=cs3[0:32, :, 127:128].broadcast_to([32, NT, 128]), in1=cs3[0:32], op=ALU.subtract), reads=['cs'], writes=['t2'])
            P.op('dve', lambda e: e.tensor_tensor(out=t23[32:64], in0=cs3[32:64, :, 0:1].broadcast_to([32, NT, 128]), in1=cs3[32:64], op=ALU.subtract), reads=['cs'], writes=['t2'])
            P.op('act', lambda e: e.activation(out=t2[:], in_=t2[:], func=AF.Exp), reads=['t2'], writes=['t2'])
            P.op('dve', lambda e: e.tensor_tensor(out=t2[:], in0=t2[:], in1=xr[:], op=ALU.mult), reads=['t2', 'xr'], writes=['t2'])
            P.op('act', lambda e: e.activation(out=xr[:], in_=cs[:], func=AF.Exp), reads=['cs', 'xr', 't2'], writes=['xr'])
            c8e = min(8, NT)
            for c8 in range(NT // c8e):
                pi = 4 + c8 % 2
                bq = c8 % 2
                for i in range(c8e):
                    cc = c8 * c8e + i
                    P.op('pe', lambda e: e.transpose(out=ps[pi][:, i * 64:(i + 1) * 64], in_=xr[:, cc * 128:(cc + 1) * 128], identity=identf[0:64, 0:64]), reads=['xr', 'identf'], writes=[psk[pi]])
                copy(evac_eng(), stg[:, bq, 0:c8e * 64], ps[pi][:, 0:c8e * 64], [psk[pi]], [('stg', bq)])
                P.dma('pool', ecs_d[c8 * c8e * 128:(c8 + 1) * c8e * 128, :].rearrange("(n p) c -> p n c", p=128), stg[:, bq, 0:c8e * 64].rearrange("p (n c) -> p n c", c=64),
                      reads=[('stg', bq)], writes=['ecs_d'])
            P.op('act', lambda e: e.activation(out=dcol[0:32, :], in_=cs3[0:32, :, 127], func=AF.Exp), reads=['cs'], writes=['dcol'])
            P.op('act', lambda e: e.activation(out=dcol[32:64, :], in_=cs3[32:64, :, 0], func=AF.Exp), reads=['cs'], writes=['dcol'])
            P.op('dve', lambda e: e.tensor_tensor(out=xd[:], in0=dcol[:, :].unsqueeze(2).broadcast_to([64, NT, 64]), in1=identf[0:64, 0:64].unsqueeze(1).broadcast_to([64, NT, 64]), op=ALU.mult),
                 reads=['dcol', 'identf'], writes=['xd'])
            c8n = min(8, NT)
            for c8 in range(NT // c8n):
                pi = 4 + c8 % 2
                b = c8 % 2
                P.op('pe', lambda e: e.matmul(ps[pi][:, 0:c8n * 64], onesf[0:64, :], xd[:, c8 * c8n:(c8 + 1) * c8n, :], start=True, stop=True), reads=['onesf', 'xd'], writes=[psk[pi]])
                copy(evac_eng(), stg[:, b, 0:c8n * 64], ps[pi][:, 0:c8n * 64], [psk[pi]], [('stg', b)])
                P.dma('pool', dec_d[:, c8 * c8n:(c8 + 1) * c8n, :], stg[:, b, 0:c8n * 64].rearrange("p (c k) -> p c k", k=64), reads=[('stg', b)], writes=['dec_d'])
            c4n = min(4, NT)
            for c4 in range(NT // c4n):
                pi = 6 + c4 % 2
                b = c4 % 2
                for i in range(c4n):
                    cc = c4 * c4n + i
                    P.op('pe', lambda e: e.transpose(out=ps[pi][:, i * 128:i * 128 + 64], in_=t1[:, cc * 128:(cc + 1) * 128], identity=identf[0:64, 0:64]), reads=['t1', 'identf'], writes=[psk[pi]])
                    P.op('pe', lambda e: e.transpose(out=ps[pi][:, i * 128 + 64:(i + 1) * 128], in_=t2[:, cc * 128:(cc + 1) * 128], identity=identf[0:64, 0:64]), reads=['t2', 'identf'], writes=[psk[pi]])
                copy(evac_eng(), stg[:, b, 0:c4n * 128], ps[pi][:, 0:c4n * 128], [psk[pi]], [('stg', b)])
                P.dma('pool', nbw_d[c4 * c4n * 128:(c4 + 1) * c4n * 128, :].rearrange("(n p) c -> p n c", p=128), stg[:, b, 0:c4n * 128].rearrange("p (n c) -> p n c", c=128),
                      reads=[('stg', b)], writes=['nbw_d'])
            P.barrier()
        with ExitStack() as es:
            hf = es.enter_context(SB("m2_hf", [128, 2048], F32))
            hfb = es.enter_context(SB("m2_hfb", [128, 2, 2048], BF16))
            decb = es.enter_context(SB("m2_dec", [128, NT, 64], F32))
            selt = es.enter_context(SB("m2_sel", [64, 16, 4], F32))
            ngf = es.enter_context(SB("m2_ngf", [128, 2, 512], F32))
            ng = es.enter_context(SB("m2_ng", [128, 2, 512], BF16))
            Dbc = es.enter_context(SB("m2_D", [128, 2048], F32))
            nwbc = es.enter_context(SB("m2_nw", [128, 2048], F32))
            bcf = es.enter_context(SB("m2_bc", [128, 2, 16, 128], BF16))
            xtm = es.enter_context(SB("m2_x", [128, 2, 2048], BF16))
            btm = es.enter_context(SB("m2_bt", [128, 2, 1024], BF16))
            nbw = es.enter_context(SB("m2_nbw", [128, 2, 128], F32))
            csc = es.enter_context(SB("m2_cs", [128, 2, 128], F32))
            csr = es.enter_context(SB("m2_csr", [128, 2, 128], F32))
            zt = es.enter_context(SB("m2_z", [128, 2, 2048], F32))
            hbt = es.enter_context(SB("m2_hb", [128, 2, 2048], BF16))
            xdg = es.enter_context(SB("m2_xd", [64, 2, 512], F32))
            ecs = es.enter_context(SB("m2_ecs", [128, 2, 64], F32))
            gg = es.enter_context(SB("m2_g", [128, 2, 2, 512], F32))
            selbig = es.enter_context(SB("m2_selbig", [128, 64, 128], F32))
            mt = es.enter_context(SB("m2_mt", [128, 2, 512], BF16))
            yo = es.enter_context(SB("m2_yo", [128, 2, 2048], F32))
            xd = es.enter_context(SB("m2_xd", [128, 2048], F32))
            xw = es.enter_context(SB("m2_xw", [128, 2, 256], BF16))
            tmpt = es.enter_context(SB("m2_tmp", [128, 2, 256], F32))
            y2 = es.enter_context(SB("m2_y", [128, 1024], F32))
            y3 = es.enter_context(SB("m2_y3", [128, 1024], F32))
            yb = es.enter_context(SB("m2_yb", [128, 1024], BF16))
            ss = es.enter_context(SB("m2_ss", [128, 8], F32))
            ytt = es.enter_context(SB("m2_yt", [128, 2, 8, 128], BF16))
            P.dma('sp', decb[:], dec_d[:, 0:NT, :], reads=['dec_d'], writes=['decb'])
            P.dma('sp', selt[:], sel_c, writes=['selt'])
            P.op('dve', lambda e: e.tensor_copy(out=selbig[0:64].bitcast(F32R), in_=identf[0:64, 0:64].unsqueeze(2).broadcast_to([64, 64, 128])), reads=['identf'], writes=['selbig'])
            P.op('dve', lambda e: e.tensor_copy(out=selbig[64:128].bitcast(F32R), in_=identf[64:128, 64:128].unsqueeze(2).broadcast_to([64, 64, 128])), reads=['identf'], writes=['selbig'])
            P.dma('sp', ngf[:], negm_c, writes=['ngf'])
            P.op('dve', lambda e: e.tensor_copy(out=ng[:], in_=ngf[:]), reads=['ngf'], writes=['ng'])
            P.dma('sp', Dbc[:], W['m_d_rep'][j].unsqueeze(0).broadcast_to([128, 2048]), writes=['Dbc'])
            P.dma('sp', nwbc[:], W['m_norm_w'][j].unsqueeze(0).broadcast_to([128, 2048]), writes=['nwbc'])

            def state_update_g(c, d, b, hstate, hkey, pbank, g):
                xb = (c * 8 + g) % 2
                P.op('pool', lambda e: e.tensor_tensor(out=xw[:, xb, :].rearrange("p (h q) -> p h q", h=4), in0=xtm[:, b, g * 256:(g + 1) * 256].rearrange("p (h q) -> p h q", h=4),
                                                        in1=nbw[:, b, 64 + d * 32 + g * 4:64 + d * 32 + g * 4 + 4].unsqueeze(2).broadcast_to([128, 4, 64]), op=ALU.mult),
                     reads=[('xtm', b), ('nbw', b)], writes=[('xw', xb)])
                P.op('pe', lambda e: e.matmul(ps[pbank][:, 0:256], btm[:, b, g * 128:(g + 1) * 128], xw[:, xb, :], start=True, stop=True), reads=[('btm', b), ('xw', xb)], writes=[psk[pbank]])
                P.op('dve', lambda e: e.tensor_tensor(out=tmpt[:, g % 2, :].rearrange("p (h q) -> p h q", h=4), in0=hstate[:, g * 256:(g + 1) * 256].rearrange("p (h q) -> p h q", h=4),
                                                       in1=decb[:, c, d * 32 + g * 4:d * 32 + g * 4 + 4].unsqueeze(2).broadcast_to([128, 4, 64]), op=ALU.mult),
                     reads=[(hkey, g), 'decb'], writes=[('tmpt', g % 2)])
                P.op('dve', lambda e: e.tensor_tensor(out=hstate[:, g * 256:(g + 1) * 256], in0=tmpt[:, g % 2, :], in1=ps[pbank][:, 0:256], op=ALU.add), reads=[('tmpt', g % 2), psk[pbank]], writes=[(hkey, g)])

            def state_update(c, d, b, hstate, hkey, pbank):
                for g in range(8):
                    state_update_g(c, d, b, hstate, hkey, pbank, g)

            hfkeys = [('hf', g) for g in range(8)]

            P.op('dve', lambda e: e.memset(hf[:], 0.0), writes=hfkeys)
            P.op('dve', lambda e: e.memset(hfb[:], 0.0), writes=[('hfb', 0), ('hfb', 1)])
            for ci, c in enumerate(range(NT - 1, -1, -1)):
                b = ci % 2
                P.dma('sp', xtm[:, b, :], xtm_d[c * 128:(c + 1) * 128, :], writes=[('xtm', b)])
                P.dma('sp', btm[:, b, :], btm_d[c * 128:(c + 1) * 128, :], writes=[('btm', b)])
                P.dma('sp', nbw[:, b, :], nbw_d[c * 128:(c + 1) * 128, :], writes=[('nbw', b)])
                P.dma('pool', hb_d[c], hfb[:, b, :], reads=[('hfb', b)], writes=['hb_d'])
                if c > 0:
                    state_update(c, 1, b, hf, 'hf', 6)
                    P.op('pool', lambda e: e.tensor_copy(out=hfb[:, 1 - b, :], in_=hf[:]), reads=hfkeys, writes=[('hfb', 1 - b)])
            P.barrier()
            P.op('dve', lambda e: e.memset(hf[:], 0.0), writes=hfkeys)
            P.op('dve', lambda e: e.memset(hfb[:], 0.0), writes=[('hfb', 0), ('hfb', 1)])
            for c in range(NT):
                b = c % 2
                P.dma('sp', bcf[:, b], xbcT_d[:, 16:32, c * 128:(c + 1) * 128], writes=[('bcf', b)])
                P.dma('sp', xtm[:, b, :], xtm_d[c * 128:(c + 1) * 128, :], writes=[('xtm', b)])
                P.dma('sp', btm[:, b, :], btm_d[c * 128:(c + 1) * 128, :], writes=[('btm', b)])
                P.dma('sp', nbw[:, b, :], nbw_d[c * 128:(c + 1) * 128, :], writes=[('nbw', b)])
                P.dma('sp', csc[:, b, :], cs_d[:, c * 128:(c + 1) * 128], writes=[('csc', b)])
                P.op('pool', lambda e: e.tensor_copy(out=csr[:, b, :].bitcast(F32R), in_=csc[:, b, :]), reads=[('csc', b)], writes=[('csr', b)])
                P.dma('sp', zt[:, b, :], z_d[c * 128:(c + 1) * 128, :], writes=[('zt', b)])
                P.dma('sp', hbt[:, b, :], hb_d[c], writes=[('hbt', b)])
                P.dma('sp', ecs[:, b, :], ecs_d[c * 128:(c + 1) * 128, :], writes=[('ecs', b)])
                def cbmm(half):
                    for gi in range(4):
                        g = half * 4 + gi
                        P.op('pe', lambda e: e.matmul(ps[0][:, gi * 128:(gi + 1) * 128], bcf[:, b, g, :], bcf[:, b, 8 + g, :], start=True, stop=True), reads=[('bcf', b)], writes=[psk[0]])

                def front_pe_act(g):
                    gp = g % 2
                    for d in range(2):
                        pB = 3 if d == 0 else 6
                        for hh in range(4):
                            krow = d * 32 + g * 4 + hh
                            P.op('pe', lambda e: e.matmul(ps[pB][:, hh * 128:(hh + 1) * 128], selbig[:, krow, :].bitcast(F32R), csr[:, b, :].bitcast(F32R), start=(hh == 0), stop=False), reads=['selbig', ('csr', b)], writes=[psk[pB]])
                        P.op('pe', lambda e: e.matmul(ps[pB][:, :], identb[:], ng[:, d, :], start=False, stop=True), reads=['identb', 'ng'], writes=[psk[pB]])
                        for hh in range(4):
                            hcol = d * 32 + g * 4 + hh
                            P.op('act', lambda e: e.activation(out=gg[:, gp, d, hh * 128:(hh + 1) * 128], in_=ps[pB][:, hh * 128:(hh + 1) * 128], func=AF.Exp, bias=nbw[:, b, hcol:hcol + 1]),
                                 reads=[psk[pB], ('nbw', b)], writes=[('gg', gp, d)])
                        hsrc, hk = (hfb, ('hfb', b)) if d == 0 else (hbt, ('hbt', b))
                        P.op('pe', lambda e: e.matmul(ps[1 + d][:, 0:256], bcf[:, b, 8 + g, :], hsrc[:, b, g * 256:(g + 1) * 256], start=True, stop=True), reads=[('bcf', b), hk], writes=[psk[1 + d]])

                def front_dve(g):
                    for d in range(2):
                        P.op('dve', lambda e: e.tensor_tensor(out=yo[:, d, g * 256:(g + 1) * 256].rearrange("p (h q) -> p h q", h=4), in0=ps[1 + d][:, 0:256].rearrange("p (h q) -> p h q", h=4),
                                                               in1=ecs[:, b, d * 32 + g * 4:d * 32 + g * 4 + 4].unsqueeze(2).broadcast_to([128, 4, 64]), op=ALU.mult),
                             reads=[psk[1 + d], ('ecs', b)], writes=[('yo', d, g // 4)])

                def back_dve(g):
                    gp = g % 2
                    gi = g % 4
                    P.op('dve', lambda e: e.tensor_tensor(out=gg[:, gp, 0, :], in0=gg[:, gp, 0, :], in1=gg[:, gp, 1, :], op=ALU.add), reads=[('gg', gp, 0), ('gg', gp, 1)], writes=[('gg', gp, 0)])
                    P.op('dve', lambda e: e.tensor_tensor(out=mt[:, gp, :].rearrange("p (h q) -> p h q", h=4), in0=gg[:, gp, 0, :].rearrange("p (h q) -> p h q", h=4),
                                                           in1=ps[0][:, gi * 128:(gi + 1) * 128].unsqueeze(1).broadcast_to([128, 4, 128]), op=ALU.mult),
                         reads=[('gg', gp, 0), psk[0]], writes=[('mt', gp)])

                def back_pe(g):
                    gp = g % 2
                    gi = g % 4
                    py = 4 + gi // 2
                    for hh in range(4):
                        h = g * 4 + hh
                        oc = (gi % 2) * 256 + hh * 64
                        P.op('pe', lambda e: e.matmul(ps[py][:, oc:oc + 64], mt[:, gp, hh * 128:(hh + 1) * 128], xtm[:, b, h * 64:(h + 1) * 64], start=True, stop=True),
                             reads=[('mt', gp), ('xtm', b)], writes=[psk[py]])

                def state_g(g):
                    xb = (c * 8 + g) % 2
                    P.op('pool', lambda e: e.tensor_tensor(out=xw[:, xb, :].rearrange("p (h q) -> p h q", h=4), in0=xtm[:, b, g * 256:(g + 1) * 256].rearrange("p (h q) -> p h q", h=4),
                                                            in1=nbw[:, b, 64 + g * 4:64 + g * 4 + 4].unsqueeze(2).broadcast_to([128, 4, 64]), op=ALU.mult),
                         reads=[('xtm', b), ('nbw', b)], writes=[('xw', xb)])
                    P.op('pe', lambda e: e.matmul(ps[7][:, 0:256], btm[:, b, g * 128:(g + 1) * 128], xw[:, xb, :], start=True, stop=True), reads=[('btm', b), ('xw', xb)], writes=[psk[7]])
                    P.op('dve', lambda e: e.tensor_tensor(out=tmpt[:, g % 2, :].rearrange("p (h q) -> p h q", h=4), in0=hf[:, g * 256:(g + 1) * 256].rearrange("p (h q) -> p h q", h=4),
                                                           in1=decb[:, c, g * 4:g * 4 + 4].unsqueeze(2).broadcast_to([128, 4, 64]), op=ALU.mult),
                         reads=[('hf', g), 'decb'], writes=[('tmpt', g % 2)])
                    P.op('dve', lambda e: e.tensor_tensor(out=hf[:, g * 256:(g + 1) * 256], in0=tmpt[:, g % 2, :], in1=ps[7][:, 0:256], op=ALU.add), reads=[('tmpt', g % 2), psk[7]], writes=[('hf', g)])
                    P.op('dve', lambda e: e.tensor_copy(out=hfb[:, 1 - b, g * 256:(g + 1) * 256], in_=hf[:, g * 256:(g + 1) * 256]), reads=[('hf', g)], writes=[('hfb', 1 - b)])

                def epilogue(half):
                    c0 = half * 1024
                    P.op('pool', lambda e: e.tensor_tensor(out=yo[:, 0, c0:c0 + 1024], in0=yo[:, 0, c0:c0 + 1024], in1=yo[:, 1, c0:c0 + 1024], op=ALU.add), reads=[('yo', 0, half), ('yo', 1, half)], writes=[('yo', 0, half)])
                    for q in range(2):
                        P.op('dve', lambda e: e.tensor_tensor(out=y2[:, q * 512:(q + 1) * 512], in0=xd[:, c0 + q * 512:c0 + (q + 1) * 512], in1=ps[4 + q][:, :], op=ALU.add), reads=['xd', psk[4 + q]], writes=['y2'])
                    P.op('dve', lambda e: e.tensor_tensor(out=y2[:], in0=y2[:], in1=yo[:, 0, c0:c0 + 1024], op=ALU.add), reads=['y2', ('yo', 0, half)], writes=['y2'])
                    P.op('dve', lambda e: e.tensor_tensor(out=y2[:], in0=y2[:], in1=zt[:, b, c0:c0 + 1024], op=ALU.mult), reads=['y2', ('zt', b)], writes=['y2'])
                    P.op('dve', lambda e: e.tensor_tensor(out=y3[:], in0=y2[:], in1=y2[:], op=ALU.mult), reads=['y2', 'y3'], writes=['y3'])
                    P.op('dve', lambda e: e.tensor_reduce(out=ss[:, 0:4], in_=y3[:].rearrange("p (g c) -> p g c", g=4), op=ALU.add, axis=AX.X), reads=['y3'], writes=['ss'])
                    P.op('dve', lambda e: e.tensor_scalar(out=ss[:, 0:4], in0=ss[:, 0:4], scalar1=1.0 / 256, scalar2=1e-5, op0=ALU.mult, op1=ALU.add), reads=['ss'], writes=['ss'])
                    P.op('act', lambda e: e.activation(out=ss[:, 0:4], in_=ss[:, 0:4], func=AF.Ln), reads=['ss'], writes=['ss'])
                    P.op('act', lambda e: e.activation(out=ss[:, 4:8], in_=ss[:, 0:4], func=AF.Exp, scale=-0.5), reads=['ss'], writes=['ss2'])
                    P.op('dve', lambda e: e.tensor_tensor(out=y2[:].rearrange("p (g c) -> p g c", g=4), in0=y2[:].rearrange("p (g c) -> p g c", g=4), in1=ss[:, 4:8].unsqueeze(2).broadcast_to([128, 4, 256]), op=ALU.mult),
                         reads=['y2', 'ss2'], writes=['y2'])
                    P.op('pool', lambda e: e.tensor_tensor(out=yb[:], in0=y2[:], in1=nwbc[:, c0:c0 + 1024], op=ALU.mult), reads=['y2', 'nwbc'], writes=['yb'])
                    pv = ps[7][:, :].bitcast(BF16)
                    for i in range(8):
                        P.op('pe', lambda e: e.transpose(out=pv[:, i * 128:(i + 1) * 128], in_=yb[:, i * 128:(i + 1) * 128], identity=identb[:]), reads=['yb', 'identb'], writes=[psk[7]])
                    yb2 = (c * 2 + half) % 2
                    copy('dve', ytt[:, yb2, :, :], pv.rearrange("p (n t) -> p n t", n=8), [psk[7]], [('ytt', yb2)])
                    P.dma('pool', yT_d[:, half * 8:(half + 1) * 8, c * 128:(c + 1) * 128], ytt[:, yb2, :, :], reads=[('ytt', yb2)], writes=['yT_d'])

                for q in range(2):
                    P.op('pool', lambda e: e.tensor_tensor(out=xd[:, q * 1024:(q + 1) * 1024], in0=xtm[:, b, q * 1024:(q + 1) * 1024], in1=Dbc[:, q * 1024:(q + 1) * 1024], op=ALU.mult), reads=[('xtm', b), 'Dbc'], writes=['xd'])
                cbmm(0)
                front_pe_act(0)
                front_dve(0)
                for g in range(8):
                    if g + 1 < 8:
                        front_pe_act(g + 1)
                    back_dve(g)
                    if g + 1 < 8:
                        front_dve(g + 1)
                    back_pe(g)
                    if c < NT - 1:
                        state_g(g)
                    if g == 3:
                        cbmm(1)
                    if g % 4 == 3:
                        epilogue(g // 4)
            P.barrier()

    def phase_rwkv(j, L):
        NC = L // 64
        NB = L // 512
        Lh = min(L, 1024)
        NH = L // Lh
        NCh = Lh // 64
        NBh = Lh // 512
        LAM = math.exp(-0.5)
        win = wb['r_w_in'][j]
        with SB("r1_xT", [128, 8, L], BF16) as xT, SB("r1_w", [128, 2, 8, 128], BF16) as wsl, \
                SB("r1_rb", [128, L + 2], F32) as rb, SB("r1_tmp", [128, L], F32) as tmp, \
                SB("r1_o", [128, 2, L], F32) as orow, SB("r1_mu", [128, 3, 34], F32) as mu:
            P.dma('sp', xT[:], xT_d[:, :, 0:L], writes=['r1_xT'])
            P.dma('sp', mu[:, 0, :], W['r_mu_t'][j], writes=['mu0'])
            P.op('dve', lambda e: e.tensor_scalar(out=mu[:, 1, :], in0=mu[:, 0, :], scalar1=0.5, scalar2=None, op0=ALU.mult), reads=['mu0'], writes=['mu1'])
            P.op('dve', lambda e: e.tensor_scalar(out=mu[:, 2, :], in0=mu[:, 0, :], scalar1=-1.0, scalar2=1.0, op0=ALU.mult, op1=ALU.add), reads=['mu0'], writes=['mu2'])
            P.op('dve', lambda e: e.memset(rb[:, 0:1], 0.0), writes=['rb'])
            P.op('dve', lambda e: e.memset(rb[:, L + 1:L + 2], 0.0), writes=['rb'])
            k = 0
            for fc in range(34):
                wbuf = fc % 2
                P.dma('sp', wsl[:, wbuf], win[:, :, fc * 128:(fc + 1) * 128], writes=[('r1w', wbuf)])
                for tb in range(NB):
                    pi = k % 6
                    k += 1
                    for dc in range(8):
                        P.op('pe', lambda e: e.matmul(ps[pi][:, :], wsl[:, wbuf, dc, :], xT[:, dc, tb * 512:(tb + 1) * 512], start=(dc == 0), stop=(dc == 7)),
                             reads=['r1_xT', ('r1w', wbuf)], writes=[psk[pi]])
                    P.op('act', lambda e: e.copy(out=rb[:, 1 + tb * 512:1 + (tb + 1) * 512], in_=ps[pi][:, :]), reads=[psk[pi]], writes=['rb'])
                P.op('dve', lambda e: e.tensor_tensor(out=tmp[:], in0=rb[:, 0:L], in1=rb[:, 2:L + 2], op=ALU.add), reads=['rb'], writes=['tmp'])
                P.op('dve', lambda e: e.tensor_scalar(out=tmp[:], in0=tmp[:], scalar1=mu[:, 1, fc:fc + 1], scalar2=None, op0=ALU.mult), reads=['tmp', 'mu1'], writes=['tmp'])
                ob_ = fc % 2
                P.op('dve', lambda e: e.scalar_tensor_tensor(out=orow[:, ob_, :], in0=rb[:, 1:L + 1], scalar=mu[:, 2, fc:fc + 1], in1=tmp[:], op0=ALU.mult, op1=ALU.add),
                     reads=['rb', 'tmp', 'mu2'], writes=[('orow', ob_)])
                P.dma('pool', u_d[:, fc, 0:L], orow[:, ob_, :], reads=[('orow', ob_)], writes=['u_d'])
            P.barrier()
        if RW_STAGE < 2:
            return
        with ExitStack() as es:
            A_ = lambda n, sh, dt: es.enter_context(SB(n, sh, dt))
            wupf = A_("r2_wupf", [128, 2, 1024], F32)
            wupb = A_("r2_wupb", [128, 2, 1024], BF16)
            cols = A_("r2_cols", [128, 9, 8], F32)
            blk = A_("r2_blk", [128, 128], F32)
            m64 = A_("r2_m64", [128, Lh], F32)
            lwt = A_("r2_lwt", [128, Lh], BF16)
            lat = A_("r2_lat", [128, Lh], BF16)
            rows = {n: A_("r2_" + n, [128, Lh], F32) for n in ('r', 'k', 'v', 'sg0', 'sg1', 'a0', 'a1', 'kk', 't1', 't2', 't4', 'bs', 'e0', 'e1', 'ei')}
            opn = A_("r2_opn", [128, 2, NCh, 4, 64], F32)
            gbs = A_("r2_gbs", [128, 2, Lh], F32)
            wcs = A_("r2_wcs", [128, 2, NCh], F32)
            P.dma('sp', wupf[:, 0, :], W['r_wup_t'][j], writes=['wupf'])
            P.dma('sp', wupf[:, 1, :], W['r_aup_t'][j], writes=['wupf'])
            P.op('dve', lambda e: e.tensor_copy(out=wupb[:], in_=wupf[:]), reads=['wupf'], writes=['wupb'])
            P.dma('sp', cols[:, 0:2, :], W['r_w0_t'][j], writes=['cols'])
            P.dma('sp', cols[:, 2:4, :], W['r_a0_t'][j], writes=['cols'])
            P.dma('sp', cols[:, 4, :], W['r_kk_t'][j], writes=['cols'])
            P.dma('sp', cols[:, 5, :], W['r_ka_t'][j], writes=['cols'])
            P.dma('sp', cols[:, 7, :], W['r_rk_t'][j], writes=['cols'])
            P.op('dve', lambda e: e.tensor_scalar(out=cols[:, 6, :], in0=cols[:, 5, :], scalar1=-1.0, scalar2=1.0, op0=ALU.mult, op1=ALU.add), reads=['cols'], writes=['cols6'])
            P.dma('sp', blk[:], blk_c, writes=['blk'])
            P.dma('sp', m64[:], mask64_c[:, 0:Lh], writes=['m64'])
            R = rows
            kq = 0
            for hf_ in range(NH):
                t0h = hf_ * Lh
                P.dma('sp', R['t1'][:], u_d[:, 32, t0h:t0h + Lh], writes=['t1'])
                P.op('act', lambda e: e.activation(out=lwt[:], in_=R['t1'][:], func=AF.Tanh), reads=['t1'], writes=['lwt'])
                P.dma('sp', R['t2'][:], u_d[:, 33, t0h:t0h + Lh], writes=['t2'])
                P.op('act', lambda e: e.copy(out=lat[:], in_=R['t2'][:]), reads=['t2'], writes=['lat'])
                for cc in range(8):
                    P.dma('sp', R['r'][:], u_d[:, cc, t0h:t0h + Lh], writes=['r'])
                    P.dma('sp', R['k'][:], u_d[:, 8 + cc, t0h:t0h + Lh], writes=['k'])
                    P.dma('sp', R['v'][:], u_d[:, 16 + cc, t0h:t0h + Lh], writes=['v'])
                    P.dma('sp', R['t4'][:], u_d[:, 24 + cc, t0h:t0h + Lh], writes=['t4'])
                    P.op('act', lambda e: e.activation(out=gbs[:, 0, :], in_=R['t4'][:], func=AF.Silu), reads=['t4'], writes=['gbs0'])
                    for tb in range(NBh):
                        sl = slice(tb * 512, (tb + 1) * 512)
                        for d in range(2):
                            P.op('pe', lambda e: e.matmul(ps[d][:, :], wupb[d * 64:(d + 1) * 64, 0, cc * 128:(cc + 1) * 128], lwt[d * 64:(d + 1) * 64, sl], start=True, stop=True),
                                 reads=['wupb', 'lwt'], writes=[psk[d]])
                            P.op('act', lambda e: e.activation(out=R['sg%d' % d][:, sl], in_=ps[d][:, :], func=AF.Sigmoid, bias=cols[:, d, cc:cc + 1]), reads=[psk[d], 'cols'], writes=['sg%d' % d])
                            P.op('pe', lambda e: e.matmul(ps[2 + d][:, :], wupb[d * 64:(d + 1) * 64, 1, cc * 128:(cc + 1) * 128], lat[d * 64:(d + 1) * 64, sl], start=True, stop=True),
                                 reads=['wupb', 'lat'], writes=[psk[2 + d]])
                            P.op('act', lambda e: e.activation(out=R['a%d' % d][:, sl], in_=ps[2 + d][:, :], func=AF.Sigmoid, bias=cols[:, 2 + d, cc:cc + 1]), reads=[psk[2 + d], 'cols'], writes=['a%d' % d])
                    P.op('dve', lambda e: e.tensor_scalar(out=R['kk'][:], in0=R['k'][:], scalar1=cols[:, 4, cc:cc + 1], scalar2=None, op0=ALU.mult), reads=['k', 'cols'], writes=['kk'])
                    P.op('dve', lambda e: e.tensor_tensor(out=R['t1'][:], in0=R['kk'][:], in1=R['kk'][:], op=ALU.mult), reads=['kk', 't1'], writes=['t1'])
                    for tb in range(NBh):
                        sl = slice(tb * 512, (tb + 1) * 512)
                        pi = 4 + tb % 2
                        P.op('pe', lambda e: e.matmul(ps[pi][:, :], blk[:], R['t1'][:, sl], start=True, stop=True), reads=['blk', 't1'], writes=[psk[pi]])
                        P.op('act', lambda e: e.activation(out=R['t2'][:, sl], in_=ps[pi][:, :], func=AF.Sqrt), reads=[psk[pi], 't2'], writes=['t2'])
                    P.op('dve', lambda e: e.tensor_scalar(out=R['t2'][:], in0=R['t2'][:], scalar1=1e-12, scalar2=None, op0=ALU.max), reads=['t2'], writes=['t2'])
                    P.op('dve', lambda e: e.reciprocal(out=R['t2'][:], in_=R['t2'][:]), reads=['t2'], writes=['t2'])
                    P.op('dve', lambda e: e.tensor_tensor(out=R['kk'][:], in0=R['kk'][:], in1=R['t2'][:], op=ALU.mult), reads=['kk', 't2'], writes=['kk'])
                    for d in range(2):
                        sg = R['sg%d' % d]
                        a_ = R['a%d' % d]
                        ob2 = kq % 2
                        kq += 1
                        P.op('dve', lambda e: e.tensor_scalar(out=R['t1'][:], in0=a_[:], scalar1=cols[:, 5, cc:cc + 1], scalar2=cols[:, 6, cc:cc + 1], op0=ALU.mult, op1=ALU.add), reads=['a%d' % d, 'cols', 'cols6', 't1'], writes=['t1'])
                        P.op('dve', lambda e: e.tensor_tensor(out=R['t1'][:], in0=R['t1'][:], in1=R['k'][:], op=ALU.mult), reads=['t1', 'k'], writes=['t1'])
                        if d == 0:
                            P.op('dve', lambda e: e.scalar_tensor_tensor(out=R['bs'][:], in0=R['t1'][:], scalar=cols[:, 7, cc:cc + 1], in1=R['r'][:], op0=ALU.mult, op1=ALU.mult), reads=['t1', 'r', 'cols', 'bs'], writes=['bs'])
                        else:
                            P.op('dve', lambda e: e.scalar_tensor_tensor(out=R['t2'][:], in0=R['t1'][:], scalar=cols[:, 7, cc:cc + 1], in1=R['r'][:], op0=ALU.mult, op1=ALU.mult), reads=['t1', 'r', 'cols', 't2'], writes=['t2'])
                            P.op('dve', lambda e: e.tensor_tensor(out=R['bs'][:], in0=R['bs'][:], in1=R['t2'][:], op=ALU.add), reads=['bs', 't2'], writes=['bs'])
                        P.op('dve', lambda e: e.tensor_tensor_scan(out=R['t2'][:], data0=m64[:], data1=sg[:], initial=0.0, op0=ALU.mult, op1=ALU.add), reads=['m64', 'sg%d' % d, 't2'], writes=['t2'])
                        t23 = R['t2'][:].rearrange("p (c t) -> p c t", t=64)
                        t43 = R['t4'][:].rearrange("p (c t) -> p c t", t=64)
                        if d == 1:
                            P.op('dve', lambda e: e.tensor_tensor(out=t43, in0=t23[:, :, 63:64].broadcast_to([128, NCh, 64]), in1=t23, op=ALU.subtract), reads=['t2', 't4'], writes=['t4'])
                            P.op('dve', lambda e: e.tensor_tensor(out=R['t2'][:], in0=R['t4'][:], in1=sg[:], op=ALU.add), reads=['t4', 'sg%d' % d], writes=['t2'])
                        P.op('act', lambda e: e.activation(out=R['e1'][:], in_=R['t2'][:], func=AF.Exp, scale=-LAM), reads=['t2', 'e1'], writes=['e1'])
                        P.op('act', lambda e: e.activation(out=R['ei'][:], in_=R['t2'][:], func=AF.Exp, scale=LAM), reads=['t2', 'ei'], writes=['ei'])
                        P.op('dve', lambda e: e.tensor_tensor(out=R['t4'][:], in0=R['t2'][:], in1=sg[:], op=ALU.subtract), reads=['t2', 'sg%d' % d, 't4'], writes=['t4'])
                        P.op('act', lambda e: e.activation(out=R['e0'][:], in_=R['t4'][:], func=AF.Exp, scale=-LAM), reads=['t4', 'e0'], writes=['e0'])
                        e13 = R['e1'][:].rearrange("p (c t) -> p c t", t=64)
                        ecol = 63 if d == 0 else 0
                        P.op('act', lambda e: e.copy(out=wcs[:, d, :], in_=e13[:, :, ecol]), reads=['e1'], writes=['wcs'])
                        def o3(kind):
                            return opn[:, ob2, :, kind, :]
                        def v3(n):
                            return R[n][:].rearrange("p (c t) -> p c t", t=64)
                        P.op('dve', lambda e: e.tensor_tensor(out=R['t4'][:], in0=R['kk'][:], in1=a_[:], op=ALU.mult), reads=['kk', 'a%d' % d, 't4'], writes=['t4'])
                        P.op('dve', lambda e: e.tensor_tensor(out=o3(0), in0=v3('t4'), in1=v3('ei'), op=ALU.mult), reads=['t4', 'ei'], writes=[('opn', ob2)])
                        P.op('pool', lambda e: e.tensor_tensor(out=o3(1), in0=v3('t1'), in1=v3('ei'), op=ALU.mult), reads=['t1', 'ei'], writes=[('opn', ob2)])
                        P.op('dve', lambda e: e.scalar_tensor_tensor(out=o3(2), in0=v3('kk'), scalar=-1.0, in1=v3('e0'), op0=ALU.mult, op1=ALU.mult), reads=['kk', 'e0'], writes=[('opn', ob2)])
                        P.op('pool', lambda e: e.tensor_tensor(out=o3(3), in0=v3('r'), in1=v3('e1'), op=ALU.mult), reads=['r', 'e1'], writes=[('opn', ob2)])
                        for hh in range(2):
                            P.dma('pool', rop_d[:, 2 * cc + hh, d, hf_ * NCh:(hf_ + 1) * NCh, :, :], opn[hh * 64:(hh + 1) * 64, ob2], reads=[('opn', ob2)], writes=['rop_d'])
                    for tb in range(NBh):
                        sl = slice(tb * 512, (tb + 1) * 512)
                        pi = 6 + tb % 2
                        P.op('pe', lambda e: e.matmul(ps[pi][:, :], blk[:], R['bs'][:, sl], start=True, stop=True), reads=['blk', 'bs'], writes=[psk[pi]])
                        P.op('dve', lambda e: e.tensor_tensor(out=gbs[:, 1, sl], in0=ps[pi][:, :], in1=R['v'][:, sl], op=ALU.mult), reads=[psk[pi], 'v'], writes=['gbs1'])
                    P.dma('pool', gb_d[:, cc, :, t0h:t0h + Lh], gbs[:], reads=['gbs0', 'gbs1'], writes=['gb_d'])
                    for hh in range(2):
                        P.dma('pool', wc_d[:, 2 * cc + hh, :, hf_ * NCh:(hf_ + 1) * NCh], wcs[hh * 64:(hh + 1) * 64], reads=['wcs'], writes=['wc_d'])
            P.barrier()
        if RW_STAGE < 3:
            return
        with ExitStack() as es:
            A_ = lambda n, sh, dt: es.enter_context(SB(n, sh, dt))
            fr = lambda ap: ap.bitcast(F32R)
            mk = A_("r3_mk", [128, 2, 128], F32)
            mkn = A_("r3_mkn", [64, 2, 64], F32)
            wcall = A_("r3_wc", [64, 16, 2, NC], F32)
            lncol = A_("r3_ln", [128, 2, 8], F32)
            rop = A_("r3_rop", [128, 2, 16, 4, 64], F32)
            vf = A_("r3_vf", [128, 2, 8, 128], F32)
            ropr = A_("r3_ropr", [128, 2, 16, 4, 64], F32)
            Hsr = A_("r3_Hsr", [128, 16, 64], F32)
            AT = A_("r3_AT", [128, 16, 128], F32)
            PT = A_("r3_PT", [128, 2, 16, 2, 64], F32)
            Qm = A_("r3_Q", [128, 2, 16, 64], F32)
            Z = A_("r3_Z", [128, 16, 64], F32)
            Zx = A_("r3_Zx", [128, 16, 64], F32)
            Xs = A_("r3_X", [128, 16, 64], F32)
            BKs = A_("r3_BK", [128, 16, 64], F32)
            Hs = A_("r3_H", [128, 16, 64], F32)
            Ht = A_("r3_Ht", [64, 16, 64], F32)
            ysb = A_("r3_y", [128, 2, 1024], F32)
            ybl = A_("r3_yb", [64, 2, 1024], F32)
            ysq = A_("r3_ysq", [64, 1024], F32)
            gst = A_("r3_gst", [64, 4, 16], F32)
            gbt = A_("r3_gb", [128, 2, 8, 2, 64], F32)
            ofm = A_("r3_ofm", [128, 512], F32)
            ofb = A_("r3_ofb", [128, 2, 8, 64], BF16)
            P.dma('sp', mk[:], rmask_c, writes=['mk'])
            P.dma('sp', mkn[:], rmaskn_c, writes=['mkn'])
            P.dma('sp', wcall[:], wc_d[:, :, :, 0:NC], writes=['wcall'])
            P.dma('sp', lncol[:, 0, :], W['r_lnw_t'][j], writes=['lncol'])
            P.dma('sp', lncol[:, 1, :], W['r_lnb_t'][j], writes=['lncol'])
            P.op('dve', lambda e: e.memset(vf[:], 0.0), writes=[('vf', 0), ('vf', 1)])
            P.op('dve', lambda e: e.memset(rop[:, 0].rearrange("p h k t -> p (h k t)"), 0.0), writes=[('rop', 0)])
            P.op('dve', lambda e: e.memset(rop[:, 1].rearrange("p h k t -> p (h k t)"), 0.0), writes=[('rop', 1)])
            P.op('dve', lambda e: e.memset(ropr[:, 0].rearrange("p h k t -> p (h k t)"), 0.0), writes=[('ropr', 0)])
            P.op('dve', lambda e: e.memset(ropr[:, 1].rearrange("p h k t -> p (h k t)"), 0.0), writes=[('ropr', 1)])
            P.op('dve', lambda e: e.memset(Hsr[:], 0.0), writes=['Hsr'])
            if RW_X != 2:
                P.op('dve', lambda e: e.memset(PT[:].rearrange("p a h k t -> p (a h k t)"), 0.0), writes=[('PT', 0, 0), ('PT', 0, 1), ('PT', 1, 0), ('PT', 1, 1)])
                P.op('dve', lambda e: e.memset(Qm[:], 0.0), writes=[('Q', 0, 0), ('Q', 0, 1), ('Q', 1, 0), ('Q', 1, 1)])
                P.op('dve', lambda e: e.memset(Zx[:], 0.0), writes=['Zx'])
                P.op('dve', lambda e: e.memset(Xs[:], 0.0), writes=[('Xs', 0), ('Xs', 1)])
                P.op('dve', lambda e: e.memset(ysb[:], 0.0), writes=[('ysb', 0), ('ysb', 1)])

            def hview(bank_pair, q, w_):
                return ps[bank_pair + q][0:64, :].rearrange("p (h t) -> p h t", t=w_)

            def prefetch(si, c, d):
                b = si % 2
                P.dma('sp', rop[0:64, b], rop_d[:, :, d, c, :, :], writes=[('rop', b)])
                P.op('pool', lambda e: e.tensor_copy(out=fr(ropr[0:64, b].rearrange("p h k t -> p (h k t)")), in_=rop[0:64, b].rearrange("p h k t -> p (h k t)")), reads=[('rop', b)], writes=[('ropr', b)])
                P.dma('sp', vf[:, b, :, 64:128], u_d[:, 16:24, c * 64:(c + 1) * 64], writes=[('vf', b)])

            def step(si, c, d, nxt_c=None):
                b = si % 2
                for cc in range(8):
                    pi = 6 + cc // 4
                    P.op('pe', lambda e: e.transpose(out=ps[pi][:, (cc % 4) * 128:(cc % 4 + 1) * 128], in_=vf[:, b, cc, :], identity=identf[:]), reads=[('vf', b), 'identf'], writes=[psk[pi]])
                for q in range(2):
                    src = ps[6 + q][64:128, :].rearrange("p (h v) -> p h v", v=64)
                    copy('act', fr(Z[64:128, q * 8:(q + 1) * 8, :]), src, [psk[6 + q]], ['Zv'])
                    copy('pool', fr(Zx[64:128, q * 8:(q + 1) * 8, :]), Z[64:128, q * 8:(q + 1) * 8, :], ['Zv'], ['Zx'])
                if RW_SUB < 2:
                    return b
                for h in range(16):
                    pi = h // 4
                    P.op('pe', lambda e: e.matmul(ps[pi][:, (h % 4) * 128:(h % 4 + 1) * 128], fr(ropr[:, b, h, 0:2, :].rearrange("p a t -> p (a t)")),
                                                  fr(ropr[:, b, h, 2:4, :].rearrange("p a t -> p (a t)")), start=True, stop=True), reads=[('ropr', b)], writes=[psk[pi]])
                for q in range(4):
                    P.op('dve', lambda e: e.tensor_tensor(out=fr(AT[:, q * 4:(q + 1) * 4, :]), in0=ps[q][:, :].rearrange("p (h t) -> p h t", t=128),
                                                           in1=mk[:, d:d + 1, :].broadcast_to([128, 4, 128]), op=ALU.mult), reads=[psk[q], 'mk'], writes=['AT'])
                if nxt_c is not None:
                    prefetch(si + 1, nxt_c, d)
                for h in range(16):
                    pi = 4 + h // 8
                    P.op('pe', lambda e: e.matmul(ps[pi][0:64, (h % 8) * 64:(h % 8 + 1) * 64], fr(ropr[:, b, h, 2, :]), fr(ropr[:, b, h, 0, :]), start=True, stop=True),
                         reads=[('ropr', b)], writes=[psk[pi]])
                for q in range(2):
                    P.op('dve', lambda e: e.tensor_tensor(out=fr(Qm[0:64, 0, q * 8:(q + 1) * 8, :]), in0=hview(4, q, 64),
                                                           in1=mkn[:, d:d + 1, :].broadcast_to([64, 8, 64]), op=ALU.mult), reads=[psk[4 + q], 'mkn'], writes=[('Q', 0, q)])
                P.op('act', lambda e: e.activation(out=fr(PT[0:64, 0, :, 0, :]), in_=AT[0:64, :, 0:64], func=AF.Identity), reads=['AT'], writes=[('PT', 0, 0), ('PT', 0, 1)])
                P.op('dve', lambda e: e.tensor_tensor(out=fr(PT[0:64, 1, :, 1, :]), in0=AT[0:64, :, 0:64], in1=identf[0:64, 0:64].unsqueeze(1).broadcast_to([64, 16, 64]), op=ALU.add),
                     reads=['AT', 'identf'], writes=[('PT', 1, 0), ('PT', 1, 1)])
                if RW_SUB < 4:
                    return b
                for h in range(16):
                    pi = h // 4
                    P.op('pe', lambda e: e.transpose(out=ps[pi][:, (h % 4) * 128:(h % 4 + 1) * 128], in_=rop[:, b, h, 0:2, :].rearrange("p a t -> p (a t)"), identity=identf[:]),
                         reads=[('rop', b), 'identf'], writes=[psk[pi]])
                for q in range(4):
                    copy('act' if q % 2 == 0 else 'dve', fr(BKs[:, q * 4:(q + 1) * 4, :]), ps[q][:, :].rearrange("p (h k) -> p h k", k=128)[:, :, 0:64], [psk[q]], ['BKs'])
                if RW_STAGE < 4:
                    return b
                for q in range(2):
                    for h in range(q * 8, q * 8 + 8):
                        P.op('pe', lambda e: e.matmul(ps[0 + q][0:64, (h % 8) * 64:(h % 8 + 1) * 64], fr(Qm[:, 0, h, :]), fr(PT[:, 0, h, 0, :]), start=True, stop=True),
                             reads=[('Q', 0, q), ('PT', 0, q)], writes=[psk[0 + q]])
                    for h in range(q * 8, q * 8 + 8):
                        P.op('pe', lambda e: e.matmul(ps[4 + q][0:64, (h % 8) * 64:(h % 8 + 1) * 64], fr(PT[:, 0, h, 0, :]), fr(Qm[:, 0, h, :]), start=True, stop=True),
                             reads=[('Q', 0, q), ('PT', 0, q)], writes=[psk[4 + q]])
                    copy('act', fr(PT[0:64, 1, q * 8:(q + 1) * 8, 0, :]), hview(0, q, 64), [psk[0 + q]], [('PT', 1, q)])
                    copy('dve', fr(Qm[0:64, 1, q * 8:(q + 1) * 8, :]), hview(4, q, 64), [psk[4 + q]], [('Q', 1, q)])
                cur = 1
                for lv in range(1, 5):
                    nxt = 1 - cur
                    for q in range(2):
                        for h in range(q * 8, q * 8 + 8):
                            pi = 2 * q + (h % 8) // 4
                            P.op('pe', lambda e: e.matmul(ps[pi][0:64, (h % 4) * 128:(h % 4 + 1) * 128], fr(Qm[:, cur, h, :]), fr(PT[:, cur, h, :, :].rearrange("p k t -> p (k t)")), start=True, stop=True),
                                 reads=[('Q', cur, q), ('PT', cur, q)], writes=[psk[pi]])
                        for h in range(q * 8, q * 8 + 8):
                            P.op('pe', lambda e: e.matmul(ps[4 + q][0:64, (h % 8) * 64:(h % 8 + 1) * 64], fr(PT[:, cur, h, 0, :]), fr(Qm[:, cur, h, :]), start=True, stop=True),
                                 reads=[('Q', cur, q), ('PT', cur, q)], writes=[psk[4 + q]])
                        for hb in range(2):
                            pi = 2 * q + hb
                            h0 = q * 8 + hb * 4
                            pv3 = ps[pi][0:64, :].rearrange("p (h k t) -> p h k t", k=2, t=64)
                            if lv < 4:
                                P.op('act', lambda e: e.activation(out=fr(PT[0:64, nxt, h0:h0 + 4, 0, :]), in_=pv3[:, :, 0, :], func=AF.Identity), reads=[psk[pi]], writes=[('PT', nxt, q)])
                            P.op('dve', lambda e: e.tensor_tensor(out=fr(PT[0:64, nxt, h0:h0 + 4, 1, :]), in0=pv3[:, :, 1, :], in1=PT[0:64, cur, h0:h0 + 4, 1, :], op=ALU.add),
                                 reads=[psk[pi], ('PT', cur, q)], writes=[('PT', nxt, q)])
                        copy('act' if lv == 4 else 'dve', fr(Qm[0:64, nxt, q * 8:(q + 1) * 8, :]), hview(4, q, 64), [psk[4 + q]], [('Q', nxt, q)])
                    cur = nxt
                nxt = 1 - cur
                for q in range(2):
                    for h in range(q * 8, q * 8 + 8):
                        P.op('pe', lambda e: e.matmul(ps[0 + q][0:64, (h % 8) * 64:(h % 8 + 1) * 64], fr(Qm[:, cur, h, :]), fr(PT[:, cur, h, 1, :]), start=True, stop=True),
                             reads=[('Q', cur, q), ('PT', cur, q)], writes=[psk[0 + q]])
                    P.op('dve', lambda e: e.tensor_tensor(out=fr(PT[0:64, nxt, q * 8:(q + 1) * 8, 1, :]), in0=hview(0, q, 64), in1=PT[0:64, cur, q * 8:(q + 1) * 8, 1, :], op=ALU.add),
                         reads=[psk[0 + q], ('PT', cur, q)], writes=[('PT', nxt, q)])
                cur = nxt
                if RW_STAGE < 5:
                    return b
                for q in range(2):
                    for h in range(q * 8, q * 8 + 8):
                        oc = (h % 8) * 64
                        P.op('pe', lambda e: e.matmul(ps[0 + q][0:64, oc:oc + 64], fr(ropr[:, b, h, 2, :]), fr(Hsr[:, h, :]), start=True, stop=False), reads=[('ropr', b), 'Hsr'], writes=[psk[0 + q]])
                        P.op('pe', lambda e: e.matmul(ps[0 + q][0:64, oc:oc + 64], fr(AT[:, h, 0:64]), fr(Zx[:, h, :]), start=False, stop=True), reads=['AT', 'Zx'], writes=[psk[0 + q]])
                    copy('act' if q == 0 else 'dve', fr(Xs[0:64, q * 8:(q + 1) * 8, :]), hview(0, q, 64), [psk[q]], [('Xs', q)])
                for q in range(2):
                    for h in range(q * 8, q * 8 + 8):
                        oc = (h % 8) * 64
                        P.op('pe', lambda e: e.matmul(ps[2 + q][0:64, oc:oc + 64], fr(PT[:, cur, h, 1, :]), fr(Xs[:, h, :]), start=True, stop=True), reads=[('PT', cur, q), ('Xs', q)], writes=[psk[2 + q]])
                    copy('act' if q == 0 else 'dve', fr(Z[0:64, q * 8:(q + 1) * 8, :]), hview(2, q, 64), [psk[2 + q]], [('Zu', q)])
                for q in range(2):
                    for h in range(q * 8, q * 8 + 8):
                        oc = (h % 8) * 64
                        P.op('pe', lambda e: e.matmul(ps[4 + q][0:64, oc:oc + 64], fr(ropr[:, b, h, 3, :]), fr(Hsr[:, h, :]), start=True, stop=False), reads=[('ropr', b), 'Hsr'], writes=[psk[4 + q]])
                        P.op('pe', lambda e: e.matmul(ps[4 + q][0:64, oc:oc + 64], fr(AT[:, h, 64:128]), fr(Z[:, h, :]), start=False, stop=True), reads=['AT', ('Zu', q), 'Zv'], writes=[psk[4 + q]])
                if RW_STAGE < 6:
                    return b
                for h in range(16):
                    pi = 6 + h // 8
                    oc = (h % 8) * 64
                    P.op('pe', lambda e: e.matmul(ps[pi][0:64, oc:oc + 64], fr(BKs[:, h, :]), fr(Z[:, h, :]), start=True, stop=True), reads=['BKs', ('Zu', h // 8), 'Zv'], writes=[psk[pi]])
                for q in range(2):
                    P.op('dve', lambda e: e.tensor_tensor(out=Ht[:, q * 8:(q + 1) * 8, :], in0=Hs[0:64, q * 8:(q + 1) * 8, :], in1=hview(6, q, 64), op=ALU.add), reads=['Hs', psk[6 + q]], writes=['Ht'])
                P.op('dve', lambda e: e.tensor_tensor(out=Hs[0:64], in0=Ht[:], in1=wcall[:, :, d, c:c + 1].broadcast_to([64, 16, 64]), op=ALU.mult), reads=['Ht', 'wcall'], writes=['Hs'])
                P.op('pool', lambda e: e.tensor_copy(out=fr(Hsr[0:64]), in_=Hs[0:64]), reads=['Hs'], writes=['Hsr'])
                return b

            P.op('dve', lambda e: e.memset(Hs[:], 0.0), writes=['Hs'])
            P.op('dve', lambda e: e.memset(Hsr[:], 0.0), writes=['Hsr'])
            seqB = list(range(NC - 1, -1, -1))
            prefetch(0, seqB[0], 1)
            for si, c in enumerate(seqB):
                b = step(si, c, 1, seqB[si + 1] if si + 1 < NC else None)
                if RW_STAGE < 9:
                    continue
                for q in range(2):
                    copy('act' if q == 0 else 'dve', ysb[0:64, b, q * 512:(q + 1) * 512], ps[4 + q][0:64, :], [psk[4 + q]], [('ysb', b)])
                P.dma('pool', ybt_d[c * 64:(c + 1) * 64, :], ysb[0:64, b, :], reads=[('ysb', b)], writes=['ybt_d'])
            P.barrier()
            P.op('dve', lambda e: e.memset(Hs[:], 0.0), writes=['Hs'])
            P.op('dve', lambda e: e.memset(Hsr[:], 0.0), writes=['Hsr'])
            prefetch(0, 0, 0)
            for si, c in enumerate(range(NC)):
                b = si % 2
                P.dma('sp', ybl[:, b, :], ybt_d[c * 64:(c + 1) * 64, :], writes=[('ybl', b)])
                P.dma('sp', gbt[:, b], gb_d[:, :, :, c * 64:(c + 1) * 64], writes=[('gbt', b)])
                step(si, c, 0, c + 1 if c + 1 < NC else None)
                if RW_STAGE < 9:
                    continue
                for q in range(2):
                    P.op('dve', lambda e: e.tensor_tensor(out=ysb[0:64, b, q * 512:(q + 1) * 512], in0=ybl[:, b, q * 512:(q + 1) * 512], in1=ps[4 + q][0:64, :], op=ALU.add),
                         reads=[('ybl', b), psk[4 + q]], writes=[('ysb', b)])
                y3 = ysb[0:64, b, :].rearrange("p (h v) -> p h v", v=64)
                P.op('dve', lambda e: e.tensor_reduce(out=gst[:, 0, :], in_=y3, op=ALU.add, axis=AX.X), reads=[('ysb', b)], writes=['gst0'])
                P.op('pool', lambda e: e.tensor_tensor(out=ysq[:], in0=ysb[0:64, b, :], in1=ysb[0:64, b, :], op=ALU.mult), reads=[('ysb', b)], writes=['ysq'])
                P.op('dve', lambda e: e.tensor_reduce(out=gst[:, 1, :], in_=ysq[:].rearrange("p (h v) -> p h v", v=64), op=ALU.add, axis=AX.X), reads=['ysq'], writes=['gst1'])
                P.op('dve', lambda e: e.tensor_scalar(out=gst[:, 0, :], in0=gst[:, 0, :], scalar1=1.0 / 64, scalar2=None, op0=ALU.mult), reads=['gst0'], writes=['gst0'])
                P.op('dve', lambda e: e.tensor_tensor(out=gst[:, 2, :], in0=gst[:, 0, :], in1=gst[:, 0, :], op=ALU.mult), reads=['gst0'], writes=['gst2'])
                P.op('dve', lambda e: e.scalar_tensor_tensor(out=gst[:, 1, :], in0=gst[:, 1, :], scalar=1.0 / 64, in1=gst[:, 2, :], op0=ALU.mult, op1=ALU.subtract), reads=['gst1', 'gst2'], writes=['gst1'])
                P.op('dve', lambda e: e.tensor_scalar(out=gst[:, 1, :], in0=gst[:, 1, :], scalar1=64e-5, scalar2=None, op0=ALU.add), reads=['gst1'], writes=['gst1'])
                P.op('act', lambda e: e.activation(out=gst[:, 1, :], in_=gst[:, 1, :], func=AF.Sqrt), reads=['gst1'], writes=['gst1'])
                P.op('dve', lambda e: e.reciprocal(out=gst[:, 3, :], in_=gst[:, 1, :]), reads=['gst1'], writes=['gst3'])
                P.op('dve', lambda e: e.tensor_tensor(out=y3, in0=y3, in1=gst[:, 0, :].unsqueeze(2).broadcast_to([64, 16, 64]), op=ALU.subtract), reads=[('ysb', b), 'gst0'], writes=[('ysb', b)])
                P.op('dve', lambda e: e.tensor_tensor(out=y3, in0=y3, in1=gst[:, 3, :].unsqueeze(2).broadcast_to([64, 16, 64]), op=ALU.mult), reads=[('ysb', b), 'gst3'], writes=[('ysb', b)])
                for cc in range(8):
                    pi = 6 + cc // 4
                    P.op('pe', lambda e: e.transpose(out=ps[pi][:, (cc % 4) * 128:(cc % 4 + 1) * 128], in_=ysb[:, b, cc * 128:(cc + 1) * 128], identity=identf[:]), reads=[('ysb', b), 'identf'], writes=[psk[pi]])
                o3_ = ofm[:].rearrange("p (c t) -> p c t", t=64)
                for q in range(2):
                    P.op('dve', lambda e: e.tensor_tensor(out=o3_[:, q * 4:(q + 1) * 4, :], in0=ps[6 + q][:, :].rearrange("p (c t) -> p c t", t=128)[:, :, 0:64],
                                                           in1=lncol[:, 0, q * 4:(q + 1) * 4].unsqueeze(2).broadcast_to([128, 4, 64]), op=ALU.mult), reads=[psk[6 + q], 'lncol'], writes=['ofm'])
                P.op('pool', lambda e: e.tensor_tensor(out=o3_, in0=o3_, in1=lncol[:, 1, :].unsqueeze(2).broadcast_to([128, 8, 64]), op=ALU.add), reads=['ofm', 'lncol'], writes=['ofm'])
                P.op('pool', lambda e: e.tensor_tensor(out=o3_, in0=o3_, in1=gbt[:, b, :, 1, :], op=ALU.add), reads=['ofm', ('gbt', b)], writes=['ofm'])
                P.op('pool', lambda e: e.tensor_tensor(out=ofb[:, b], in0=o3_, in1=gbt[:, b, :, 0, :], op=ALU.mult), reads=['ofm', ('gbt', b)], writes=[('ofb', b)])
                P.dma('pool', yT_d[:, 0:8, c * 64:(c + 1) * 64], ofb[:, b], reads=[('ofb', b)], writes=['yT_d'])
            P.barrier()

    t0 = 0
    for L in seq_lens:
        phase_prep(t0, L)
        P.lastw['xres_src'] = None
        for li, (kind, j) in enumerate(layers):
            last = (li == nl - 1)
            x_src = xin[t0:t0 + L, :] if li == 0 else xres[(li - 1) % 2][0:L, :]
            x_dst = yout[t0:t0 + L, :] if last else xres[li % 2][0:L, :]
            if kind == 'm':
                phase_mamba(j, L)
                phase_out(li, t0, L, 16, wb['m_w_out'][j], x_src, x_dst, last)
            if kind == 'r':
                phase_rwkv(j, L)
                phase_out(li, t0, L, 8, wb['r_w_out'][j], x_src, x_dst, last)
            if kind == 'a':
                phase_attn(j, L)
                phase_out(li, t0, L, 8, wb['a_w_out'][j], x_src, x_dst, last)
        t0 += L
    P.barrier()
    return nc


def host_consts(Lmax):
    t = np.arange(Lmax)
    row = (t // 64).astype(np.float32)
    col = (t % 64).astype(np.float32)
    inv = (10000.0 ** (-np.arange(0, 64, 2, dtype=np.float32) / 64)).astype(np.float32)
    ar = row[:, None] * inv[None]
    ac = col[:, None] * inv[None]
    rope = np.stack([np.cos(ar), np.sin(ar), np.cos(ac), np.sin(ac)], axis=1).astype(np.float32)
    segmask = np.ones((64, Lmax), np.float32)
    segmask[:, ::128] = 0.0
    sel = np.zeros((64, 16, 4), np.float32)
    for d in range(2):
        for g in range(8):
            for hh in range(4):
                sel[d * 32 + g * 4 + hh, d * 8 + g, hh] = 1.0
    s_ = np.arange(128)[:, None]
    q_ = np.arange(128)[None, :]
    nf = np.where(q_ < s_, -30000.0, 0.0).astype(np.float32)
    nb_ = np.where(q_ > s_, -30000.0, 0.0).astype(np.float32)
    negm = np.stack([np.tile(nf, (1, 4)), np.tile(nb_, (1, 4))], axis=1).astype(np.float32)
    blk = np.zeros((128, 128), np.float32)
    blk[:64, :64] = 1.0
    blk[64:, 64:] = 1.0
    mask64 = np.ones((128, 1024), np.float32)
    mask64[:, ::64] = 0.0
    s2 = np.arange(64)[:, None]
    t2 = np.arange(64)[None, :]
    rmask = np.zeros((128, 2, 128), np.float32)
    for rk in range(2):
        rmask[rk * 64:(rk + 1) * 64, 0, 0:64] = (s2 < t2)
        rmask[rk * 64:(rk + 1) * 64, 0, 64:128] = (s2 <= t2)
        rmask[rk * 64:(rk + 1) * 64, 1, 0:64] = (s2 > t2)
        rmask[rk * 64:(rk + 1) * 64, 1, 64:128] = (s2 >= t2)
    rmaskn = np.zeros((64, 2, 64), np.float32)
    rmaskn[:, 0, :] = (t2 < s2)
    rmaskn[:, 1, :] = (t2 > s2)
    return {"ident_f": np.eye(128, dtype=np.float32), "rope_t": rope, "segmask": segmask, "sel_c": sel, "negm_c": negm,
            "blk_c": blk, "mask64_c": mask64, "rmask_c": rmask, "rmaskn_c": rmaskn}


def rwkv_host_layout(inp):
    nb = inp['r_mu'].shape[0]
    def t8(a):
        return np.ascontiguousarray(a.reshape(nb, 8, 128).transpose(0, 2, 1)).astype(np.float32)
    def t82(a):
        return np.ascontiguousarray(a.reshape(nb, 2, 8, 128).transpose(0, 3, 1, 2)).astype(np.float32)
    return {
        'r_w_in': inp['r_w_in'], 'r_w_out': inp['r_w_out'],
        'r_mu_t': np.ascontiguousarray(inp['r_mu'].reshape(nb, 34, 128).transpose(0, 2, 1)).astype(np.float32),
        'r_wup_t': np.ascontiguousarray(inp['r_w_up'].reshape(nb, 128, 1024)), 'r_aup_t': np.ascontiguousarray(inp['r_a_up'].reshape(nb, 128, 1024)),
        'r_w0_t': t82(inp['r_w0']), 'r_a0_t': t82(inp['r_a0']),
        'r_kk_t': t8(inp['r_k_k']), 'r_ka_t': t8(inp['r_k_a']), 'r_rk_t': t8(inp['r_r_k'].reshape(nb, 1024)),
        'r_lnw_t': t8(inp['r_ln_w']), 'r_lnb_t': t8(inp['r_ln_b']),
    }


def mamba_host_layout(inp):
    na = inp['m_conv_w'].shape[0]
    cw = np.ascontiguousarray(inp['m_conv_w'].reshape(na, 5, 32, 128).transpose(0, 3, 2, 1)).astype(np.float32)
    cb = np.ascontiguousarray(inp['m_conv_b'].reshape(na, 32, 128).transpose(0, 2, 1)).astype(np.float32)
    return {
        'm_w_in': inp['m_w_in'], 'm_w_out': inp['m_w_out'], 'm_convw_t': cw, 'm_convb_t': cb,
        'm_alog_c': np.ascontiguousarray(inp['m_a_log'].reshape(na, 64, 1)), 'm_dtb_c': np.ascontiguousarray(inp['m_dt_bias'].reshape(na, 64, 1)),
        'm_d_rep': np.ascontiguousarray(np.repeat(inp['m_d'], 64, axis=1)), 'm_norm_w': inp['m_norm_w'],
    }


SEQS = [4096, 4096, 2048]
LAYERS = [('m', 0), ('r', 0), ('a', 0), ('m', 1)]


def run(seq_lens, layers, xs, weights, n_cores=8, dbg=False):
    nc = build(seq_lens, layers, dbg)
    hc = host_consts(max(seq_lens))
    in_maps = []
    for c in range(n_cores):
        m = {"xin": np.ascontiguousarray(xs[c], dtype=np.float32)}
        m.update(hc)
        if not any(k == 'm' for k, _ in layers):
            for kk in ("segmask", "sel_c", "negm_c"):
                m.pop(kk, None)
        if not any(k == 'r' for k, _ in layers):
            for kk in ("blk_c", "mask64_c", "rmask_c", "rmaskn_c"):
                m.pop(kk, None)
        m.update(weights)
        in_maps.append(m)
    res = run_bass_kernel_spmd(nc, in_maps, core_ids=list(range(n_cores)))
    if dbg:
        return res.results
    return [r["yout"] for r in res.results]


def kernel(**inputs):
    inp = {k: np.asarray(v) for k, v in inputs.items()}
    layers = LAYERS
    xp, xs = inp['x_prompt'], inp['x_sample']
    xs_core = [np.concatenate([xp[2 * c], xp[2 * c + 1], xs[c]], axis=0) for c in range(8)]
    w = {'ln_g': inp['ln_g'], 'ln_b': inp['ln_b']}
    w.update(mamba_host_layout(inp))
    for k in ('a_w_in', 'a_q_norm', 'a_k_norm', 'a_w_out'):
        w[k] = inp[k]
    w.update(rwkv_host_layout(inp))
    ys = run(SEQS, layers, xs_core, w)
    y_prompt = np.stack([ys[c // 2][(c % 2) * 4096:(c % 2 + 1) * 4096] for c in range(16)], axis=0)
    y_sample = np.stack([ys[c][8192:10240] for c in range(8)], axis=0)
    return (y_prompt.astype(np.float32), y_sample.astype(np.float32))
```

```python
import math
from contextlib import ExitStack
import numpy as np
import ml_dtypes
import concourse.bass as bass
import concourse.mybir as mybir
from concourse.bass_utils import run_bass_kernel_spmd

F32 = mybir.dt.float32
F32R = mybir.dt.float32r
BF16 = mybir.dt.bfloat16
AF = mybir.ActivationFunctionType
ALU = mybir.AluOpType
AX = mybir.AxisListType

D = 1024
DEPTH = 4
LN_EPS = 1e-5
ALPHA = (2.0 * DEPTH) ** 0.25
A_HD = 128
A_H = 8
A_KV = 2
A_IN = 2560
QK_EPS = 1e-6
M_DI = 2048
M_H = 32
M_P = 64
M_N = 128
M_G = 8
M_CONVCH = 4096
M_IN = 6208
R_IN = 4352
R_H = 16
import os
RW_STAGE = int(os.environ.get('RW_STAGE', '9'))
RW_SUB = int(os.environ.get('RW_SUB', '9'))
RW_X = int(os.environ.get('RW_X', '0'))


class Prog:
    def __init__(self, nc):
        self.nc = nc
        self.engs = {'pe': nc.tensor, 'act': nc.scalar, 'dve': nc.vector, 'pool': nc.gpsimd, 'sp': nc.sync}
        self.sem = {k: nc.alloc_semaphore(name='s_' + k) for k in self.engs}
        self.cnt = {k: 0 for k in self.engs}
        self.waited = {k: {} for k in self.engs}
        self.NR = 6
        self.ring = {q: [nc.alloc_semaphore(name='d_%s%d' % (q, i)) for i in range(self.NR)] for q in ('sp', 'pool', 'act')}
        self.ring_cnt = {q: [0] * self.NR for q in self.ring}
        self.ring_i = {q: 0 for q in self.ring}
        self.lastw = {}
        self.readers = {}
        self.n_ins = 0

    def _wait(self, eng, tok):
        sem, sid, val, _ = tok
        w = self.waited[eng]
        if w.get(sid, 0) >= val:
            return
        w[sid] = val
        self.engs[eng].wait_ge(sem, val)

    def _deps(self, eng, reads, writes):
        toks = []
        for k in reads:
            t = self.lastw.get(k)
            if t is not None:
                toks.append(t)
        for k in writes:
            t = self.lastw.get(k)
            if t is not None and t[3] != eng:
                toks.append(t)
            for t in self.readers.get(k, ()):
                if t[3] != eng:
                    toks.append(t)
        for t in toks:
            self._wait(eng, t)

    def _update(self, tok, reads, writes):
        for k in writes:
            self.lastw[k] = tok
            self.readers[k] = []
        for k in reads:
            if k in writes:
                continue
            lst = self.readers.setdefault(k, [])
            if tok[3] != 'dma':
                lst[:] = [t for t in lst if t[3] != tok[3]]
            lst.append(tok)

    def op(self, eng, fn, reads=(), writes=()):
        self._deps(eng, reads, writes)
        ins = fn(self.engs[eng])
        self.cnt[eng] += 1
        ins.then_inc(self.sem[eng], 1)
        tok = (self.sem[eng], 'e_' + eng, self.cnt[eng], eng)
        self._update(tok, reads, writes)
        self.n_ins += 1
        return tok

    def dma(self, q, out, in_, reads=(), writes=()):
        self._deps(q, reads, writes)
        j = self.ring_i[q] % self.NR
        self.ring_i[q] += 1
        sem = self.ring[q][j]
        sid = 'r_%s%d' % (q, j)
        prev = 16 * self.ring_cnt[q][j]
        if prev > 0:
            self._wait(q, (sem, sid, prev, 'dma'))
        ins = self.engs[q].dma_start(out=out, in_=in_)
        ins.then_inc(sem, 16)
        self.ring_cnt[q][j] += 1
        tok = (sem, sid, 16 * self.ring_cnt[q][j], 'dma')
        self._update(tok, reads, writes)
        self.n_ins += 1
        return tok

    def barrier(self):
        for e in self.engs:
            for f in self.engs:
                if f != e and self.cnt[f] > 0:
                    self._wait(e, (self.sem[f], 'e_' + f, self.cnt[f], f))
            for q in self.ring:
                for j in range(self.NR):
                    if self.ring_cnt[q][j] > 0:
                        self._wait(e, (self.ring[q][j], 'r_%s%d' % (q, j), 16 * self.ring_cnt[q][j], 'dma'))
        self.lastw.clear()
        self.readers.clear()


class Ctx:
    pass


def build(seq_lens, layers, dbg=False):
    T = sum(seq_lens)
    Lmax = max(seq_lens)
    nc = bass.Bass("TRN2", target_bir_lowering=False)
    P = Prog(nc)
    c = Ctx()
    c.nc, c.P, c.T, c.Lmax = nc, P, T, Lmax

    def din(name, shape, dt=F32):
        return nc.dram_tensor(name, list(shape), dt, kind="ExternalInput").ap()

    def dscr(name, shape, dt):
        return nc.dram_tensor(name, list(shape), dt, kind=("ExternalOutput" if dbg else "Internal")).ap()

    n_m = sum(1 for k, _ in layers if k == 'm')
    n_r = sum(1 for k, _ in layers if k == 'r')
    n_a = sum(1 for k, _ in layers if k == 'a')
    nl = len(layers)
    xin = din("xin", [T, D])
    yout = nc.dram_tensor("yout", [T, D], F32, kind="ExternalOutput").ap()
    ln_g = din("ln_g", [nl, D])
    ln_b = din("ln_b", [nl, D])
    ident_f = din("ident_f", [128, 128])
    rope_t = din("rope_t", [Lmax, 4, 32])
    W = {}
    if n_a:
        W['a_w_in'] = din("a_w_in", [n_a, D, A_IN])
        W['a_q_norm'] = din("a_q_norm", [n_a, 128])
        W['a_k_norm'] = din("a_k_norm", [n_a, 128])
        W['a_w_out'] = din("a_w_out", [n_a, D, D])
    if n_m:
        W['m_w_in'] = din("m_w_in", [n_m, D, M_IN])
        W['m_w_out'] = din("m_w_out", [n_m, M_DI, D])
        W['m_convw_t'] = din("m_convw_t", [n_m, 128, 32, 5])
        W['m_convb_t'] = din("m_convb_t", [n_m, 128, 32])
        W['m_alog_c'] = din("m_alog_c", [n_m, 64, 1])
        W['m_dtb_c'] = din("m_dtb_c", [n_m, 64, 1])
        W['m_d_rep'] = din("m_d_rep", [n_m, 2048])
        W['m_norm_w'] = din("m_norm_w", [n_m, 2048])
        segmask = din("segmask", [64, Lmax])
        sel_c = din("sel_c", [64, 16, 4])
        negm_c = din("negm_c", [128, 2, 512])
    if n_r:
        W['r_w_in'] = din("r_w_in", [n_r, D, R_IN])
        W['r_w_out'] = din("r_w_out", [n_r, D, D])
        W['r_mu_t'] = din("r_mu_t", [n_r, 128, 34])
        W['r_wup_t'] = din("r_wup_t", [n_r, 128, 1024])
        W['r_aup_t'] = din("r_aup_t", [n_r, 128, 1024])
        W['r_w0_t'] = din("r_w0_t", [n_r, 128, 2, 8])
        W['r_a0_t'] = din("r_a0_t", [n_r, 128, 2, 8])
        for nm in ('r_kk_t', 'r_ka_t', 'r_rk_t', 'r_lnw_t', 'r_lnb_t'):
            W[nm] = din(nm, [n_r, 128, 8])
        blk_c = din("blk_c", [128, 128])
        mask64_c = din("mask64_c", [128, 1024])
        rmask_c = din("rmask_c", [128, 2, 128])
        rmaskn_c = din("rmaskn_c", [64, 2, 64])
    c.W = W

    xres = [dscr("xres%d" % i, [Lmax, D], F32) for i in range(2)]
    xT_d = dscr("xT_d", [128, 8, Lmax], BF16)
    yT_d = dscr("yT_d", [128, 16, Lmax], BF16)
    wb = {}
    if n_a:
        wb['a_w_in'] = dscr("wb_a_w_in", [n_a, 128, 8, A_IN], BF16)
        wb['a_w_out'] = dscr("wb_a_w_out", [n_a, 128, 8, D], BF16)
        qT_d = dscr("qT_d", [128, A_H, Lmax], BF16)
        kT_d = dscr("kT_d", [128, A_KV, Lmax], BF16)
        v_d = dscr("v_d", [Lmax, A_KV * 128], BF16)
        gT_d = dscr("gT_d", [128, 8, Lmax], BF16)

    if n_m:
        wb['m_w_in'] = dscr("wb_m_w_in", [n_m, 128, 8, M_IN], BF16)
        wb['m_w_out'] = dscr("wb_m_w_out", [n_m, 128, 16, D], BF16)
        xbcT_d = dscr("xbcT_d", [128, 32, Lmax], BF16)
        xtm_d = dscr("xtm_d", [Lmax, 2048], BF16)
        btm_d = dscr("btm_d", [Lmax, 1024], BF16)
        z_d = dscr("z_d", [Lmax, 2048], F32)
        cs_d = dscr("cs_d", [128, Lmax], F32)
        nbw_d = dscr("nbw_d", [Lmax, 128], F32)
        ecs_d = dscr("ecs_d", [Lmax, 64], F32)
        dec_d = dscr("dec_d", [128, Lmax // 128, 64], F32)
        hb_d = dscr("hb_d", [Lmax // 128, 128, 2048], BF16)

    if n_r:
        wb['r_w_in'] = dscr("wb_r_w_in", [n_r, 128, 8, R_IN], BF16)
        wb['r_w_out'] = dscr("wb_r_w_out", [n_r, 128, 8, D], BF16)
        u_d = dscr("u_d", [128, 34, Lmax], F32)
        rop_d = dscr("rop_d", [64, 16, 2, Lmax // 64, 4, 64], F32)
        gb_d = dscr("gb_d", [128, 8, 2, Lmax], F32)
        wc_d = dscr("wc_d", [64, 16, 2, Lmax // 64], F32)
        ybt_d = dscr("ybt_d", [Lmax, 1024], F32)

    ps = [nc.alloc_psum_tensor("ps%d" % i, [128, 512], F32) for i in range(8)]
    psk = ["ps%d" % i for i in range(8)]
    identf = nc.alloc_sbuf_tensor("identf", [128, 128], F32)
    identb = nc.alloc_sbuf_tensor("identb", [128, 128], BF16)
    onesb = nc.alloc_sbuf_tensor("onesb", [128, 128], BF16)
    onesf = nc.alloc_sbuf_tensor("onesf", [128, 128], F32)
    P.dma('sp', identf[:], ident_f, writes=['identf'])
    P.op('dve', lambda e: e.tensor_copy(out=identb[:], in_=identf[:]), reads=['identf'], writes=['identb'])
    P.op('dve', lambda e: e.memset(onesb[:], 1.0), writes=['onesb'])
    P.op('dve', lambda e: e.memset(onesf[:], 1.0), writes=['onesf'])

    rr = {'ev': 0, 'uid': 0}

    def SB(name, shape, dt):
        rr['uid'] += 1
        return nc.sbuf_tensor("%s_%d" % (name, rr['uid']), shape, dt)

    def evac_eng():
        rr['ev'] += 1
        return 'act' if rr['ev'] % 2 else 'dve'

    def copy(eng, out, in_, reads, writes):
        if eng == 'act':
            if out.dtype == F32R:
                return P.op('act', lambda e: e.activation(out=out, in_=in_, func=AF.Identity), reads, writes)
            return P.op('act', lambda e: e.copy(out=out, in_=in_), reads, writes)
        return P.op(eng, lambda e: e.tensor_copy(out=out, in_=in_), reads, writes)

    def cast_weight(src, dst, din_, F):
        with SB("cw_f", [128, 2, F], F32) as wf, SB("cw_b", [128, 2, F], BF16) as wbf:
            for dc in range(din_ // 128):
                b = dc % 2
                P.dma('sp', wf[:, b, :], src[dc * 128:(dc + 1) * 128, :], writes=[('cwf', b)])
                eng = ('dve', 'act', 'pool')[dc % 3]
                copy(eng, wbf[:, b, :], wf[:, b, :], [('cwf', b)], [('cwb', b)])
                P.dma('pool', dst[:, dc, :], wbf[:, b, :], reads=[('cwb', b)], writes=[('wbd', id(dst))])
            P.barrier()

    ai = 0
    for k, j in layers:
        if k == 'm':
            cast_weight(W['m_w_in'][j], wb['m_w_in'][j], D, M_IN)
            cast_weight(W['m_w_out'][j], wb['m_w_out'][j], M_DI, D)
        if k == 'r':
            cast_weight(W['r_w_in'][j], wb['r_w_in'][j], D, R_IN)
            cast_weight(W['r_w_out'][j], wb['r_w_out'][j], D, D)
        if k == 'a':
            cast_weight(W['a_w_in'][j], wb['a_w_in'][j], D, A_IN)
            cast_weight(W['a_w_out'][j], wb['a_w_out'][j], D, D)

    def transpose_to_xT(xt_tile_ap, xkey, tok0, tb):
        pass

    def phase_prep(t0, L):
        with SB("pp_x", [128, 2, D], F32) as xt, SB("pp_t", [128, 2, 8, 128], BF16) as tt:
            for it in range(L // 128):
                b = it % 2
                P.dma('sp', xt[:, b, :], xin[t0 + it * 128:t0 + (it + 1) * 128, :], writes=[('ppx', b)])
                for hf in range(2):
                    pk = psk[(it * 2 + hf) % 8]
                    pt = ps[(it * 2 + hf) % 8]
                    for i in range(4):
                        dc = hf * 4 + i
                        P.op('pe', lambda e, dc=dc, i=i, pt=pt: e.transpose(out=pt[:, i * 128:(i + 1) * 128], in_=xt[:, b, dc * 128:(dc + 1) * 128], identity=identf[:]),
                             reads=[('ppx', b), 'identf'], writes=[pk])
                    copy(evac_eng(), tt[:, b, hf * 4:(hf + 1) * 4, :], pt[:, :].rearrange("p (c t) -> p c t", c=4), [pk], [('ppt', b, hf)])
                P.dma('pool', xT_d[:, :, it * 128:(it + 1) * 128], tt[:, b, :, :], reads=[('ppt', b, 0), ('ppt', b, 1)], writes=['xT_d'])
            P.barrier()

    def phase_out(li, t0, L, cin_chunks, wout_b, x_src, x_dst, last):
        CC = cin_chunks
        with SB("po_w", [128, CC, D], BF16) as wo, \
                SB("po_g", [128, D], F32) as gbc, SB("po_b", [128, D], F32) as bbc, \
                SB("po_y", [128, 2, CC, 128], BF16) as yt, \
                SB("po_x", [128, 2, D], F32) as xr, \
                SB("po_z", [128, 2, D], F32) as zt, \
                SB("po_st", [128, 2, 16], F32) as st, \
                SB("po_t", [128, 2, 8, 128], BF16) as tt:
            P.dma('sp', wo[:], wout_b, writes=['po_w'])
            P.dma('sp', gbc[:], ln_g[li].unsqueeze(0).broadcast_to([128, D]), writes=['po_g'])
            P.dma('sp', bbc[:], ln_b[li].unsqueeze(0).broadcast_to([128, D]), writes=['po_b'])
            for it in range(L // 128):
                b = it % 2
                P.dma('sp', yt[:, b, :, :], yT_d[:, 0:CC, it * 128:(it + 1) * 128], reads=['yT_d'], writes=[('poy', b)])
                P.dma('sp', xr[:, b, :], x_src[it * 128:(it + 1) * 128, :], reads=['xres_src'], writes=[('pox', b)])
                pb = [(it * 4 + n) % 8 for n in range(4)]
                for nb in range(2):
                    for cc in range(CC):
                        P.op('pe', lambda e, nb=nb, cc=cc: e.matmul(ps[pb[nb]][:, :], yt[:, b, cc, :], wo[:, cc, nb * 512:(nb + 1) * 512], start=(cc == 0), stop=(cc == CC - 1)),
                             reads=[('poy', b), 'po_w'], writes=[psk[pb[nb]]])
                for nb in range(2):
                    P.op('dve', lambda e, nb=nb: e.scalar_tensor_tensor(out=zt[:, b, nb * 512:(nb + 1) * 512], in0=xr[:, b, nb * 512:(nb + 1) * 512], scalar=ALPHA,
                                                                          in1=ps[pb[nb]][:, :], op0=ALU.mult, op1=ALU.add),
                         reads=[('pox', b), psk[pb[nb]]], writes=[('poz', b, nb)])
                    P.op('dve', lambda e, nb=nb: e.bn_stats(out=st[:, b, nb * 6:(nb + 1) * 6], in_=zt[:, b, nb * 512:(nb + 1) * 512]),
                         reads=[('poz', b, nb)], writes=[('post', b, nb)])
                P.op('dve', lambda e: e.bn_aggr(out=st[:, b, 12:14], in_=st[:, b, 0:12]), reads=[('post', b, 0), ('post', b, 1)], writes=[('pomv', b)])
                P.op('dve', lambda e: e.tensor_scalar(out=st[:, b, 14:15], in0=st[:, b, 13:14], scalar1=LN_EPS, scalar2=None, op0=ALU.add), reads=[('pomv', b)], writes=[('pors', b)])
                P.op('act', lambda e: e.activation(out=st[:, b, 14:15], in_=st[:, b, 14:15], func=AF.Sqrt), reads=[('pors', b)], writes=[('pors', b)])
                P.op('dve', lambda e: e.reciprocal(out=st[:, b, 15:16], in_=st[:, b, 14:15]), reads=[('pors', b)], writes=[('pors2', b)])
                P.op('dve', lambda e: e.tensor_scalar(out=zt[:, b, :], in0=zt[:, b, :], scalar1=st[:, b, 12:13], scalar2=st[:, b, 15:16], op0=ALU.subtract, op1=ALU.mult),
                     reads=[('poz', b, 0), ('poz', b, 1), ('pomv', b), ('pors2', b)], writes=[('poz', b, 0), ('poz', b, 1)])
                P.op('pool', lambda e: e.tensor_tensor(out=zt[:, b, :], in0=zt[:, b, :], in1=gbc[:], op=ALU.mult), reads=[('poz', b, 0), ('poz', b, 1), 'po_g'], writes=[('poz', b, 0), ('poz', b, 1)])
                P.op('pool', lambda e: e.tensor_tensor(out=zt[:, b, :], in0=zt[:, b, :], in1=bbc[:], op=ALU.add), reads=[('poz', b, 0), ('poz', b, 1), 'po_b'], writes=[('poz', b, 0), ('poz', b, 1)])
                P.dma('pool', x_dst[it * 128:(it + 1) * 128, :], zt[:, b, :], reads=[('poz', b, 0), ('poz', b, 1)], writes=['xres_dst'])
                if not last:
                    for hf in range(2):
                        pi = pb[2 + hf]
                        for i in range(4):
                            dc = hf * 4 + i
                            P.op('pe', lambda e, dc=dc, i=i, pi=pi: e.transpose(out=ps[pi][:, i * 128:(i + 1) * 128], in_=zt[:, b, dc * 128:(dc + 1) * 128], identity=identf[:]),
                                 reads=[('poz', b, 0), ('poz', b, 1), 'identf'], writes=[psk[pi]])
                        copy(evac_eng(), tt[:, b, hf * 4:(hf + 1) * 4, :], ps[pi][:, :].rearrange("p (c t) -> p c t", c=4), [psk[pi]], [('pot', b, hf)])
                    P.dma('pool', xT_d[:, :, it * 128:(it + 1) * 128], tt[:, b, :, :], reads=[('pot', b, 0), ('pot', b, 1)], writes=['xT_d'])
            P.barrier()

    def phase_attn(j, L):
        NT = L // 128
        scale = A_HD ** -0.5
        win = wb['a_w_in'][j]
        with SB("a1_xT", [128, 8, L], BF16) as xT, \
                SB("a1_w", [128, 8, 1536], BF16) as wq, \
                SB("a1_wg", [128, 8, 1024], BF16) as wg, \
                SB("a1_gq", [128, 2, 128], F32) as gqk, \
                SB("a1_rope", [128, 2, 4, 32], F32) as rp, \
                SB("a1_q", [128, 2, 1280], F32) as qf, \
                SB("a1_sq", [128, 1280], F32) as sq, \
                SB("a1_ss", [128, 2, 16], F32) as ss, \
                SB("a1_qr", [128, 2, 1280], BF16) as qr, \
                SB("a1_v", [128, 2, 256], BF16) as vb, \
                SB("a1_qT", [128, 2, 10, 128], BF16) as qTt, \
                SB("a1_g", [128, 2, 512], BF16) as gt:
            P.dma('sp', xT[:], xT_d[:, :, 0:L], reads=['xT_d'], writes=['a1_xT'])
            P.dma('sp', wq[:], win[:, :, 0:1536], writes=['a1_w'])
            P.dma('sp', wg[:], win[:, :, 1536:2560], writes=['a1_wg'])
            P.dma('sp', gqk[:, 0, :], W['a_q_norm'][j].unsqueeze(0).broadcast_to([128, 128]), writes=['a1_gq'])
            P.dma('sp', gqk[:, 1, :], W['a_k_norm'][j].unsqueeze(0).broadcast_to([128, 128]), writes=['a1_gq'])
            for it in range(NT):
                b = it % 2
                P.dma('sp', rp[:, b, :, :], rope_t[it * 128:(it + 1) * 128, :, :], writes=[('a1rp', b)])
                pb = [(it * 3 + n) % 6 for n in range(3)]
                for n in range(3):
                    for dc in range(8):
                        P.op('pe', lambda e, n=n, dc=dc: e.matmul(ps[pb[n]][:, :], xT[:, dc, it * 128:(it + 1) * 128], wq[:, dc, n * 512:(n + 1) * 512], start=(dc == 0), stop=(dc == 7)),
                             reads=['a1_xT', 'a1_w'], writes=[psk[pb[n]]])
                P.op('act', lambda e: e.copy(out=qf[:, b, 0:512], in_=ps[pb[0]][:, :]), reads=[psk[pb[0]]], writes=[('a1q', b)])
                P.op('act', lambda e: e.copy(out=qf[:, b, 512:1024], in_=ps[pb[1]][:, :]), reads=[psk[pb[1]]], writes=[('a1q', b)])
                P.op('act', lambda e: e.copy(out=qf[:, b, 1024:1280], in_=ps[pb[2]][:, 0:256]), reads=[psk[pb[2]]], writes=[('a1q', b)])
                P.op('act', lambda e: e.copy(out=vb[:, b, :], in_=ps[pb[2]][:, 256:512]), reads=[psk[pb[2]]], writes=[('a1v', b)])
                P.dma('pool', v_d[it * 128:(it + 1) * 128, :], vb[:, b, :], reads=[('a1v', b)], writes=['v_d'])
                P.op('dve', lambda e: e.tensor_tensor(out=sq[:], in0=qf[:, b, :], in1=qf[:, b, :], op=ALU.mult), reads=[('a1q', b)], writes=['a1sq'])
                P.op('dve', lambda e: e.tensor_reduce(out=ss[:, b, 0:10], in_=sq[:].rearrange("p (h d) -> p h d", h=10), op=ALU.add, axis=AX.X), reads=['a1sq'], writes=[('a1ss', b)])
                P.op('dve', lambda e: e.tensor_scalar(out=ss[:, b, 0:10], in0=ss[:, b, 0:10], scalar1=1.0 / 128, scalar2=QK_EPS, op0=ALU.mult, op1=ALU.add), reads=[('a1ss', b)], writes=[('a1ss', b)])
                P.op('act', lambda e: e.activation(out=ss[:, b, 0:10], in_=ss[:, b, 0:10], func=AF.Sqrt), reads=[('a1ss', b)], writes=[('a1ss', b)])
                P.op('dve', lambda e: e.reciprocal(out=ss[:, b, 0:10], in_=ss[:, b, 0:10]), reads=[('a1ss', b)], writes=[('a1ss', b)])
                q3 = qf[:, b, :].rearrange("p (h d) -> p h d", h=10)
                P.op('dve', lambda e: e.tensor_tensor(out=q3, in0=q3, in1=ss[:, b, 0:10].unsqueeze(2).broadcast_to([128, 10, 128]), op=ALU.mult), reads=[('a1q', b), ('a1ss', b)], writes=[('a1q', b)])
                P.op('pool', lambda e: e.tensor_tensor(out=q3[:, 0:8, :], in0=q3[:, 0:8, :], in1=gqk[:, 0:1, :].broadcast_to([128, 8, 128]), op=ALU.mult), reads=[('a1q', b), 'a1_gq'], writes=[('a1q', b)])
                P.op('pool', lambda e: e.tensor_tensor(out=q3[:, 8:10, :], in0=q3[:, 8:10, :], in1=gqk[:, 1:2, :].broadcast_to([128, 2, 128]), op=ALU.mult), reads=[('a1q', b), 'a1_gq'], writes=[('a1q', b)])
                q5 = qf[:, b, :].rearrange("p (h a f) -> p h a f", h=10, a=2)
                o5 = qr[:, b, :].rearrange("p (h a f) -> p h a f", h=10, a=2)
                s5 = sq[:].rearrange("p (h a f) -> p h a f", h=10, a=2)
                cosb = rp[:, b, 0:4:2, :]
                sinb = rp[:, b, 1:4:2, :]
                cb4 = cosb.unsqueeze(1).broadcast_to([128, 10, 2, 32])
                sb4 = sinb.unsqueeze(1).broadcast_to([128, 10, 2, 32])
                P.op('dve', lambda e: e.tensor_tensor(out=s5[:, :, :, 0:32], in0=q5[:, :, :, 0:32], in1=cb4, op=ALU.mult),
                     reads=[('a1q', b), ('a1rp', b)], writes=['a1sq'])
                P.op('pool', lambda e: e.tensor_tensor(out=s5[:, :, :, 32:64], in0=q5[:, :, :, 32:64], in1=cb4, op=ALU.mult),
                     reads=[('a1q', b), ('a1rp', b)], writes=['a1sq'])
                P.op('dve', lambda e: e.tensor_tensor(out=q5[:, :, :, 0:32], in0=q5[:, :, :, 0:32], in1=sb4, op=ALU.mult),
                     reads=[('a1q', b), ('a1rp', b), 'a1sq'], writes=[('a1q', b)])
                P.op('pool', lambda e: e.tensor_tensor(out=q5[:, :, :, 32:64], in0=q5[:, :, :, 32:64], in1=sb4, op=ALU.mult),
                     reads=[('a1q', b), ('a1rp', b), 'a1sq'], writes=[('a1q', b)])
                P.op('dve', lambda e: e.tensor_tensor(out=o5[:, :, :, 0:32], in0=s5[:, :, :, 0:32], in1=q5[:, :, :, 32:64], op=ALU.subtract),
                     reads=[('a1q', b), 'a1sq'], writes=[('a1qr', b)])
                P.op('dve', lambda e: e.tensor_tensor(out=o5[:, :, :, 32:64], in0=s5[:, :, :, 32:64], in1=q5[:, :, :, 0:32], op=ALU.add),
                     reads=[('a1q', b), 'a1sq'], writes=[('a1qr', b)])
                for h in range(10):
                    pi = 6 + (h // 8)
                    pv = ps[pi][:, :].bitcast(BF16)
                    P.op('pe', lambda e, h=h, pv=pv: e.transpose(out=pv[:, (h % 8) * 128:(h % 8 + 1) * 128], in_=qr[:, b, h * 128:(h + 1) * 128], identity=identb[:]),
                         reads=[('a1qr', b), 'identb'], writes=[psk[pi]])
                P.op('act', lambda e: e.copy(out=qTt[:, b, 0:8, :], in_=ps[6][:, :].bitcast(BF16).rearrange("p (h t) -> p h t", h=8)), reads=[psk[6]], writes=[('a1qT', b)])
                P.op('dve', lambda e: e.tensor_copy(out=qTt[:, b, 8:10, :], in_=ps[7][:, :].bitcast(BF16)[:, 0:256].rearrange("p (h t) -> p h t", h=2)), reads=[psk[7]], writes=[('a1kT', b)])
                P.dma('pool', qT_d[:, :, it * 128:(it + 1) * 128], qTt[:, b, 0:8, :], reads=[('a1qT', b)], writes=['qT_d'])
                P.dma('pool', kT_d[:, :, it * 128:(it + 1) * 128], qTt[:, b, 8:10, :], reads=[('a1kT', b)], writes=['kT_d'])
            k = 0
            for fc in range(8):
                for tbk in range(L // 512):
                    b = k % 2
                    pi = k % 6
                    k += 1
                    for dc in range(8):
                        P.op('pe', lambda e, dc=dc, pi=pi: e.matmul(ps[pi][:, :], wg[:, dc, fc * 128:(fc + 1) * 128], xT[:, dc, tbk * 512:(tbk + 1) * 512], start=(dc == 0), stop=(dc == 7)),
                             reads=['a1_xT', 'a1_wg'], writes=[psk[pi]])
                    P.op('act', lambda e, pi=pi: e.activation(out=gt[:, b, :], in_=ps[pi][:, :], func=AF.Silu), reads=[psk[pi]], writes=[('a1g', b)])
                    P.dma('pool', gT_d[:, fc, tbk * 512:(tbk + 1) * 512], gt[:, b, :], reads=[('a1g', b)], writes=['gT_d'])
            P.barrier()
        with SB("a2_kT", [128, A_KV, L], BF16) as kT, \
                SB("a2_v", [128, NT, 256], BF16) as V, \
                SB("a2_q", [128, 2, 512], BF16) as qb, \
                SB("a2_g", [128, 2, 512], BF16) as gb, \
                SB("a2_p", [128, 3, 512], BF16) as pT, \
                SB("a2_r", [128, 2, 512], F32) as rc, \
                SB("a2_o", [128, 2, 512], BF16) as ob, \
                SB("a2_m", [128, 8], F32) as mm, SB("a2_racc", [128, 2, 2, 512], F32) as racc:
            P.dma('sp', kT[:], kT_d[:, :, 0:L], reads=['kT_d'], writes=['a2_kT'])
            P.dma('sp', V[:], v_d[0:L, :].rearrange("(n p) c -> p n c", p=128), reads=['v_d'], writes=['a2_v'])
            P.dma('sp', rc[:, 0, 0:128], W['a_q_norm'][j].unsqueeze(0).broadcast_to([128, 128]), writes=[('a2r', 0)])
            P.dma('sp', rc[:, 0, 128:256], W['a_k_norm'][j].unsqueeze(0).broadcast_to([128, 128]), writes=[('a2r', 0)])
            P.op('dve', lambda e: e.tensor_reduce(out=mm[:, 0:2], in_=rc[:, 0, 0:256].rearrange("p (a d) -> p a d", a=2), op=ALU.max, axis=AX.X, apply_absolute_value=True),
                 reads=[('a2r', 0)], writes=['a2m'])
            P.op('dve', lambda e: e.tensor_tensor(out=mm[:, 2:3], in0=mm[:, 0:1], in1=mm[:, 1:2], op=ALU.mult), reads=['a2m'], writes=['a2m2'])
            P.op('dve', lambda e: e.tensor_scalar(out=mm[:, 3:4], in0=mm[:, 2:3], scalar1=-math.sqrt(128.0), scalar2=None, op0=ALU.mult), reads=['a2m2'], writes=['a2m3'])
            k = 0
            for h in range(A_H):
                kv = h // (A_H // A_KV)
                for qbk in range(L // 512):
                    b = k % 2
                    k += 1
                    P.dma('sp', qb[:, b, :], qT_d[:, h, qbk * 512:(qbk + 1) * 512], reads=['qT_d'], writes=[('a2q', b)])
                    P.dma('sp', gb[:, b, :], gT_d[:, h, qbk * 512:(qbk + 1) * 512], reads=['gT_d'], writes=[('a2g', b)])
                    po = 4 + 2 * b
                    pr = 5 + 2 * b

                    def qk(st):
                        pi = st % 4
                        P.op('pe', lambda e: e.matmul(ps[pi][:, :], kT[:, kv, st * 128:(st + 1) * 128], qb[:, b, :], start=True, stop=True),
                             reads=['a2_kT', ('a2q', b)], writes=[psk[pi]])
                    qk(0)
                    for st in range(NT):
                        if st + 1 < NT:
                            qk(st + 1)
                        pi = st % 4
                        pb3 = st % 3
                        P.op('act', lambda e: e.activation(out=pT[:, pb3, :], in_=ps[pi][:, :], func=AF.Exp, scale=scale, bias=mm[:, 3:4]),
                             reads=[psk[pi], 'a2m3'], writes=[('a2p', pb3)])
                        P.op('pe', lambda e: e.matmul(ps[po][:, :], V[:, st, kv * 128:(kv + 1) * 128], pT[:, pb3, :], start=(st == 0), stop=(st == NT - 1)),
                             reads=['a2_v', ('a2p', pb3)], writes=[psk[po]])
                        par = st % 2
                        reng = 'dve' if par == 0 else 'pool'
                        if st < 2:
                            P.op(reng, lambda e: e.tensor_copy(out=racc[:, b, par, :], in_=pT[:, pb3, :]), reads=[('a2p', pb3)], writes=[('racc', b, par)])
                        else:
                            P.op(reng, lambda e: e.tensor_tensor(out=racc[:, b, par, :], in0=racc[:, b, par, :], in1=pT[:, pb3, :], op=ALU.add), reads=[('a2p', pb3), ('racc', b, par)], writes=[('racc', b, par)])
                    P.op('pe', lambda e: e.matmul(ps[pr][:, :], onesf[:], racc[:, b, 0, :], start=True, stop=False), reads=['onesf', ('racc', b, 0)], writes=[psk[pr]])
                    P.op('pe', lambda e: e.matmul(ps[pr][:, :], onesf[:], racc[:, b, 1, :], start=False, stop=True), reads=['onesf', ('racc', b, 1)], writes=[psk[pr]])
                    P.op('dve', lambda e: e.reciprocal(out=rc[:, b, :], in_=ps[pr][:, :]), reads=[psk[pr]], writes=[('a2r', b)])
                    P.op('dve', lambda e: e.tensor_tensor(out=rc[:, b, :], in0=ps[po][:, :], in1=rc[:, b, :], op=ALU.mult), reads=[psk[po], ('a2r', b)], writes=[('a2r', b)])
                    P.op('pool', lambda e: e.tensor_tensor(out=ob[:, b, :], in0=rc[:, b, :], in1=gb[:, b, :], op=ALU.mult), reads=[('a2r', b), ('a2g', b)], writes=[('a2o', b)])
                    P.dma('pool', yT_d[:, h, qbk * 512:(qbk + 1) * 512], ob[:, b, :], reads=[('a2o', b)], writes=['yT_d'])
            P.barrier()

    def phase_mamba(j, L):
        NT = L // 128
        NB = L // 512
        win = wb['m_w_in'][j]
        with SB("m1_xT", [128, 8, L], BF16) as xT, SB("m1_w", [128, 2, 8, 128], BF16) as wsl, \
                SB("m1_cw", [128, 32, 5], F32) as cw, SB("m1_cb", [128, 32], F32) as cb, \
                SB("m1_rb", [128, L + 4], F32) as rb, SB("m1_acc", [128, L], F32) as acc, \
                SB("m1_ob", [128, 2, L], BF16) as ob, SB("m1_tt", [128, 2, 8, 128], BF16) as tt:
            P.dma('sp', xT[:], xT_d[:, :, 0:L], writes=['m1_xT'])
            P.dma('sp', cw[:], W['m_convw_t'][j], writes=['m1_cw'])
            P.dma('sp', cb[:], W['m_convb_t'][j], writes=['m1_cw'])
            P.op('dve', lambda e: e.memset(rb[:, 0:2], 0.0), writes=['m1_rb'])
            P.op('dve', lambda e: e.memset(rb[:, L + 2:L + 4], 0.0), writes=['m1_rb'])
            k = 0
            kt = 0
            gsz = min(8, NT)
            for fc in range(32):
                wbuf = fc % 2
                P.dma('sp', wsl[:, wbuf], win[:, :, 2048 + fc * 128:2048 + (fc + 1) * 128], writes=[('m1w', wbuf)])
                for tb in range(NB):
                    pi = k % 4
                    k += 1
                    for dc in range(8):
                        P.op('pe', lambda e: e.matmul(ps[pi][:, :], wsl[:, wbuf, dc, :], xT[:, dc, tb * 512:(tb + 1) * 512], start=(dc == 0), stop=(dc == 7)),
                             reads=['m1_xT', ('m1w', wbuf)], writes=[psk[pi]])
                    P.op('act', lambda e: e.copy(out=rb[:, 2 + tb * 512:2 + (tb + 1) * 512], in_=ps[pi][:, :]), reads=[psk[pi]], writes=['m1_rb'])
                P.op('dve', lambda e: e.tensor_scalar(out=acc[:], in0=rb[:, 0:L], scalar1=cw[:, fc, 0:1], scalar2=cb[:, fc:fc + 1], op0=ALU.mult, op1=ALU.add),
                     reads=['m1_rb', 'm1_cw'], writes=['m1_acc'])
                for kk in range(1, 5):
                    P.op('dve', lambda e: e.scalar_tensor_tensor(out=acc[:], in0=rb[:, kk:kk + L], scalar=cw[:, fc, kk:kk + 1], in1=acc[:], op0=ALU.mult, op1=ALU.add),
                         reads=['m1_rb', 'm1_cw', 'm1_acc'], writes=['m1_acc'])
                obb = fc % 2
                P.op('act', lambda e: e.activation(out=ob[:, obb, :], in_=acc[:], func=AF.Silu), reads=['m1_acc'], writes=[('m1ob', obb)])
                P.dma('pool', xbcT_d[:, fc, 0:L], ob[:, obb, :], reads=[('m1ob', obb)], writes=['xbcT_d'])
                if fc < 24:
                    for t8 in range(NT // gsz):
                        pi = 4 + (kt % 4)
                        tb2 = kt % 2
                        kt += 1
                        pv = ps[pi][:, :].bitcast(BF16)
                        for i in range(gsz):
                            tk = t8 * gsz + i
                            P.op('pe', lambda e: e.transpose(out=pv[:, i * 128:(i + 1) * 128], in_=ob[:, obb, tk * 128:(tk + 1) * 128], identity=identb[:]),
                                 reads=[('m1ob', obb), 'identb'], writes=[psk[pi]])
                        copy(evac_eng(), tt[:, tb2, 0:gsz, :], pv[:, 0:gsz * 128].rearrange("p (n c) -> p n c", n=gsz), [psk[pi]], [('m1tt', tb2)])
                        if fc < 16:
                            dst = xtm_d[t8 * gsz * 128:(t8 + 1) * gsz * 128, fc * 128:(fc + 1) * 128]
                        else:
                            dst = btm_d[t8 * gsz * 128:(t8 + 1) * gsz * 128, (fc - 16) * 128:(fc - 15) * 128]
                        P.dma('pool', dst.rearrange("(n p) c -> p n c", p=128), tt[:, tb2, 0:gsz, :], reads=[('m1tt', tb2)], writes=['xtm_d'])
            P.barrier()
        with SB("m1b_xT", [128, 8, L], BF16) as xT, SB("m1b_wz", [128, 8, 2048], BF16) as wz, SB("m1b_z", [128, 2, 2048], F32) as zt:
            P.dma('sp', xT[:], xT_d[:, :, 0:L], writes=['m1_xT'])
            P.dma('sp', wz[:], win[:, :, 0:2048], writes=['m1_wz'])
            for it in range(NT):
                b = it % 2
                for n in range(4):
                    pi = (it * 4 + n) % 8
                    for dc in range(8):
                        P.op('pe', lambda e: e.matmul(ps[pi][:, :], xT[:, dc, it * 128:(it + 1) * 128], wz[:, dc, n * 512:(n + 1) * 512], start=(dc == 0), stop=(dc == 7)),
                             reads=['m1_xT', 'm1_wz'], writes=[psk[pi]])
                    P.op('act', lambda e: e.activation(out=zt[:, b, n * 512:(n + 1) * 512], in_=ps[pi][:, :], func=AF.Silu), reads=[psk[pi]], writes=[('m1z', b)])
                P.dma('pool', z_d[it * 128:(it + 1) * 128, :], zt[:, b, :], reads=[('m1z', b)], writes=['z_d'])
            P.barrier()
        with SB("m1c_xT", [128, 8, L], BF16) as xT, SB("m1c_wdt", [128, 8, 64], BF16) as wdt, \
                SB("m1c_x", [64, L], F32) as xr, SB("m1c_t1", [64, L], F32) as t1, SB("m1c_t2", [64, L], F32) as t2, \
                SB("m1c_cs", [64, L], F32) as cs, SB("m1c_vv", [64, L], F32) as vv, SB("m1c_msk", [64, L], F32) as msk, \
                SB("m1c_col", [64, 8], F32) as col, SB("m1c_dcol", [64, NT], F32) as dcol, SB("m1c_xd", [64, NT, 64], F32) as xd, \
                SB("m1c_st", [128, 2, 512], F32) as stg:
            P.dma('sp', xT[:], xT_d[:, :, 0:L], writes=['m1_xT'])
            P.dma('sp', wdt[:], win[:, :, 6144:6208], writes=['m1_wdt'])
            P.dma('sp', msk[:], segmask[:, 0:L], writes=['msk'])
            P.dma('sp', col[:, 0:1], W['m_alog_c'][j], writes=['col0'])
            P.dma('sp', col[:, 2:3], W['m_dtb_c'][j], writes=['col2'])
            P.op('act', lambda e: e.activation(out=col[:, 1:2], in_=col[:, 0:1], func=AF.Exp), reads=['col0'], writes=['col1'])
            P.op('dve', lambda e: e.tensor_scalar(out=col[:, 3:4], in0=col[:, 1:2], scalar1=-1.0, scalar2=None, op0=ALU.mult), reads=['col1'], writes=['col3'])
            for tb in range(NB):
                pi = tb % 4
                for dc in range(8):
                    P.op('pe', lambda e: e.matmul(ps[pi][0:64, :], wdt[:, dc, :], xT[:, dc, tb * 512:(tb + 1) * 512], start=(dc == 0), stop=(dc == 7)),
                         reads=['m1_xT', 'm1_wdt'], writes=[psk[pi]])
                P.op('dve', lambda e: e.tensor_scalar(out=xr[:, tb * 512:(tb + 1) * 512], in0=ps[pi][0:64, :], scalar1=col[:, 2:3], scalar2=None, op0=ALU.add),
                     reads=[psk[pi], 'col2'], writes=['xr'])
            P.op('act', lambda e: e.activation(out=t1[:], in_=xr[:], func=AF.Abs), reads=['xr'], writes=['t1'])
            P.op('act', lambda e: e.activation(out=t1[:], in_=t1[:], func=AF.Exp, scale=-1.0), reads=['t1'], writes=['t1'])
            P.op('act', lambda e: e.activation(out=t1[:], in_=t1[:], func=AF.Ln, bias=1.0), reads=['t1'], writes=['t1'])
            P.op('dve', lambda e: e.tensor_scalar(out=xr[:], in0=xr[:], scalar1=0.0, scalar2=None, op0=ALU.max), reads=['xr'], writes=['xr'])
            P.op('dve', lambda e: e.tensor_tensor(out=xr[:], in0=xr[:], in1=t1[:], op=ALU.add), reads=['xr', 't1'], writes=['xr'])
            P.op('dve', lambda e: e.tensor_scalar(out=vv[:], in0=xr[:], scalar1=col[:, 3:4], scalar2=None, op0=ALU.mult), reads=['xr', 'col3'], writes=['vv'])
            P.op('dve', lambda e: e.tensor_tensor_scan(out=cs[:], data0=msk[:], data1=vv[:], initial=0.0, op0=ALU.mult, op1=ALU.add), reads=['msk', 'vv'], writes=['cs'])
            cs3 = cs[:].rearrange("p (c t) -> p c t", t=128)
            t13 = t1[:].rearrange("p (c t) -> p c t", t=128)
            t23 = t2[:].rearrange("p (c t) -> p c t", t=128)
            P.op('dve', lambda e: e.tensor_tensor(out=t13[32:64], in0=cs3[32:64, :, 127:128].broadcast_to([32, NT, 128]), in1=cs3[32:64], op=ALU.subtract), reads=['cs', 't1'], writes=['t1'])
            P.op('dve', lambda e: e.tensor_tensor(out=cs[32:64, :], in0=t1[32:64, :], in1=vv[32:64, :], op=ALU.add), reads=['t1', 'vv', 'cs'], writes=['cs'])
            P.op('dve', lambda e: e.tensor_copy(out=msk[:].bitcast(F32R), in_=cs[:]), reads=['cs', 'msk'], writes=['msk'])
            P.op('dve', lambda e: e.tensor_tensor(out=vv[:], in0=cs[:], in1=msk[:], op=ALU.subtract), reads=['cs', 'msk', 'vv'], writes=['vv'])
            P.dma('pool', cs_d[0:64, 0:L], msk[:], reads=['msk'], writes=['cs_d'])
            P.dma('pool', cs_d[64:128, 0:L], vv[:], reads=['vv'], writes=['cs_d'])
            P.op('act', lambda e: e.activation(out=t1[:], in_=xr[:], func=AF.Ln), reads=['xr', 't1'], writes=['t1'])
            P.op('dve', lambda e: e.tensor_tensor(out=t1[:], in0=t1[:], in1=cs[:], op=ALU.subtract), reads=['t1', 'cs'], writes=['t1'])
            P.op('dve', lambda e: e.tensor_tensor(out=t23[0:32], in0=cs3[0:32, :, 127:128].broadcast_to([32, NT, 128]), in1=cs3[0:32], op=ALU.subtract), reads=['cs'], writes=['t2'])
            P.op('dve', lambda e: e.tensor_tensor(out=t23[32:64], in0=cs3[32:64, :, 0:1].broadcast_to([32, NT, 128]), in1=cs3[32:64], op=ALU.subtract), reads=['cs'], writes=['t2'])
            P.op('act', lambda e: e.activation(out=t2[:], in_=t2[:], func=AF.Exp), reads=['t2'], writes=['t2'])
            P.op('dve', lambda e: e.tensor_tensor(out=t2[:], in0=t2[:], in1=xr[:], op=ALU.mult), reads=['t2', 'xr'], writes=['t2'])
            P.op('act', lambda e: e.activation(out=xr[:], in_=cs[:], func=AF.Exp), reads=['cs', 'xr', 't2'], writes=['xr'])
            c8e = min(8, NT)
            for c8 in range(NT // c8e):
                pi = 4 + c8 % 2
                bq = c8 % 2
                for i in range(c8e):
                    cc = c8 * c8e + i
                    P.op('pe', lambda e: e.transpose(out=ps[pi][:, i * 64:(i + 1) * 64], in_=xr[:, cc * 128:(cc + 1) * 128], identity=identf[0:64, 0:64]), reads=['xr', 'identf'], writes=[psk[pi]])
                copy(evac_eng(), stg[:, bq, 0:c8e * 64], ps[pi][:, 0:c8e * 64], [psk[pi]], [('stg', bq)])
                P.dma('pool', ecs_d[c8 * c8e * 128:(c8 + 1) * c8e * 128, :].rearrange("(n p) c -> p n c", p=128), stg[:, bq, 0:c8e * 64].rearrange("p (n c) -> p n c", c=64),
                      reads=[('stg', bq)], writes=['ecs_d'])
            P.op('act', lambda e: e.activation(out=dcol[0:32, :], in_=cs3[0:32, :, 127], func=AF.Exp), reads=['cs'], writes=['dcol'])
            P.op('act', lambda e: e.activation(out=dcol[32:64, :], in_=cs3[32:64, :, 0], func=AF.Exp), reads=['cs'], writes=['dcol'])
            P.op('dve', lambda e: e.tensor_tensor(out=xd[:], in0=dcol[:, :].unsqueeze(2).broadcast_to([64, NT, 64]), in1=identf[0:64, 0:64].unsqueeze(1).broadcast_to([64, NT, 64]), op=ALU.mult),
                 reads=['dcol', 'identf'], writes=['xd'])
            c8n = min(8, NT)
            for c8 in range(NT // c8n):
                pi = 4 + c8 % 2
                b = c8 % 2
                P.op('pe', lambda e: e.matmul(ps[pi][:, 0:c8n * 64], onesf[0:64, :], xd[:, c8 * c8n:(c8 + 1) * c8n, :], start=True, stop=True), reads=['onesf', 'xd'], writes=[psk[pi]])
                copy(evac_eng(), stg[:, b, 0:c8n * 64], ps[pi][:, 0:c8n * 64], [psk[pi]], [('stg', b)])
                P.dma('pool', dec_d[:, c8 * c8n:(c8 + 1) * c8n, :], stg[:, b, 0:c8n * 64].rearrange("p (c k) -> p c k", k=64), reads=[('stg', b)], writes=['dec_d'])
            c4n = min(4, NT)
            for c4 in range(NT // c4n):
                pi = 6 + c4 % 2
                b = c4 % 2
                for i in range(c4n):
                    cc = c4 * c4n + i
                    P.op('pe', lambda e: e.transpose(out=ps[pi][:, i * 128:i * 128 + 64], in_=t1[:, cc * 128:(cc + 1) * 128], identity=identf[0:64, 0:64]), reads=['t1', 'identf'], writes=[psk[pi]])
                    P.op('pe', lambda e: e.transpose(out=ps[pi][:, i * 128 + 64:(i + 1) * 128], in_=t2[:, cc * 128:(cc + 1) * 128], identity=identf[0:64, 0:64]), reads=['t2', 'identf'], writes=[psk[pi]])
                copy(evac_eng(), stg[:, b, 0:c4n * 128], ps[pi][:, 0:c4n * 128], [psk[pi]], [('stg', b)])
                P.dma('pool', nbw_d[c4 * c4n * 128:(c4 + 1) * c4n * 128, :].rearrange("(n p) c -> p n c", p=128), stg[:, b, 0:c4n * 128].rearrange("p (n c) -> p n c", c=128),
                      reads=[('stg', b)], writes=['nbw_d'])
            P.barrier()
        with ExitStack() as es:
            hf = es.enter_context(SB("m2_hf", [128, 2048], F32))
            hfb = es.enter_context(SB("m2_hfb", [128, 2, 2048], BF16))
            decb = es.enter_context(SB("m2_dec", [128, NT, 64], F32))
            selt = es.enter_context(SB("m2_sel", [64, 16, 4], F32))
            ngf = es.enter_context(SB("m2_ngf", [128, 2, 512], F32))
            ng = es.enter_context(SB("m2_ng", [128, 2, 512], BF16))
            Dbc = es.enter_context(SB("m2_D", [128, 2048], F32))
            nwbc = es.enter_context(SB("m2_nw", [128, 2048], F32))
            bcf = es.enter_context(SB("m2_bc", [128, 2, 16, 128], BF16))
            xtm = es.enter_context(SB("m2_x", [128, 2, 2048], BF16))
            btm = es.enter_context(SB("m2_bt", [128, 2, 1024], BF16))
            nbw = es.enter_context(SB("m2_nbw", [128, 2, 128], F32))
            csc = es.enter_context(SB("m2_cs", [128, 2, 128], F32))
            csr = es.enter_context(SB("m2_csr", [128, 2, 128], F32))
            zt = es.enter_context(SB("m2_z", [128, 2, 2048], F32))
            hbt = es.enter_context(SB("m2_hb", [128, 2, 2048], BF16))
            xdg = es.enter_context(SB("m2_xd", [64, 2, 512], F32))
            ecs = es.enter_context(SB("m2_ecs", [128, 2, 64], F32))
            gg = es.enter_context(SB("m2_g", [128, 2, 2, 512], F32))
            selbig = es.enter_context(SB("m2_selbig", [128, 64, 128], F32))
            mt = es.enter_context(SB("m2_mt", [128, 2, 512], BF16))
            yo = es.enter_context(SB("m2_yo", [128, 2, 2048], F32))
            xd = es.enter_context(SB("m2_xd", [128, 2048], F32))
            xw = es.enter_context(SB("m2_xw", [128, 2, 256], BF16))
            tmpt = es.enter_context(SB("m2_tmp", [128, 2, 256], F32))
            y2 = es.enter_context(SB("m2_y", [128, 1024], F32))
            y3 = es.enter_context(SB("m2_y3", [128, 1024], F32))
            yb = es.enter_context(SB("m2_yb", [128, 1024], BF16))
            ss = es.enter_context(SB("m2_ss", [128, 8], F32))
            ytt = es.enter_context(SB("m2_yt", [128, 2, 8, 128], BF16))
            P.dma('sp', decb[:], dec_d[:, 0:NT, :], reads=['dec_d'], writes=['decb'])
            P.dma('sp', selt[:], sel_c, writes=['selt'])
            P.op('dve', lambda e: e.tensor_copy(out=selbig[0:64].bitcast(F32R), in_=identf[0:64, 0:64].unsqueeze(2).broadcast_to([64, 64, 128])), reads=['identf'], writes=['selbig'])
            P.op('dve', lambda e: e.tensor_copy(out=selbig[64:128].bitcast(F32R), in_=identf[64:128, 64:128].unsqueeze(2).broadcast_to([64, 64, 128])), reads=['identf'], writes=['selbig'])
            P.dma('sp', ngf[:], negm_c, writes=['ngf'])
            P.op('dve', lambda e: e.tensor_copy(out=ng[:], in_=ngf[:]), reads=['ngf'], writes=['ng'])
            P.dma('sp', Dbc[:], W['m_d_rep'][j].unsqueeze(0).broadcast_to([128, 2048]), writes=['Dbc'])
            P.dma('sp', nwbc[:], W['m_norm_w'][j].unsqueeze(0).broadcast_to([128, 2048]), writes=['nwbc'])

            def state_update_g(c, d, b, hstate, hkey, pbank, g):
                xb = (c * 8 + g) % 2
                P.op('pool', lambda e: e.tensor_tensor(out=xw[:, xb, :].rearrange("p (h q) -> p h q", h=4), in0=xtm[:, b, g * 256:(g + 1) * 256].rearrange("p (h q) -> p h q", h=4),
                                                        in1=nbw[:, b, 64 + d * 32 + g * 4:64 + d * 32 + g * 4 + 4].unsqueeze(2).broadcast_to([128, 4, 64]), op=ALU.mult),
                     reads=[('xtm', b), ('nbw', b)], writes=[('xw', xb)])
                P.op('pe', lambda e: e.matmul(ps[pbank][:, 0:256], btm[:, b, g * 128:(g + 1) * 128], xw[:, xb, :], start=True, stop=True), reads=[('btm', b), ('xw', xb)], writes=[psk[pbank]])
                P.op('dve', lambda e: e.tensor_tensor(out=tmpt[:, g % 2, :].rearrange("p (h q) -> p h q", h=4), in0=hstate[:, g * 256:(g + 1) * 256].rearrange("p (h q) -> p h q", h=4),
                                                       in1=decb[:, c, d * 32 + g * 4:d * 32 + g * 4 + 4].unsqueeze(2).broadcast_to([128, 4, 64]), op=ALU.mult),
                     reads=[(hkey, g), 'decb'], writes=[('tmpt', g % 2)])
                P.op('dve', lambda e: e.tensor_tensor(out=hstate[:, g * 256:(g + 1) * 256], in0=tmpt[:, g % 2, :], in1=ps[pbank][:, 0:256], op=ALU.add), reads=[('tmpt', g % 2), psk[pbank]], writes=[(hkey, g)])

            def state_update(c, d, b, hstate, hkey, pbank):
                for g in range(8):
                    state_update_g(c, d, b, hstate, hkey, pbank, g)

            hfkeys = [('hf', g) for g in range(8)]

            P.op('dve', lambda e: e.memset(hf[:], 0.0), writes=hfkeys)
            P.op('dve', lambda e: e.memset(hfb[:], 0.0), writes=[('hfb', 0), ('hfb', 1)])
            for ci, c in enumerate(range(NT - 1, -1, -1)):
                b = ci % 2
                P.dma('sp', xtm[:, b, :], xtm_d[c * 128:(c + 1) * 128, :], writes=[('xtm', b)])
                P.dma('sp', btm[:, b, :], btm_d[c * 128:(c + 1) * 128, :], writes=[('btm', b)])
                P.dma('sp', nbw[:, b, :], nbw_d[c * 128:(c + 1) * 128, :], writes=[('nbw', b)])
                P.dma('pool', hb_d[c], hfb[:, b, :], reads=[('hfb', b)], writes=['hb_d'])
                if c > 0:
                    state_update(c, 1, b, hf, 'hf', 6)
                    P.op('pool', lambda e: e.tensor_copy(out=hfb[:, 1 - b, :], in_=hf[:]), reads=hfkeys, writes=[('hfb', 1 - b)])
            P.barrier()
            P.op('dve', lambda e: e.memset(hf[:], 0.0), writes=hfkeys)
            P.op('dve', lambda e: e.memset(hfb[:], 0.0), writes=[('hfb', 0), ('hfb', 1)])
            for c in range(NT):
                b = c % 2
                P.dma('sp', bcf[:, b], xbcT_d[:, 16:32, c * 128:(c + 1) * 128], writes=[('bcf', b)])
                P.dma('sp', xtm[:, b, :], xtm_d[c * 128:(c + 1) * 128, :], writes=[('xtm', b)])
                P.dma('sp', btm[:, b, :], btm_d[c * 128:(c + 1) * 128, :], writes=[('btm', b)])
                P.dma('sp', nbw[:, b, :], nbw_d[c * 128:(c + 1) * 128, :], writes=[('nbw', b)])
                P.dma('sp', csc[:, b, :], cs_d[:, c * 128:(c + 1) * 128], writes=[('csc', b)])
                P.op('pool', lambda e: e.tensor_copy(out=csr[:, b, :].bitcast(F32R), in_=csc[:, b, :]), reads=[('csc', b)], writes=[('csr', b)])
                P.dma('sp', zt[:, b, :], z_d[c * 128:(c + 1) * 128, :], writes=[('zt', b)])
                P.dma('sp', hbt[:, b, :], hb_d[c], writes=[('hbt', b)])
                P.dma('sp', ecs[:, b, :], ecs_d[c * 128:(c + 1) * 128, :], writes=[('ecs', b)])
                def cbmm(half):
                    for gi in range(4):
                        g = half * 4 + gi
                        P.op('pe', lambda e: e.matmul(ps[0][:, gi * 128:(gi + 1) * 128], bcf[:, b, g, :], bcf[:, b, 8 + g, :], start=True, stop=True), reads=[('bcf', b)], writes=[psk[0]])

                def front_pe_act(g):
                    gp = g % 2
                    for d in range(2):
                        pB = 3 if d == 0 else 6
                        for hh in range(4):
                            krow = d * 32 + g * 4 + hh
                            P.op('pe', lambda e: e.matmul(ps[pB][:, hh * 128:(hh + 1) * 128], selbig[:, krow, :].bitcast(F32R), csr[:, b, :].bitcast(F32R), start=(hh == 0), stop=False), reads=['selbig', ('csr', b)], writes=[psk[pB]])
                        P.op('pe', lambda e: e.matmul(ps[pB][:, :], identb[:], ng[:, d, :], start=False, stop=True), reads=['identb', 'ng'], writes=[psk[pB]])
                        for hh in range(4):
                            hcol = d * 32 + g * 4 + hh
                            P.op('act', lambda e: e.activation(out=gg[:, gp, d, hh * 128:(hh + 1) * 128], in_=ps[pB][:, hh * 128:(hh + 1) * 128], func=AF.Exp, bias=nbw[:, b, hcol:hcol + 1]),
                                 reads=[psk[pB], ('nbw', b)], writes=[('gg', gp, d)])
                        hsrc, hk = (hfb, ('hfb', b)) if d == 0 else (hbt, ('hbt', b))
                        P.op('pe', lambda e: e.matmul(ps[1 + d][:, 0:256], bcf[:, b, 8 + g, :], hsrc[:, b, g * 256:(g + 1) * 256], start=True, stop=True), reads=[('bcf', b), hk], writes=[psk[1 + d]])

                def front_dve(g):
                    for d in range(2):
                        P.op('dve', lambda e: e.tensor_tensor(out=yo[:, d, g * 256:(g + 1) * 256].rearrange("p (h q) -> p h q", h=4), in0=ps[1 + d][:, 0:256].rearrange("p (h q) -> p h q", h=4),
                                                               in1=ecs[:, b, d * 32 + g * 4:d * 32 + g * 4 + 4].unsqueeze(2).broadcast_to([128, 4, 64]), op=ALU.mult),
                             reads=[psk[1 + d], ('ecs', b)], writes=[('yo', d, g // 4)])

                def back_dve(g):
                    gp = g % 2
                    gi = g % 4
                    P.op('pool', lambda e: e.tensor_tensor(out=gg[:, gp, 0, :], in0=gg[:, gp, 0, :], in1=gg[:, gp, 1, :], op=ALU.add), reads=[('gg', gp, 0), ('gg', gp, 1)], writes=[('gg', gp, 0)])
                    P.op('dve', lambda e: e.tensor_tensor(out=mt[:, gp, :].rearrange("p (h q) -> p h q", h=4), in0=gg[:, gp, 0, :].rearrange("p (h q) -> p h q", h=4),
                                                           in1=ps[0][:, gi * 128:(gi + 1) * 128].unsqueeze(1).broadcast_to([128, 4, 128]), op=ALU.mult),
                         reads=[('gg', gp, 0), psk[0]], writes=[('mt', gp)])

                def back_pe(g):
                    gp = g % 2
                    gi = g % 4
                    py = 4 + gi // 2
                    for hh in range(4):
                        h = g * 4 + hh
                        oc = (gi % 2) * 256 + hh * 64
                        P.op('pe', lambda e: e.matmul(ps[py][:, oc:oc + 64], mt[:, gp, hh * 128:(hh + 1) * 128], xtm[:, b, h * 64:(h + 1) * 64], start=True, stop=True),
                             reads=[('mt', gp), ('xtm', b)], writes=[psk[py]])

                def state_g(g):
                    xb = (c * 8 + g) % 2
                    P.op('pool', lambda e: e.tensor_tensor(out=xw[:, xb, :].rearrange("p (h q) -> p h q", h=4), in0=xtm[:, b, g * 256:(g + 1) * 256].rearrange("p (h q) -> p h q", h=4),
                                                            in1=nbw[:, b, 64 + g * 4:64 + g * 4 + 4].unsqueeze(2).broadcast_to([128, 4, 64]), op=ALU.mult),
                         reads=[('xtm', b), ('nbw', b)], writes=[('xw', xb)])
                    P.op('pe', lambda e: e.matmul(ps[7][:, 0:256], btm[:, b, g * 128:(g + 1) * 128], xw[:, xb, :], start=True, stop=True), reads=[('btm', b), ('xw', xb)], writes=[psk[7]])
                    P.op('dve', lambda e: e.tensor_tensor(out=tmpt[:, g % 2, :].rearrange("p (h q) -> p h q", h=4), in0=hf[:, g * 256:(g + 1) * 256].rearrange("p (h q) -> p h q", h=4),
                                                           in1=decb[:, c, g * 4:g * 4 + 4].unsqueeze(2).broadcast_to([128, 4, 64]), op=ALU.mult),
                         reads=[('hf', g), 'decb'], writes=[('tmpt', g % 2)])
                    P.op('dve', lambda e: e.tensor_tensor(out=hf[:, g * 256:(g + 1) * 256], in0=tmpt[:, g % 2, :], in1=ps[7][:, 0:256], op=ALU.add), reads=[('tmpt', g % 2), psk[7]], writes=[('hf', g)])
                    P.op('dve', lambda e: e.tensor_copy(out=hfb[:, 1 - b, g * 256:(g + 1) * 256], in_=hf[:, g * 256:(g + 1) * 256]), reads=[('hf', g)], writes=[('hfb', 1 - b)])

                def epilogue(half):
                    c0 = half * 1024
                    P.op('pool', lambda e: e.tensor_tensor(out=yo[:, 0, c0:c0 + 1024], in0=yo[:, 0, c0:c0 + 1024], in1=yo[:, 1, c0:c0 + 1024], op=ALU.add), reads=[('yo', 0, half), ('yo', 1, half)], writes=[('yo', 0, half)])
                    for q in range(2):
                        P.op('dve', lambda e: e.tensor_tensor(out=y2[:, q * 512:(q + 1) * 512], in0=xd[:, c0 + q * 512:c0 + (q + 1) * 512], in1=ps[4 + q][:, :], op=ALU.add), reads=['xd', psk[4 + q]], writes=['y2'])
                    P.op('dve', lambda e: e.tensor_tensor(out=y2[:], in0=y2[:], in1=yo[:, 0, c0:c0 + 1024], op=ALU.add), reads=['y2', ('yo', 0, half)], writes=['y2'])
                    P.op('dve', lambda e: e.tensor_tensor(out=y2[:], in0=y2[:], in1=zt[:, b, c0:c0 + 1024], op=ALU.mult), reads=['y2', ('zt', b)], writes=['y2'])
                    P.op('dve', lambda e: e.tensor_tensor(out=y3[:], in0=y2[:], in1=y2[:], op=ALU.mult), reads=['y2', 'y3'], writes=['y3'])
                    P.op('dve', lambda e: e.tensor_reduce(out=ss[:, 0:4], in_=y3[:].rearrange("p (g c) -> p g c", g=4), op=ALU.add, axis=AX.X), reads=['y3'], writes=['ss'])
                    P.op('dve', lambda e: e.tensor_scalar(out=ss[:, 0:4], in0=ss[:, 0:4], scalar1=1.0 / 256, scalar2=1e-5, op0=ALU.mult, op1=ALU.add), reads=['ss'], writes=['ss'])
                    P.op('act', lambda e: e.activation(out=ss[:, 0:4], in_=ss[:, 0:4], func=AF.Ln), reads=['ss'], writes=['ss'])
                    P.op('act', lambda e: e.activation(out=ss[:, 4:8], in_=ss[:, 0:4], func=AF.Exp, scale=-0.5), reads=['ss'], writes=['ss2'])
                    P.op('dve', lambda e: e.tensor_tensor(out=y2[:].rearrange("p (g c) -> p g c", g=4), in0=y2[:].rearrange("p (g c) -> p g c", g=4), in1=ss[:, 4:8].unsqueeze(2).broadcast_to([128, 4, 256]), op=ALU.mult),
                         reads=['y2', 'ss2'], writes=['y2'])
                    P.op('pool', lambda e: e.tensor_tensor(out=yb[:], in0=y2[:], in1=nwbc[:, c0:c0 + 1024], op=ALU.mult), reads=['y2', 'nwbc'], writes=['yb'])
                    pv = ps[7][:, :].bitcast(BF16)
                    for i in range(8):
                        P.op('pe', lambda e: e.transpose(out=pv[:, i * 128:(i + 1) * 128], in_=yb[:, i * 128:(i + 1) * 128], identity=identb[:]), reads=['yb', 'identb'], writes=[psk[7]])
                    yb2 = (c * 2 + half) % 2
                    copy('dve', ytt[:, yb2, :, :], pv.rearrange("p (n t) -> p n t", n=8), [psk[7]], [('ytt', yb2)])
                    P.dma('pool', yT_d[:, half * 8:(half + 1) * 8, c * 128:(c + 1) * 128], ytt[:, yb2, :, :], reads=[('ytt', yb2)], writes=['yT_d'])

                for q in range(2):
                    P.op('pool', lambda e: e.tensor_tensor(out=xd[:, q * 1024:(q + 1) * 1024], in0=xtm[:, b, q * 1024:(q + 1) * 1024], in1=Dbc[:, q * 1024:(q + 1) * 1024], op=ALU.mult), reads=[('xtm', b), 'Dbc'], writes=['xd'])
                cbmm(0)
                front_pe_act(0)
                front_dve(0)
                for g in range(8):
                    if g + 1 < 8:
                        front_pe_act(g + 1)
                    back_dve(g)
                    if g + 1 < 8:
                        front_dve(g + 1)
                    back_pe(g)
                    if c < NT - 1:
                        state_g(g)
                    if g == 3:
                        cbmm(1)
                    if g % 4 == 3:
                        epilogue(g // 4)
            P.barrier()

    def phase_rwkv(j, L):
        NC = L // 64
        NB = L // 512
        Lh = min(L, 1024)
        NH = L // Lh
        NCh = Lh // 64
        NBh = Lh // 512
        LAM = math.exp(-0.5)
        win = wb['r_w_in'][j]
        with SB("r1_xT", [128, 8, L], BF16) as xT, SB("r1_w", [128, 2, 8, 128], BF16) as wsl, \
                SB("r1_rb", [128, L + 2], F32) as rb, SB("r1_tmp", [128, L], F32) as tmp, \
                SB("r1_o", [128, 2, L], F32) as orow, SB("r1_mu", [128, 3, 34], F32) as mu:
            P.dma('sp', xT[:], xT_d[:, :, 0:L], writes=['r1_xT'])
            P.dma('sp', mu[:, 0, :], W['r_mu_t'][j], writes=['mu0'])
            P.op('dve', lambda e: e.tensor_scalar(out=mu[:, 1, :], in0=mu[:, 0, :], scalar1=0.5, scalar2=None, op0=ALU.mult), reads=['mu0'], writes=['mu1'])
            P.op('dve', lambda e: e.tensor_scalar(out=mu[:, 2, :], in0=mu[:, 0, :], scalar1=-1.0, scalar2=1.0, op0=ALU.mult, op1=ALU.add), reads=['mu0'], writes=['mu2'])
            P.op('dve', lambda e: e.memset(rb[:, 0:1], 0.0), writes=['rb'])
            P.op('dve', lambda e: e.memset(rb[:, L + 1:L + 2], 0.0), writes=['rb'])
            k = 0
            for fc in range(34):
                wbuf = fc % 2
                P.dma('sp', wsl[:, wbuf], win[:, :, fc * 128:(fc + 1) * 128], writes=[('r1w', wbuf)])
                for tb in range(NB):
                    pi = k % 6
                    k += 1
                    for dc in range(8):
                        P.op('pe', lambda e: e.matmul(ps[pi][:, :], wsl[:, wbuf, dc, :], xT[:, dc, tb * 512:(tb + 1) * 512], start=(dc == 0), stop=(dc == 7)),
                             reads=['r1_xT', ('r1w', wbuf)], writes=[psk[pi]])
                    P.op('act', lambda e: e.copy(out=rb[:, 1 + tb * 512:1 + (tb + 1) * 512], in_=ps[pi][:, :]), reads=[psk[pi]], writes=['rb'])
                P.op('dve', lambda e: e.tensor_tensor(out=tmp[:], in0=rb[:, 0:L], in1=rb[:, 2:L + 2], op=ALU.add), reads=['rb'], writes=['tmp'])
                P.op('dve', lambda e: e.tensor_scalar(out=tmp[:], in0=tmp[:], scalar1=mu[:, 1, fc:fc + 1], scalar2=None, op0=ALU.mult), reads=['tmp', 'mu1'], writes=['tmp'])
                ob_ = fc % 2
                P.op('dve', lambda e: e.scalar_tensor_tensor(out=orow[:, ob_, :], in0=rb[:, 1:L + 1], scalar=mu[:, 2, fc:fc + 1], in1=tmp[:], op0=ALU.mult, op1=ALU.add),
                     reads=['rb', 'tmp', 'mu2'], writes=[('orow', ob_)])
                P.dma('pool', u_d[:, fc, 0:L], orow[:, ob_, :], reads=[('orow', ob_)], writes=['u_d'])
            P.barrier()
        if RW_STAGE < 2:
            return
        with ExitStack() as es:
            A_ = lambda n, sh, dt: es.enter_context(SB(n, sh, dt))
            wupf = A_("r2_wupf", [128, 2, 1024], F32)
            wupb = A_("r2_wupb", [128, 2, 1024], BF16)
            cols = A_("r2_cols", [128, 9, 8], F32)
            blk = A_("r2_blk", [128, 128], F32)
            m64 = A_("r2_m64", [128, Lh], F32)
            lwt = A_("r2_lwt", [128, Lh], BF16)
            lat = A_("r2_lat", [128, Lh], BF16)
            rows = {n: A_("r2_" + n, [128, Lh], F32) for n in ('r', 'k', 'v', 'sg0', 'sg1', 'a0', 'a1', 'kk', 't1', 't2', 't4', 'bs', 'e0', 'e1', 'ei')}
            opn = A_("r2_opn", [128, 2, NCh, 4, 64], F32)
            gbs = A_("r2_gbs", [128, 2, Lh], F32)
            wcs = A_("r2_wcs", [128, 2, NCh], F32)
            P.dma('sp', wupf[:, 0, :], W['r_wup_t'][j], writes=['wupf'])
            P.dma('sp', wupf[:, 1, :], W['r_aup_t'][j], writes=['wupf'])
            P.op('dve', lambda e: e.tensor_copy(out=wupb[:], in_=wupf[:]), reads=['wupf'], writes=['wupb'])
            P.dma('sp', cols[:, 0:2, :], W['r_w0_t'][j], writes=['cols'])
            P.dma('sp', cols[:, 2:4, :], W['r_a0_t'][j], writes=['cols'])
            P.dma('sp', cols[:, 4, :], W['r_kk_t'][j], writes=['cols'])
            P.dma('sp', cols[:, 5, :], W['r_ka_t'][j], writes=['cols'])
            P.dma('sp', cols[:, 7, :], W['r_rk_t'][j], writes=['cols'])
            P.op('dve', lambda e: e.tensor_scalar(out=cols[:, 6, :], in0=cols[:, 5, :], scalar1=-1.0, scalar2=1.0, op0=ALU.mult, op1=ALU.add), reads=['cols'], writes=['cols6'])
            P.dma('sp', blk[:], blk_c, writes=['blk'])
            P.dma('sp', m64[:], mask64_c[:, 0:Lh], writes=['m64'])
            R = rows
            kq = 0
            for hf_ in range(NH):
                t0h = hf_ * Lh
                P.dma('sp', R['t1'][:], u_d[:, 32, t0h:t0h + Lh], writes=['t1'])
                P.op('act', lambda e: e.activation(out=lwt[:], in_=R['t1'][:], func=AF.Tanh), reads=['t1'], writes=['lwt'])
                P.dma('sp', R['t2'][:], u_d[:, 33, t0h:t0h + Lh], writes=['t2'])
                P.op('act', lambda e: e.copy(out=lat[:], in_=R['t2'][:]), reads=['t2'], writes=['lat'])
                for cc in range(8):
                    P.dma('sp', R['r'][:], u_d[:, cc, t0h:t0h + Lh], writes=['r'])
                    P.dma('sp', R['k'][:], u_d[:, 8 + cc, t0h:t0h + Lh], writes=['k'])
                    P.dma('sp', R['v'][:], u_d[:, 16 + cc, t0h:t0h + Lh], writes=['v'])
                    P.dma('sp', R['t4'][:], u_d[:, 24 + cc, t0h:t0h + Lh], writes=['t4'])
                    P.op('act', lambda e: e.activation(out=gbs[:, 0, :], in_=R['t4'][:], func=AF.Silu), reads=['t4'], writes=['gbs0'])
                    for tb in range(NBh):
                        sl = slice(tb * 512, (tb + 1) * 512)
                        for d in range(2):
                            P.op('pe', lambda e: e.matmul(ps[d][:, :], wupb[d * 64:(d + 1) * 64, 0, cc * 128:(cc + 1) * 128], lwt[d * 64:(d + 1) * 64, sl], start=True, stop=True),
                                 reads=['wupb', 'lwt'], writes=[psk[d]])
                            P.op('act', lambda e: e.activation(out=R['sg%d' % d][:, sl], in_=ps[d][:, :], func=AF.Sigmoid, bias=cols[:, d, cc:cc + 1]), reads=[psk[d], 'cols'], writes=['sg%d' % d])
                            P.op('pe', lambda e: e.matmul(ps[2 + d][:, :], wupb[d * 64:(d + 1) * 64, 1, cc * 128:(cc + 1) * 128], lat[d * 64:(d + 1) * 64, sl], start=True, stop=True),
                                 reads=['wupb', 'lat'], writes=[psk[2 + d]])
                            P.op('act', lambda e: e.activation(out=R['a%d' % d][:, sl], in_=ps[2 + d][:, :], func=AF.Sigmoid, bias=cols[:, 2 + d, cc:cc + 1]), reads=[psk[2 + d], 'cols'], writes=['a%d' % d])
                    P.op('dve', lambda e: e.tensor_scalar(out=R['kk'][:], in0=R['k'][:], scalar1=cols[:, 4, cc:cc + 1], scalar2=None, op0=ALU.mult), reads=['k', 'cols'], writes=['kk'])
                    P.op('dve', lambda e: e.tensor_tensor(out=R['t1'][:], in0=R['kk'][:], in1=R['kk'][:], op=ALU.mult), reads=['kk', 't1'], writes=['t1'])
                    for tb in range(NBh):
                        sl = slice(tb * 512, (tb + 1) * 512)
                        pi = 4 + tb % 2
                        P.op('pe', lambda e: e.matmul(ps[pi][:, :], blk[:], R['t1'][:, sl], start=True, stop=True), reads=['blk', 't1'], writes=[psk[pi]])
                        P.op('act', lambda e: e.activation(out=R['t2'][:, sl], in_=ps[pi][:, :], func=AF.Sqrt), reads=[psk[pi], 't2'], writes=['t2'])
                    P.op('dve', lambda e: e.tensor_scalar(out=R['t2'][:], in0=R['t2'][:], scalar1=1e-12, scalar2=None, op0=ALU.max), reads=['t2'], writes=['t2'])
                    P.op('dve', lambda e: e.reciprocal(out=R['t2'][:], in_=R['t2'][:]), reads=['t2'], writes=['t2'])
                    P.op('dve', lambda e: e.tensor_tensor(out=R['kk'][:], in0=R['kk'][:], in1=R['t2'][:], op=ALU.mult), reads=['kk', 't2'], writes=['kk'])
                    for d in range(2):
                        sg = R['sg%d' % d]
                        a_ = R['a%d' % d]
                        ob2 = kq % 2
                        kq += 1
                        P.op('dve', lambda e: e.tensor_scalar(out=R['t1'][:], in0=a_[:], scalar1=cols[:, 5, cc:cc + 1], scalar2=cols[:, 6, cc:cc + 1], op0=ALU.mult, op1=ALU.add), reads=['a%d' % d, 'cols', 'cols6', 't1'], writes=['t1'])
                        P.op('dve', lambda e: e.tensor_tensor(out=R['t1'][:], in0=R['t1'][:], in1=R['k'][:], op=ALU.mult), reads=['t1', 'k'], writes=['t1'])
                        if d == 0:
                            P.op('dve', lambda e: e.scalar_tensor_tensor(out=R['bs'][:], in0=R['t1'][:], scalar=cols[:, 7, cc:cc + 1], in1=R['r'][:], op0=ALU.mult, op1=ALU.mult), reads=['t1', 'r', 'cols', 'bs'], writes=['bs'])
                        else:
                            P.op('dve', lambda e: e.scalar_tensor_tensor(out=R['t2'][:], in0=R['t1'][:], scalar=cols[:, 7, cc:cc + 1], in1=R['r'][:], op0=ALU.mult, op1=ALU.mult), reads=['t1', 'r', 'cols', 't2'], writes=['t2'])
                            P.op('dve', lambda e: e.tensor_tensor(out=R['bs'][:], in0=R['bs'][:], in1=R['t2'][:], op=ALU.add), reads=['bs', 't2'], writes=['bs'])
                        P.op('dve', lambda e: e.tensor_tensor_scan(out=R['t2'][:], data0=m64[:], data1=sg[:], initial=0.0, op0=ALU.mult, op1=ALU.add), reads=['m64', 'sg%d' % d, 't2'], writes=['t2'])
                        t23 = R['t2'][:].rearrange("p (c t) -> p c t", t=64)
                        t43 = R['t4'][:].rearrange("p (c t) -> p c t", t=64)
                        if d == 1:
                            P.op('dve', lambda e: e.tensor_tensor(out=t43, in0=t23[:, :, 63:64].broadcast_to([128, NCh, 64]), in1=t23, op=ALU.subtract), reads=['t2', 't4'], writes=['t4'])
                            P.op('dve', lambda e: e.tensor_tensor(out=R['t2'][:], in0=R['t4'][:], in1=sg[:], op=ALU.add), reads=['t4', 'sg%d' % d], writes=['t2'])
                        P.op('act', lambda e: e.activation(out=R['e1'][:], in_=R['t2'][:], func=AF.Exp, scale=-LAM), reads=['t2', 'e1'], writes=['e1'])
                        P.op('act', lambda e: e.activation(out=R['ei'][:], in_=R['t2'][:], func=AF.Exp, scale=LAM), reads=['t2', 'ei'], writes=['ei'])
                        P.op('dve', lambda e: e.tensor_tensor(out=R['t4'][:], in0=R['t2'][:], in1=sg[:], op=ALU.subtract), reads=['t2', 'sg%d' % d, 't4'], writes=['t4'])
                        P.op('act', lambda e: e.activation(out=R['e0'][:], in_=R['t4'][:], func=AF.Exp, scale=-LAM), reads=['t4', 'e0'], writes=['e0'])
                        e13 = R['e1'][:].rearrange("p (c t) -> p c t", t=64)
                        ecol = 63 if d == 0 else 0
                        P.op('act', lambda e: e.copy(out=wcs[:, d, :], in_=e13[:, :, ecol]), reads=['e1'], writes=['wcs'])
                        def o3(kind):
                            return opn[:, ob2, :, kind, :]
                        def v3(n):
                            return R[n][:].rearrange("p (c t) -> p c t", t=64)
                        P.op('dve', lambda e: e.tensor_tensor(out=R['t4'][:], in0=R['kk'][:], in1=a_[:], op=ALU.mult), reads=['kk', 'a%d' % d, 't4'], writes=['t4'])
                        P.op('dve', lambda e: e.tensor_tensor(out=o3(0), in0=v3('t4'), in1=v3('ei'), op=ALU.mult), reads=['t4', 'ei'], writes=[('opn', ob2)])
                        P.op('pool', lambda e: e.tensor_tensor(out=o3(1), in0=v3('t1'), in1=v3('ei'), op=ALU.mult), reads=['t1', 'ei'], writes=[('opn', ob2)])
                        P.op('dve', lambda e: e.scalar_tensor_tensor(out=o3(2), in0=v3('kk'), scalar=-1.0, in1=v3('e0'), op0=ALU.mult, op1=ALU.mult), reads=['kk', 'e0'], writes=[('opn', ob2)])
                        P.op('pool', lambda e: e.tensor_tensor(out=o3(3), in0=v3('r'), in1=v3('e1'), op=ALU.mult), reads=['r', 'e1'], writes=[('opn', ob2)])
                        for hh in range(2):
                            P.dma('pool', rop_d[:, 2 * cc + hh, d, hf_ * NCh:(hf_ + 1) * NCh, :, :], opn[hh * 64:(hh + 1) * 64, ob2], reads=[('opn', ob2)], writes=['rop_d'])
                    for tb in range(NBh):
                        sl = slice(tb * 512, (tb + 1) * 512)
                        pi = 6 + tb % 2
                        P.op('pe', lambda e: e.matmul(ps[pi][:, :], blk[:], R['bs'][:, sl], start=True, stop=True), reads=['blk', 'bs'], writes=[psk[pi]])
                        P.op('dve', lambda e: e.tensor_tensor(out=gbs[:, 1, sl], in0=ps[pi][:, :], in1=R['v'][:, sl], op=ALU.mult), reads=[psk[pi], 'v'], writes=['gbs1'])
                    P.dma('pool', gb_d[:, cc, :, t0h:t0h + Lh], gbs[:], reads=['gbs0', 'gbs1'], writes=['gb_d'])
                    for hh in range(2):
                        P.dma('pool', wc_d[:, 2 * cc + hh, :, hf_ * NCh:(hf_ + 1) * NCh], wcs[hh * 64:(hh + 1) * 64], reads=['wcs'], writes=['wc_d'])
            P.barrier()
        if RW_STAGE < 3:
            return
        with ExitStack() as es:
            A_ = lambda n, sh, dt: es.enter_context(SB(n, sh, dt))
            fr = lambda ap: ap.bitcast(F32R)
            mk = A_("r3_mk", [128, 2, 128], F32)
            mkn = A_("r3_mkn", [64, 2, 64], F32)
            wcall = A_("r3_wc", [64, 16, 2, NC], F32)
            lncol = A_("r3_ln", [128, 2, 8], F32)
            rop = A_("r3_rop", [128, 2, 16, 4, 64], F32)
            vf = A_("r3_vf", [128, 2, 8, 128], F32)
            ropr = A_("r3_ropr", [128, 2, 16, 4, 64], F32)
            Hsr = A_("r3_Hsr", [128, 16, 64], F32)
            AT = A_("r3_AT", [128, 16, 128], F32)
            PT = A_("r3_PT", [128, 2, 16, 2, 64], F32)
            Qm = A_("r3_Q", [128, 2, 16, 64], F32)
            Z = A_("r3_Z", [128, 16, 64], F32)
            Zx = A_("r3_Zx", [128, 16, 64], F32)
            Xs = A_("r3_X", [128, 16, 64], F32)
            BKs = A_("r3_BK", [128, 16, 64], F32)
            Hs = A_("r3_H", [128, 16, 64], F32)
            Ht = A_("r3_Ht", [64, 16, 64], F32)
            ysb = A_("r3_y", [128, 2, 1024], F32)
            ybl = A_("r3_yb", [64, 2, 1024], F32)
            ysq = A_("r3_ysq", [64, 1024], F32)
            gst = A_("r3_gst", [64, 4, 16], F32)
            gbt = A_("r3_gb", [128, 2, 8, 2, 64], F32)
            ofm = A_("r3_ofm", [128, 512], F32)
            ofb = A_("r3_ofb", [128, 2, 8, 64], BF16)
            P.dma('sp', mk[:], rmask_c, writes=['mk'])
            P.dma('sp', mkn[:], rmaskn_c, writes=['mkn'])
            P.dma('sp', wcall[:], wc_d[:, :, :, 0:NC], writes=['wcall'])
            P.dma('sp', lncol[:, 0, :], W['r_lnw_t'][j], writes=['lncol'])
            P.dma('sp', lncol[:, 1, :], W['r_lnb_t'][j], writes=['lncol'])
            P.op('dve', lambda e: e.memset(vf[:], 0.0), writes=[('vf', 0), ('vf', 1)])
            P.op('dve', lambda e: e.memset(rop[:, 0].rearrange("p h k t -> p (h k t)"), 0.0), writes=[('rop', 0)])
            P.op('dve', lambda e: e.memset(rop[:, 1].rearrange("p h k t -> p (h k t)"), 0.0), writes=[('rop', 1)])
            P.op('dve', lambda e: e.memset(ropr[:, 0].rearrange("p h k t -> p (h k t)"), 0.0), writes=[('ropr', 0)])
            P.op('dve', lambda e: e.memset(ropr[:, 1].rearrange("p h k t -> p (h k t)"), 0.0), writes=[('ropr', 1)])
            P.op('dve', lambda e: e.memset(Hsr[:], 0.0), writes=['Hsr'])
            if RW_X != 2:
                P.op('dve', lambda e: e.memset(PT[:].rearrange("p a h k t -> p (a h k t)"), 0.0), writes=[('PT', 0, 0), ('PT', 0, 1), ('PT', 1, 0), ('PT', 1, 1)])
                P.op('dve', lambda e: e.memset(Qm[:], 0.0), writes=[('Q', 0, 0), ('Q', 0, 1), ('Q', 1, 0), ('Q', 1, 1)])
                P.op('dve', lambda e: e.memset(Zx[:], 0.0), writes=['Zx'])
                P.op('dve', lambda e: e.memset(Xs[:], 0.0), writes=[('Xs', 0), ('Xs', 1)])
                P.op('dve', lambda e: e.memset(ysb[:], 0.0), writes=[('ysb', 0), ('ysb', 1)])

            def hview(bank_pair, q, w_):
                return ps[bank_pair + q][0:64, :].rearrange("p (h t) -> p h t", t=w_)

            def prefetch(si, c, d):
                b = si % 2
                P.dma('sp', rop[0:64, b], rop_d[:, :, d, c, :, :], writes=[('rop', b)])
                P.op('pool', lambda e: e.tensor_copy(out=fr(ropr[0:64, b].rearrange("p h k t -> p (h k t)")), in_=rop[0:64, b].rearrange("p h k t -> p (h k t)")), reads=[('rop', b)], writes=[('ropr', b)])
                P.dma('sp', vf[:, b, :, 64:128], u_d[:, 16:24, c * 64:(c + 1) * 64], writes=[('vf', b)])

            def step(si, c, d, nxt_c=None):
                b = si % 2
                for cc in range(8):
                    pi = 6 + cc // 4
                    P.op('pe', lambda e: e.transpose(out=ps[pi][:, (cc % 4) * 128:(cc % 4 + 1) * 128], in_=vf[:, b, cc, :], identity=identf[:]), reads=[('vf', b), 'identf'], writes=[psk[pi]])
                for q in range(2):
                    src = ps[6 + q][64:128, :].rearrange("p (h v) -> p h v", v=64)
                    copy('act', fr(Z[64:128, q * 8:(q + 1) * 8, :]), src, [psk[6 + q]], ['Zv'])
                    copy('pool', fr(Zx[64:128, q * 8:(q + 1) * 8, :]), Z[64:128, q * 8:(q + 1) * 8, :], ['Zv'], ['Zx'])
                if RW_SUB < 2:
                    return b
                for h in range(16):
                    pi = h // 4
                    P.op('pe', lambda e: e.matmul(ps[pi][:, (h % 4) * 128:(h % 4 + 1) * 128], fr(ropr[:, b, h, 0:2, :].rearrange("p a t -> p (a t)")),
                                                  fr(ropr[:, b, h, 2:4, :].rearrange("p a t -> p (a t)")), start=True, stop=True), reads=[('ropr', b)], writes=[psk[pi]])
                for q in range(4):
                    P.op('dve', lambda e: e.tensor_tensor(out=fr(AT[:, q * 4:(q + 1) * 4, :]), in0=ps[q][:, :].rearrange("p (h t) -> p h t", t=128),
                                                           in1=mk[:, d:d + 1, :].broadcast_to([128, 4, 128]), op=ALU.mult), reads=[psk[q], 'mk'], writes=['AT'])
                if nxt_c is not None:
                    prefetch(si + 1, nxt_c, d)
                for h in range(16):
                    pi = 4 + h // 8
                    P.op('pe', lambda e: e.matmul(ps[pi][0:64, (h % 8) * 64:(h % 8 + 1) * 64], fr(ropr[:, b, h, 2, :]), fr(ropr[:, b, h, 0, :]), start=True, stop=True),
                         reads=[('ropr', b)], writes=[psk[pi]])
                for q in range(2):
                    P.op('dve', lambda e: e.tensor_tensor(out=fr(Qm[0:64, 0, q * 8:(q + 1) * 8, :]), in0=hview(4, q, 64),
                                                           in1=mkn[:, d:d + 1, :].broadcast_to([64, 8, 64]), op=ALU.mult), reads=[psk[4 + q], 'mkn'], writes=[('Q', 0, q)])
                P.op('act', lambda e: e.activation(out=fr(PT[0:64, 0, :, 0, :]), in_=AT[0:64, :, 0:64], func=AF.Identity), reads=['AT'], writes=[('PT', 0, 0), ('PT', 0, 1)])
                P.op('dve', lambda e: e.tensor_tensor(out=fr(PT[0:64, 1, :, 1, :]), in0=AT[0:64, :, 0:64], in1=identf[0:64, 0:64].unsqueeze(1).broadcast_to([64, 16, 64]), op=ALU.add),
                     reads=['AT', 'identf'], writes=[('PT', 1, 0), ('PT', 1, 1)])
                if RW_SUB < 4:
                    return b
                for h in range(16):
                    pi = h // 4
                    P.op('pe', lambda e: e.transpose(out=ps[pi][:, (h % 4) * 128:(h % 4 + 1) * 128], in_=rop[:, b, h, 0:2, :].rearrange("p a t -> p (a t)"), identity=identf[:]),
                         reads=[('rop', b), 'identf'], writes=[psk[pi]])
                for q in range(4):
                    copy('act' if q % 2 == 0 else 'dve', fr(BKs[:, q * 4:(q + 1) * 4, :]), ps[q][:, :].rearrange("p (h k) -> p h k", k=128)[:, :, 0:64], [psk[q]], ['BKs'])
                if RW_STAGE < 4:
                    return b
                for q in range(2):
                    for h in range(q * 8, q * 8 + 8):
                        P.op('pe', lambda e: e.matmul(ps[0 + q][0:64, (h % 8) * 64:(h % 8 + 1) * 64], fr(Qm[:, 0, h, :]), fr(PT[:, 0, h, 0, :]), start=True, stop=True),
                             reads=[('Q', 0, q), ('PT', 0, q)], writes=[psk[0 + q]])
                    for h in range(q * 8, q * 8 + 8):
                        P.op('pe', lambda e: e.matmul(ps[4 + q][0:64, (h % 8) * 64:(h % 8 + 1) * 64], fr(PT[:, 0, h, 0, :]), fr(Qm[:, 0, h, :]), start=True, stop=True),
                             reads=[('Q', 0, q), ('PT', 0, q)], writes=[psk[4 + q]])
                    copy('act', fr(PT[0:64, 1, q * 8:(q + 1) * 8, 0, :]), hview(0, q, 64), [psk[0 + q]], [('PT', 1, q)])
                    copy('dve', fr(Qm[0:64, 1, q * 8:(q + 1) * 8, :]), hview(4, q, 64), [psk[4 + q]], [('Q', 1, q)])
                cur = 1
                for lv in range(1, 5):
                    nxt = 1 - cur
                    for q in range(2):
                        for h in range(q * 8, q * 8 + 8):
                            pi = 2 * q + (h % 8) // 4
                            P.op('pe', lambda e: e.matmul(ps[pi][0:64, (h % 4) * 128:(h % 4 + 1) * 128], fr(Qm[:, cur, h, :]), fr(PT[:, cur, h, :, :].rearrange("p k t -> p (k t)")), start=True, stop=True),
                                 reads=[('Q', cur, q), ('PT', cur, q)], writes=[psk[pi]])
                        for h in range(q * 8, q * 8 + 8):
                            P.op('pe', lambda e: e.matmul(ps[4 + q][0:64, (h % 8) * 64:(h % 8 + 1) * 64], fr(PT[:, cur, h, 0, :]), fr(Qm[:, cur, h, :]), start=True, stop=True),
                                 reads=[('Q', cur, q), ('PT', cur, q)], writes=[psk[4 + q]])
                        for hb in range(2):
                            pi = 2 * q + hb
                            h0 = q * 8 + hb * 4
                            pv3 = ps[pi][0:64, :].rearrange("p (h k t) -> p h k t", k=2, t=64)
                            if lv < 4:
                                P.op('act', lambda e: e.activation(out=fr(PT[0:64, nxt, h0:h0 + 4, 0, :]), in_=pv3[:, :, 0, :], func=AF.Identity), reads=[psk[pi]], writes=[('PT', nxt, q)])
                            P.op('dve', lambda e: e.tensor_tensor(out=fr(PT[0:64, nxt, h0:h0 + 4, 1, :]), in0=pv3[:, :, 1, :], in1=PT[0:64, cur, h0:h0 + 4, 1, :], op=ALU.add),
                                 reads=[psk[pi], ('PT', cur, q)], writes=[('PT', nxt, q)])
                        copy('act' if lv == 4 else 'dve', fr(Qm[0:64, nxt, q * 8:(q + 1) * 8, :]), hview(4, q, 64), [psk[4 + q]], [('Q', nxt, q)])
                    cur = nxt
                nxt = 1 - cur
                for q in range(2):
                    for h in range(q * 8, q * 8 + 8):
                        P.op('pe', lambda e: e.matmul(ps[0 + q][0:64, (h % 8) * 64:(h % 8 + 1) * 64], fr(Qm[:, cur, h, :]), fr(PT[:, cur, h, 1, :]), start=True, stop=True),
                             reads=[('Q', cur, q), ('PT', cur, q)], writes=[psk[0 + q]])
                    P.op('dve', lambda e: e.tensor_tensor(out=fr(PT[0:64, nxt, q * 8:(q + 1) * 8, 1, :]), in0=hview(0, q, 64), in1=PT[0:64, cur, q * 8:(q + 1) * 8, 1, :], op=ALU.add),
                         reads=[psk[0 + q], ('PT', cur, q)], writes=[('PT', nxt, q)])
                cur = nxt
                if RW_STAGE < 5:
                    return b
                for q in range(2):
                    for h in range(q * 8, q * 8 + 8):
                        oc = (h % 8) * 64
                        P.op('pe', lambda e: e.matmul(ps[0 + q][0:64, oc:oc + 64], fr(ropr[:, b, h, 2, :]), fr(Hsr[:, h, :]), start=True, stop=False), reads=[('ropr', b), 'Hsr'], writes=[psk[0 + q]])
                        P.op('pe', lambda e: e.matmul(ps[0 + q][0:64, oc:oc + 64], fr(AT[:, h, 0:64]), fr(Zx[:, h, :]), start=False, stop=True), reads=['AT', 'Zx'], writes=[psk[0 + q]])
                    copy('act' if q == 0 else 'dve', fr(Xs[0:64, q * 8:(q + 1) * 8, :]), hview(0, q, 64), [psk[q]], [('Xs', q)])
                for q in range(2):
                    for h in range(q * 8, q * 8 + 8):
                        oc = (h % 8) * 64
                        P.op('pe', lambda e: e.matmul(ps[2 + q][0:64, oc:oc + 64], fr(PT[:, cur, h, 1, :]), fr(Xs[:, h, :]), start=True, stop=True), reads=[('PT', cur, q), ('Xs', q)], writes=[psk[2 + q]])
                    copy('act' if q == 0 else 'dve', fr(Z[0:64, q * 8:(q + 1) * 8, :]), hview(2, q, 64), [psk[2 + q]], [('Zu', q)])
                for q in range(2):
                    for h in range(q * 8, q * 8 + 8):
                        oc = (h % 8) * 64
                        P.op('pe', lambda e: e.matmul(ps[4 + q][0:64, oc:oc + 64], fr(ropr[:, b, h, 3, :]), fr(Hsr[:, h, :]), start=True, stop=False), reads=[('ropr', b), 'Hsr'], writes=[psk[4 + q]])
                        P.op('pe', lambda e: e.matmul(ps[4 + q][0:64, oc:oc + 64], fr(AT[:, h, 64:128]), fr(Z[:, h, :]), start=False, stop=True), reads=['AT', ('Zu', q), 'Zv'], writes=[psk[4 + q]])
                if RW_STAGE < 6:
                    return b
                for h in range(16):
                    pi = 6 + h // 8
                    oc = (h % 8) * 64
                    P.op('pe', lambda e: e.matmul(ps[pi][0:64, oc:oc + 64], fr(BKs[:, h, :]), fr(Z[:, h, :]), start=True, stop=True), reads=['BKs', ('Zu', h // 8), 'Zv'], writes=[psk[pi]])
                for q in range(2):
                    P.op('dve', lambda e: e.tensor_tensor(out=Ht[:, q * 8:(q + 1) * 8, :], in0=Hs[0:64, q * 8:(q + 1) * 8, :], in1=hview(6, q, 64), op=ALU.add), reads=['Hs', psk[6 + q]], writes=['Ht'])
                P.op('dve', lambda e: e.tensor_tensor(out=Hs[0:64], in0=Ht[:], in1=wcall[:, :, d, c:c + 1].broadcast_to([64, 16, 64]), op=ALU.mult), reads=['Ht', 'wcall'], writes=['Hs'])
                P.op('dve', lambda e: e.tensor_tensor(out=fr(Hsr[0:64]), in0=Ht[:], in1=wcall[:, :, d, c:c + 1].broadcast_to([64, 16, 64]), op=ALU.mult), reads=['Ht', 'wcall'], writes=['Hsr'])
                return b

            P.op('dve', lambda e: e.memset(Hs[:], 0.0), writes=['Hs'])
            P.op('dve', lambda e: e.memset(Hsr[:], 0.0), writes=['Hsr'])
            seqB = list(range(NC - 1, -1, -1))
            prefetch(0, seqB[0], 1)
            for si, c in enumerate(seqB):
                b = step(si, c, 1, seqB[si + 1] if si + 1 < NC else None)
                if RW_STAGE < 9:
                    continue
                for q in range(2):
                    copy('act' if q == 0 else 'dve', ysb[0:64, b, q * 512:(q + 1) * 512], ps[4 + q][0:64, :], [psk[4 + q]], [('ysb', b)])
                P.dma('pool', ybt_d[c * 64:(c + 1) * 64, :], ysb[0:64, b, :], reads=[('ysb', b)], writes=['ybt_d'])
            P.barrier()
            P.op('dve', lambda e: e.memset(Hs[:], 0.0), writes=['Hs'])
            P.op('dve', lambda e: e.memset(Hsr[:], 0.0), writes=['Hsr'])
            prefetch(0, 0, 0)
            for si, c in enumerate(range(NC)):
                b = si % 2
                P.dma('sp', ybl[:, b, :], ybt_d[c * 64:(c + 1) * 64, :], writes=[('ybl', b)])
                P.dma('sp', gbt[:, b], gb_d[:, :, :, c * 64:(c + 1) * 64], writes=[('gbt', b)])
                step(si, c, 0, c + 1 if c + 1 < NC else None)
                if RW_STAGE < 9:
                    continue
                for q in range(2):
                    P.op('dve', lambda e: e.tensor_tensor(out=ysb[0:64, b, q * 512:(q + 1) * 512], in0=ybl[:, b, q * 512:(q + 1) * 512], in1=ps[4 + q][0:64, :], op=ALU.add),
                         reads=[('ybl', b), psk[4 + q]], writes=[('ysb', b)])
                y3 = ysb[0:64, b, :].rearrange("p (h v) -> p h v", v=64)
                P.op('dve', lambda e: e.tensor_reduce(out=gst[:, 0, :], in_=y3, op=ALU.add, axis=AX.X), reads=[('ysb', b)], writes=['gst0'])
                P.op('pool', lambda e: e.tensor_tensor(out=ysq[:], in0=ysb[0:64, b, :], in1=ysb[0:64, b, :], op=ALU.mult), reads=[('ysb', b)], writes=['ysq'])
                P.op('dve', lambda e: e.tensor_reduce(out=gst[:, 1, :], in_=ysq[:].rearrange("p (h v) -> p h v", v=64), op=ALU.add, axis=AX.X), reads=['ysq'], writes=['gst1'])
                P.op('dve', lambda e: e.tensor_scalar(out=gst[:, 0, :], in0=gst[:, 0, :], scalar1=1.0 / 64, scalar2=None, op0=ALU.mult), reads=['gst0'], writes=['gst0'])
                P.op('dve', lambda e: e.tensor_tensor(out=gst[:, 2, :], in0=gst[:, 0, :], in1=gst[:, 0, :], op=ALU.mult), reads=['gst0'], writes=['gst2'])
                P.op('dve', lambda e: e.scalar_tensor_tensor(out=gst[:, 1, :], in0=gst[:, 1, :], scalar=1.0 / 64, in1=gst[:, 2, :], op0=ALU.mult, op1=ALU.subtract), reads=['gst1', 'gst2'], writes=['gst1'])
                P.op('dve', lambda e: e.tensor_scalar(out=gst[:, 1, :], in0=gst[:, 1, :], scalar1=64e-5, scalar2=None, op0=ALU.add), reads=['gst1'], writes=['gst1'])
                P.op('act', lambda e: e.activation(out=gst[:, 1, :], in_=gst[:, 1, :], func=AF.Sqrt), reads=['gst1'], writes=['gst1'])
                P.op('dve', lambda e: e.reciprocal(out=gst[:, 3, :], in_=gst[:, 1, :]), reads=['gst1'], writes=['gst3'])
                P.op('dve', lambda e: e.tensor_tensor(out=y3, in0=y3, in1=gst[:, 0, :].unsqueeze(2).broadcast_to([64, 16, 64]), op=ALU.subtract), reads=[('ysb', b), 'gst0'], writes=[('ysb', b)])
                P.op('dve', lambda e: e.tensor_tensor(out=y3, in0=y3, in1=gst[:, 3, :].unsqueeze(2).broadcast_to([64, 16, 64]), op=ALU.mult), reads=[('ysb', b), 'gst3'], writes=[('ysb', b)])
                for cc in range(8):
                    pi = 6 + cc // 4
                    P.op('pe', lambda e: e.transpose(out=ps[pi][:, (cc % 4) * 128:(cc % 4 + 1) * 128], in_=ysb[:, b, cc * 128:(cc + 1) * 128], identity=identf[:]), reads=[('ysb', b), 'identf'], writes=[psk[pi]])
                o3_ = ofm[:].rearrange("p (c t) -> p c t", t=64)
                for q in range(2):
                    P.op('dve', lambda e: e.tensor_tensor(out=o3_[:, q * 4:(q + 1) * 4, :], in0=ps[6 + q][:, :].rearrange("p (c t) -> p c t", t=128)[:, :, 0:64],
                                                           in1=lncol[:, 0, q * 4:(q + 1) * 4].unsqueeze(2).broadcast_to([128, 4, 64]), op=ALU.mult), reads=[psk[6 + q], 'lncol'], writes=['ofm'])
                P.op('pool', lambda e: e.tensor_tensor(out=o3_, in0=o3_, in1=lncol[:, 1, :].unsqueeze(2).broadcast_to([128, 8, 64]), op=ALU.add), reads=['ofm', 'lncol'], writes=['ofm'])
                P.op('pool', lambda e: e.tensor_tensor(out=o3_, in0=o3_, in1=gbt[:, b, :, 1, :], op=ALU.add), reads=['ofm', ('gbt', b)], writes=['ofm'])
                P.op('pool', lambda e: e.tensor_tensor(out=ofb[:, b], in0=o3_, in1=gbt[:, b, :, 0, :], op=ALU.mult), reads=['ofm', ('gbt', b)], writes=[('ofb', b)])
                P.dma('pool', yT_d[:, 0:8, c * 64:(c + 1) * 64], ofb[:, b], reads=[('ofb', b)], writes=['yT_d'])
            P.barrier()

    t0 = 0
    for L in seq_lens:
        phase_prep(t0, L)
        P.lastw['xres_src'] = None
        for li, (kind, j) in enumerate(layers):
            last = (li == nl - 1)
            x_src = xin[t0:t0 + L, :] if li == 0 else xres[(li - 1) % 2][0:L, :]
            x_dst = yout[t0:t0 + L, :] if last else xres[li % 2][0:L, :]
            if kind == 'm':
                phase_mamba(j, L)
                phase_out(li, t0, L, 16, wb['m_w_out'][j], x_src, x_dst, last)
            if kind == 'r':
                phase_rwkv(j, L)
                phase_out(li, t0, L, 8, wb['r_w_out'][j], x_src, x_dst, last)
            if kind == 'a':
                phase_attn(j, L)
                phase_out(li, t0, L, 8, wb['a_w_out'][j], x_src, x_dst, last)
        t0 += L
    P.barrier()
    return nc


def host_consts(Lmax):
    t = np.arange(Lmax)
    row = (t // 64).astype(np.float32)
    col = (t % 64).astype(np.float32)
    inv = (10000.0 ** (-np.arange(0, 64, 2, dtype=np.float32) / 64)).astype(np.float32)
    ar = row[:, None] * inv[None]
    ac = col[:, None] * inv[None]
    rope = np.stack([np.cos(ar), np.sin(ar), np.cos(ac), np.sin(ac)], axis=1).astype(np.float32)
    segmask = np.ones((64, Lmax), np.float32)
    segmask[:, ::128] = 0.0
    sel = np.zeros((64, 16, 4), np.float32)
    for d in range(2):
        for g in range(8):
            for hh in range(4):
                sel[d * 32 + g * 4 + hh, d * 8 + g, hh] = 1.0
    s_ = np.arange(128)[:, None]
    q_ = np.arange(128)[None, :]
    nf = np.where(q_ < s_, -30000.0, 0.0).astype(np.float32)
    nb_ = np.where(q_ > s_, -30000.0, 0.0).astype(np.float32)
    negm = np.stack([np.tile(nf, (1, 4)), np.tile(nb_, (1, 4))], axis=1).astype(np.float32)
    blk = np.zeros((128, 128), np.float32)
    blk[:64, :64] = 1.0
    blk[64:, 64:] = 1.0
    mask64 = np.ones((128, 1024), np.float32)
    mask64[:, ::64] = 0.0
    s2 = np.arange(64)[:, None]
    t2 = np.arange(64)[None, :]
    rmask = np.zeros((128, 2, 128), np.float32)
    for rk in range(2):
        rmask[rk * 64:(rk + 1) * 64, 0, 0:64] = (s2 < t2)
        rmask[rk * 64:(rk + 1) * 64, 0, 64:128] = (s2 <= t2)
        rmask[rk * 64:(rk + 1) * 64, 1, 0:64] = (s2 > t2)
        rmask[rk * 64:(rk + 1) * 64, 1, 64:128] = (s2 >= t2)
    rmaskn = np.zeros((64, 2, 64), np.float32)
    rmaskn[:, 0, :] = (t2 < s2)
    rmaskn[:, 1, :] = (t2 > s2)
    return {"ident_f": np.eye(128, dtype=np.float32), "rope_t": rope, "segmask": segmask, "sel_c": sel, "negm_c": negm,
            "blk_c": blk, "mask64_c": mask64, "rmask_c": rmask, "rmaskn_c": rmaskn}


def rwkv_host_layout(inp):
    nb = inp['r_mu'].shape[0]
    def t8(a):
        return np.ascontiguousarray(a.reshape(nb, 8, 128).transpose(0, 2, 1)).astype(np.float32)
    def t82(a):
        return np.ascontiguousarray(a.reshape(nb, 2, 8, 128).transpose(0, 3, 1, 2)).astype(np.float32)
    return {
        'r_w_in': inp['r_w_in'], 'r_w_out': inp['r_w_out'],
        'r_mu_t': np.ascontiguousarray(inp['r_mu'].reshape(nb, 34, 128).transpose(0, 2, 1)).astype(np.float32),
        'r_wup_t': np.ascontiguousarray(inp['r_w_up'].reshape(nb, 128, 1024)), 'r_aup_t': np.ascontiguousarray(inp['r_a_up'].reshape(nb, 128, 1024)),
        'r_w0_t': t82(inp['r_w0']), 'r_a0_t': t82(inp['r_a0']),
        'r_kk_t': t8(inp['r_k_k']), 'r_ka_t': t8(inp['r_k_a']), 'r_rk_t': t8(inp['r_r_k'].reshape(nb, 1024)),
        'r_lnw_t': t8(inp['r_ln_w']), 'r_lnb_t': t8(inp['r_ln_b']),
    }


def mamba_host_layout(inp):
    na = inp['m_conv_w'].shape[0]
    cw = np.ascontiguousarray(inp['m_conv_w'].reshape(na, 5, 32, 128).transpose(0, 3, 2, 1)).astype(np.float32)
    cb = np.ascontiguousarray(inp['m_conv_b'].reshape(na, 32, 128).transpose(0, 2, 1)).astype(np.float32)
    return {
        'm_w_in': inp['m_w_in'], 'm_w_out': inp['m_w_out'], 'm_convw_t': cw, 'm_convb_t': cb,
        'm_alog_c': np.ascontiguousarray(inp['m_a_log'].reshape(na, 64, 1)), 'm_dtb_c': np.ascontiguousarray(inp['m_dt_bias'].reshape(na, 64, 1)),
        'm_d_rep': np.ascontiguousarray(np.repeat(inp['m_d'], 64, axis=1)), 'm_norm_w': inp['m_norm_w'],
    }


SEQS = [4096, 4096, 2048]
LAYERS = [('m', 0), ('r', 0), ('a', 0), ('m', 1)]


def run(seq_lens, layers, xs, weights, n_cores=8, dbg=False):
    nc = build(seq_lens, layers, dbg)
    hc = host_consts(max(seq_lens))
    in_maps = []
    for c in range(n_cores):
        m = {"xin": np.ascontiguousarray(xs[c], dtype=np.float32)}
        m.update(hc)
        if not any(k == 'm' for k, _ in layers):
            for kk in ("segmask", "sel_c", "negm_c"):
                m.pop(kk, None)
        if not any(k == 'r' for k, _ in layers):
            for kk in ("blk_c", "mask64_c", "rmask_c", "rmaskn_c"):
                m.pop(kk, None)
        m.update(weights)
        in_maps.append(m)
    res = run_bass_kernel_spmd(nc, in_maps, core_ids=list(range(n_cores)))
    if dbg:
        return res.results
    return [r["yout"] for r in res.results]


def kernel(**inputs):
    inp = {k: np.asarray(v) for k, v in inputs.items()}
    layers = LAYERS
    xp, xs = inp['x_prompt'], inp['x_sample']
    xs_core = [np.concatenate([xp[2 * c], xp[2 * c + 1], xs[c]], axis=0) for c in range(8)]
    w = {'ln_g': inp['ln_g'], 'ln_b': inp['ln_b']}
    w.update(mamba_host_layout(inp))
    for k in ('a_w_in', 'a_q_norm', 'a_k_norm', 'a_w_out'):
        w[k] = inp[k]
    w.update(rwkv_host_layout(inp))
    ys = run(SEQS, layers, xs_core, w)
    y_prompt = np.stack([ys[c // 2][(c % 2) * 4096:(c % 2 + 1) * 4096] for c in range(16)], axis=0)
    y_sample = np.stack([ys[c][8192:10240] for c in range(8)], axis=0)
    return (y_prompt.astype(np.float32), y_sample.astype(np.float32))
```
